# Optimizing a Trainium2 kernel written in Bass

```python
import math
import jax, jax.numpy as jnp
from jax import lax
import numpy as np

D_MODEL = 2048
BATCH = 4
SEQ = 4096
DEPTH = 4

CHUNK = 64
Q_BLOCK = 128
N_BRANCH = 4
BR_WIDTH = 1024
GMLP_BLOCK = 128
GMLP_GROUPS = 4
GMLP_GC = BR_WIDTH // GMLP_GROUPS
DIFF_HEADS = 8
DIFF_D = 64
DIFF_VD = 2 * DIFF_D
DIFF_QK = DIFF_HEADS * 2 * DIFF_D
DIFF_V = DIFF_HEADS * DIFF_VD
CONV_K = 31
SB_HEADS = 8
SB_D = 128
SB_W = SB_HEADS * SB_D
ROT_DIM = DIFF_D // 4
ROPE_THETA = 500000.0
FF_DIM = 5632
ALPHA = (2 * DEPTH) ** 0.25
BETA = (8 * DEPTH) ** -0.25
LN_EPS = 1e-5
IN_SIZES = (BR_WIDTH, BR_WIDTH,
            DIFF_QK, DIFF_QK, DIFF_V,
            BR_WIDTH, BR_WIDTH,
            SB_W, SB_W, SB_W,
            N_BRANCH * D_MODEL)
IN_COLS = sum(IN_SIZES)

kernel_name = 'hybrid_gated_streaming_encoder'


def layer_norm(x, g, b):
    xf = x.astype(jnp.float32)
    mu = jnp.mean(xf, axis=-1, keepdims=True)
    var = jnp.mean(jnp.square(xf - mu), axis=-1, keepdims=True)
    y = (xf - mu) * lax.rsqrt(var + LN_EPS)
    return (y * g + b).astype(x.dtype)


def rms_norm(x, g):
    xf = x.astype(jnp.float32)
    y = xf * lax.rsqrt(jnp.mean(jnp.square(xf), axis=-1, keepdims=True) + LN_EPS)
    return (y * g).astype(x.dtype)


def swiglu_ffn(x, w_in, w_out):
    a, gt = jnp.split(x @ w_in, 2, axis=-1)
    return (jax.nn.silu(a) * gt) @ w_out


def chunk_mask(t_idx, s_idx):
    return (s_idx // CHUNK)[None, :] <= (t_idx // CHUNK)[:, None]


def to_blocks(a):
    b, s = a.shape[:2]
    return a.reshape((b, s // Q_BLOCK, Q_BLOCK) + a.shape[2:]).swapaxes(0, 1)


def from_blocks(a):
    a = a.swapaxes(0, 1)
    return a.reshape((a.shape[0], a.shape[1] * a.shape[2]) + a.shape[3:])


def partial_rope(x, cos, sin):
    half = ROT_DIM // 2
    cos = cos.astype(x.dtype)
    sin = sin.astype(x.dtype)
    x1 = x[..., :half]
    x2 = x[..., half:ROT_DIM]
    return jnp.concatenate([x1 * cos - x2 * sin, x2 * cos + x1 * sin, x[..., ROT_DIM:]], axis=-1)


def gmlp_spatial_gating(u, v, ln_g, ln_b, w_s, b_s):
    b, s, _ = v.shape
    v = layer_norm(v, ln_g, ln_b)
    vb = v.reshape(b, s // GMLP_BLOCK, GMLP_BLOCK, GMLP_GROUPS, GMLP_GC)
    pos = jnp.arange(GMLP_BLOCK)
    w = jnp.where(chunk_mask(pos, pos)[None], w_s, 0.0)
    mixed = jnp.einsum('gts,bnsgc->bntgc', w, vb) + b_s.T[None, None, :, :, None]
    return u * mixed.reshape(b, s, BR_WIDTH)


def differential_attention(q, k, v, lam, lam_init, norm_g):
    s = q.shape[1]
    s_idx = jnp.arange(s)
    scale = DIFF_D ** -0.5

    def block(args):
        qb, t_idx = args
        sc = jnp.einsum('bthnd,bshnd->bnhts', qb, k).astype(jnp.float32) * scale
        sc = jnp.where(chunk_mask(t_idx, s_idx), sc, -jnp.inf)
        p = jax.nn.softmax(sc, axis=-1)
        a = p[:, 0] - lam * p[:, 1]
        return jnp.einsum('bhts,bshe->bthe', a.astype(v.dtype), v)

    t_blocks = jnp.arange(s).reshape(-1, Q_BLOCK)
    o = from_blocks(lax.map(block, (to_blocks(q), t_blocks)))
    o = rms_norm(o, norm_g.reshape(DIFF_HEADS, DIFF_VD)) * (1.0 - lam_init)
    return o.reshape(o.shape[0], s, DIFF_V)


def conformer_conv(a, gt, w_dw, b_dw, ln_g, ln_b):
    h = a * jax.nn.sigmoid(gt)
    h = lax.conv_general_dilated(
        h, w_dw[:, None, :], window_strides=(1,), padding=[(CONV_K - 1, 0)],
        dimension_numbers=('NWC', 'WIO', 'NWC'), feature_group_count=BR_WIDTH) + b_dw
    return jax.nn.silu(layer_norm(h, ln_g, ln_b))


def stick_breaking_attention(q, k, v):
    s = q.shape[1]
    s_idx = jnp.arange(s)
    scale = SB_D ** -0.5

    def block(args):
        qb, t_idx = args
        z = jnp.einsum('bthd,bshd->bhts', qb, k).astype(jnp.float32) * scale
        causal = s_idx[None, :] < t_idx[:, None]
        log_rest = jnp.where(causal, jax.nn.log_sigmoid(-z), 0.0)
        suffix = lax.cumsum(log_rest, axis=3, reverse=True) - log_rest
        w = jnp.where(causal, jnp.exp(jax.nn.log_sigmoid(z) + suffix), 0.0)
        return jnp.einsum('bhts,bshd->bthd', w.astype(v.dtype), v)

    t_blocks = jnp.arange(s).reshape(-1, Q_BLOCK)
    o = from_blocks(lax.map(block, (to_blocks(q), t_blocks)))
    return o.reshape(o.shape[0], s, SB_W)


def hybrid_mixer(h, cos, sin, lam_init, w_in, gmlp_ln_g, gmlp_ln_b, gmlp_ws, gmlp_bs,
                 diff_lq1, diff_lk1, diff_lq2, diff_lk2, diff_norm_g,
                 conv_w, conv_b, conv_ln_g, conv_ln_b, w_branch, w_out):
    b, s, _ = h.shape
    points = []
    acc = 0
    for size in IN_SIZES[:-1]:
        acc += size
        points.append(acc)
    (ua, va, dq, dk, dv, ca, cg, sq, sk, sv, gates) = jnp.split(h @ w_in, points, axis=-1)

    o_a = gmlp_spatial_gating(ua, va, gmlp_ln_g, gmlp_ln_b, gmlp_ws, gmlp_bs)

    q = partial_rope(dq.reshape(b, s, DIFF_HEADS * 2, DIFF_D), cos, sin).reshape(b, s, DIFF_HEADS, 2, DIFF_D)
    k = partial_rope(dk.reshape(b, s, DIFF_HEADS * 2, DIFF_D), cos, sin).reshape(b, s, DIFF_HEADS, 2, DIFF_D)
    lam = (jnp.exp(jnp.sum(diff_lq1.astype(jnp.float32) * diff_lk1.astype(jnp.float32)))
           - jnp.exp(jnp.sum(diff_lq2.astype(jnp.float32) * diff_lk2.astype(jnp.float32))) + lam_init)
    o_b = differential_attention(q, k, dv.reshape(b, s, DIFF_HEADS, DIFF_VD), lam, lam_init, diff_norm_g)

    o_c = conformer_conv(ca, cg, conv_w, conv_b, conv_ln_g, conv_ln_b)

    o_d = stick_breaking_attention(sq.reshape(b, s, SB_HEADS, SB_D), sk.reshape(b, s, SB_HEADS, SB_D),
                                   sv.reshape(b, s, SB_HEADS, SB_D))

    branches = jnp.stack([o_a, o_b, o_c, o_d], axis=2)
    proj = jnp.einsum('bsnc,ncd->bsnd', branches, w_branch)
    gate = jax.nn.sigmoid(gates.reshape(b, s, N_BRANCH, D_MODEL))
    merged = jnp.sum(gate * proj, axis=2)
    return merged @ w_out


def setup_inputs(seed: int = 0) -> dict:
    key = jax.random.key(seed)
    ks = jax.random.split(key, 28)
    L = DEPTH

    def nrm(k, shape, scale):
        return scale * jax.random.normal(k, shape, jnp.float32)

    def gain(k, shape):
        return 1.0 + nrm(k, shape, 0.02)

    x = jax.random.normal(ks[0], (BATCH, SEQ, D_MODEL), jnp.float32)
    offsets = jax.random.randint(ks[1], (BATCH, 1), 0, 1024)
    positions = (offsets + jnp.arange(SEQ)[None, :]).astype(jnp.int32)
    return {
        'x': x,
        'positions': positions,
        'ffn1_w_in': nrm(ks[2], (L, D_MODEL, 2 * FF_DIM), D_MODEL ** -0.5),
        'ffn1_w_out': nrm(ks[3], (L, FF_DIM, D_MODEL), BETA * FF_DIM ** -0.5),
        'ln1_g': gain(ks[4], (L, D_MODEL)),
        'ln1_b': nrm(ks[5], (L, D_MODEL), 0.02),
        'w_in': nrm(ks[6], (L, D_MODEL, IN_COLS), D_MODEL ** -0.5),
        'gmlp_ln_g': gain(ks[7], (L, BR_WIDTH)),
        'gmlp_ln_b': nrm(ks[8], (L, BR_WIDTH), 0.02),
        'gmlp_ws': nrm(ks[9], (L, GMLP_GROUPS, GMLP_BLOCK, GMLP_BLOCK), 0.5 * GMLP_BLOCK ** -0.5),
        'gmlp_bs': gain(ks[10], (L, GMLP_GROUPS, GMLP_BLOCK)),
        'diff_lq1': nrm(ks[11], (L, DIFF_D), 0.1),
        'diff_lk1': nrm(ks[12], (L, DIFF_D), 0.1),
        'diff_lq2': nrm(ks[13], (L, DIFF_D), 0.1),
        'diff_lk2': nrm(ks[14], (L, DIFF_D), 0.1),
        'diff_norm_g': gain(ks[15], (L, DIFF_V)),
        'conv_w': nrm(ks[16], (L, CONV_K, BR_WIDTH), CONV_K ** -0.5),
        'conv_b': nrm(ks[17], (L, BR_WIDTH), 0.02),
        'conv_ln_g': gain(ks[18], (L, BR_WIDTH)),
        'conv_ln_b': nrm(ks[19], (L, BR_WIDTH), 0.02),
        'w_branch': nrm(ks[20], (L, N_BRANCH, BR_WIDTH, D_MODEL), BETA * BR_WIDTH ** -0.5),
        'w_out': nrm(ks[21], (L, D_MODEL, D_MODEL), BETA * D_MODEL ** -0.5),
        'ln2_g': gain(ks[22], (L, D_MODEL)),
        'ln2_b': nrm(ks[23], (L, D_MODEL), 0.02),
        'ffn2_w_in': nrm(ks[24], (L, D_MODEL, 2 * FF_DIM), D_MODEL ** -0.5),
        'ffn2_w_out': nrm(ks[25], (L, FF_DIM, D_MODEL), BETA * FF_DIM ** -0.5),
        'ln3_g': gain(ks[26], (L, D_MODEL)),
        'ln3_b': nrm(ks[27], (L, D_MODEL), 0.02),
    }


def reference(x, positions, ffn1_w_in, ffn1_w_out, ln1_g, ln1_b, w_in, gmlp_ln_g, gmlp_ln_b,
              gmlp_ws, gmlp_bs, diff_lq1, diff_lk1, diff_lq2, diff_lk2, diff_norm_g,
              conv_w, conv_b, conv_ln_g, conv_ln_b, w_branch, w_out, ln2_g, ln2_b,
              ffn2_w_in, ffn2_w_out, ln3_g, ln3_b):
    inv_freq = ROPE_THETA ** (-jnp.arange(0, ROT_DIM, 2, dtype=jnp.float32) / ROT_DIM)
    ang = positions.astype(jnp.float32)[..., None] * inv_freq
    cos = jnp.cos(ang)[:, :, None, :]
    sin = jnp.sin(ang)[:, :, None, :]

    for i in range(DEPTH):
        lam_init = 0.8 - 0.6 * math.exp(-0.3 * i)
        x = layer_norm(ALPHA * x + 0.5 * swiglu_ffn(x, ffn1_w_in[i], ffn1_w_out[i]), ln1_g[i], ln1_b[i])
        mix = hybrid_mixer(x, cos, sin, lam_init, w_in[i], gmlp_ln_g[i], gmlp_ln_b[i], gmlp_ws[i], gmlp_bs[i],
                           diff_lq1[i], diff_lk1[i], diff_lq2[i], diff_lk2[i], diff_norm_g[i],
                           conv_w[i], conv_b[i], conv_ln_g[i], conv_ln_b[i], w_branch[i], w_out[i])
        x = layer_norm(ALPHA * x + mix, ln2_g[i], ln2_b[i])
        x = layer_norm(ALPHA * x + 0.5 * swiglu_ffn(x, ffn2_w_in[i], ffn2_w_out[i]), ln3_g[i], ln3_b[i])
    return x
```

```python
import math
import numpy as np
import concourse.bass as bass
import concourse.mybir as mybir
from concourse.bass_utils import run_bass_kernel_spmd

F32 = mybir.dt.float32
BF16 = mybir.dt.bfloat16
I32 = mybir.dt.int32
ALU = mybir.AluOpType
AF = mybir.ActivationFunctionType

D = 2048
FF = 5632
NCD = 16
NCF = 44
BR = 1024
IN_COLS = 18432
DEPTH = 4
ALPHA = (2 * DEPTH) ** 0.25
LN_EPS = 1e-5
T = 512
CONV_K = 31
HALO = CONV_K - 1
ROPE_THETA = 500000.0
RDMA = 12


class Res:
    __slots__ = ("w", "r", "excl")

    def __init__(self, excl=False):
        self.w = {}
        self.r = {}
        self.excl = excl


def rl(n):
    return [Res() for _ in range(n)]


class Sched:
    def __init__(self, nc):
        self.nc = nc
        self.streams = {k: [] for k in ("pe", "act", "dve", "pool", "sp")}
        self.count = {k: 0 for k in self.streams}
        self.dcount = {"sp": 0, "pool": 0, "act": 0}
        self.seen = {k: {} for k in self.streams}
        self.latest = {}
        self.barrier_tok = {}

    def barrier(self):
        self.barrier_tok = dict(self.latest)

    def op(self, stream, fn, reads=(), writes=(), dma=False):
        deps = dict(self.barrier_tok)

        def add(d):
            for k, v in d.items():
                if deps.get(k, 0) < v:
                    deps[k] = v

        for r in reads:
            add(r.w)
            if r.excl:
                add(r.r)
        for w in writes:
            add(w.w)
            add(w.r)
        if dma:
            k = self.dcount[stream]
            self.dcount[stream] = k + 1
            sem = ("d", stream, k % RDMA)
            val = 16 * (k // RDMA + 1)
            if k >= RDMA:
                if deps.get(sem, 0) < val - 16:
                    deps[sem] = val - 16
            inc = 16
        else:
            self.count[stream] += 1
            sem = ("e", stream)
            val = self.count[stream]
            inc = 1
        seen = self.seen[stream]
        waits = []
        for k, v in deps.items():
            if k == ("e", "pe") and stream == "pe":
                continue
            if seen.get(k, 0) < v:
                seen[k] = v
                waits.append((k, v))
        self.streams[stream].append((waits, fn, sem, inc))
        self.latest[sem] = val
        for r in reads:
            if r.r.get(sem, 0) < val:
                r.r[sem] = val
        for w in writes:
            w.w = {sem: val}
            w.r = {}

    def emit(self, final_waits_stream="sp"):
        nc = self.nc
        keys = set()
        for st in self.streams.values():
            for waits, fn, sem, inc in st:
                keys.add(sem)
        import contextlib
        with contextlib.ExitStack() as es:
            semobj = {}
            for k in sorted(keys):
                semobj[k] = es.enter_context(nc.semaphore("s_" + "_".join(str(x) for x in k)))
            block = es.enter_context(nc.Block())
            latest = dict(self.latest)

            def runner(name, final):
                lst = self.streams[name]

                def run(e):
                    for waits, fn, sem, inc in lst:
                        for k, v in waits:
                            e.wait_ge(semobj[k], v)
                        fn(e).then_inc(semobj[sem], inc)
                    if final:
                        for k, v in latest.items():
                            e.wait_ge(semobj[k], v)
                return run

            block.tensor(runner("pe", False))
            block.scalar(runner("act", False))
            block.vector(runner("dve", False))
            block.gpsimd(runner("pool", False))
            block.sync(runner("sp", True))


class Arena:
    def __init__(self, nc, base, limit):
        self.nc = nc
        self.off = base
        self.limit = limit
        self.n = 0

    def take(self, shape, dtype):
        nbytes = int(np.prod(shape[1:])) * (4 if dtype in (F32, I32) else 2)
        nbytes = (nbytes + 63) // 64 * 64
        assert self.off + nbytes <= self.limit, ("SBUF overflow", self.off, nbytes, self.limit)
        self.n += 1
        t = self.nc.alloc_sbuf_tensor_at(f"t{self.n}_{self.off}", list(shape), dtype, offset=self.off)
        self.off += nbytes
        return t


def build_consts():
    c = {}
    s = np.arange(128)
    c["ones_f"] = np.ones((128, 128), np.float32)
    c["utri_f"] = (s[:, None] >= s[None, :]).astype(np.float32)
    c["mchunk"] = ((s[:, None] // 64) <= (s[None, :] // 64)).astype(np.float32)
    c["mstrict"] = (s[:, None] < s[None, :]).astype(np.float32)
    rp = np.zeros((128, 128), np.float32)
    invf = np.zeros((128,), np.float32)
    sgn = np.zeros((128,), np.float32)
    freqs = (ROPE_THETA ** (-np.arange(0, 16, 2, dtype=np.float32) / 16)).astype(np.float32)
    for base in (0, 64):
        for i in range(8):
            a, b = base + i, base + 8 + i
            rp[b, a] = 1.0
            rp[a, b] = 1.0
            invf[a] = freqs[i]
            invf[b] = freqs[i]
            sgn[a] = -1.0
            sgn[b] = 1.0
    c["rperm"] = rp
    c["vec"] = np.stack([invf, sgn], axis=1).astype(np.float32)
    return c


class Builder:
    def __init__(self, S, L, dbg=None):
        self.S = S
        self.L = L
        self.NT = S // T
        self.dbg = dbg or set()
        nc = self.nc = bass.Bass("TRN2", target_bir_lowering=False)
        self.s = Sched(nc)
        self.inputs = {}
        self.outputs = {}

    def din(self, name, shape, dtype=F32):
        h = self.nc.dram_tensor(name, list(shape), dtype, kind="ExternalInput")
        self.inputs[name] = h
        return h

    def dscr(self, name, shape, dtype):
        kind = "ExternalOutput" if name in self.dbg else "Internal"
        h = self.nc.dram_tensor(name, list(shape), dtype, kind=kind)
        if kind == "ExternalOutput":
            self.outputs[name] = h
        return h

    def dma(self, out, in_, reads, writes, q="sp"):
        self.s.op(q, lambda e: e.dma_start(out=out, in_=in_), reads=reads, writes=writes, dma=True)

    def setup(self):
        nc, S, L = self.nc, self.S, self.L
        d = self.din
        self.xT = d("xT", [D, S])
        self.pos = d("pos", [1, S], I32)
        self.w = {}
        for nm, shp in [("ffn1_w_in", [L, D, 2 * FF]), ("ffn1_w_out", [L, FF, D]),
                        ("w_in", [L, D, IN_COLS]), ("w_branch", [L, 4, BR, D]), ("w_out", [L, D, D]),
                        ("ffn2_w_in", [L, D, 2 * FF]), ("ffn2_w_out", [L, FF, D])]:
            self.w[nm] = d(nm, shp)
        self.p_ln = d("p_ln", [128, L, 6, NCD])
        self.p_row = d("p_row", [L, 5, BR])
        self.p_col = d("p_col", [128, L, 4, 8])
        self.p_convw = d("p_convw", [128, L, 8, CONV_K])
        self.p_wsT = d("p_wsT", [L, 4, 128, 128])
        self.p_bs = d("p_bs", [L, 4, 128])
        self.p_lam = d("p_lam", [L, 4, 64])
        self.c_in = {k: d("c_" + k, list(v.shape)) for k, v in build_consts().items()}
        self.out = nc.dram_tensor("outT", [D, S], F32, kind="ExternalOutput")
        self.outputs["outT"] = self.out
        sc = self.dscr
        self.X = [sc("X0", [D, S], F32), sc("X1s", [D, S], F32)]
        self.X1 = sc("X1", [D, S], F32)
        self.QD = sc("QD", [8, 128, S], BF16)
        self.KD = sc("KD", [8, 128, S], BF16)
        self.VD = sc("VD", [8, 128, S // 128, 128], BF16)
        self.SQ = sc("SQ", [8, 128, S], BF16)
        self.SK = sc("SK", [8, 128, S], BF16)
        self.SV = sc("SV", [8, 128, S // 128, 128], BF16)
        self.OA = sc("OA", [BR, S], BF16)
        self.OB = sc("OB", [BR, S], BF16)
        self.OC = sc("OC", [BR, S], BF16)
        self.OD = sc("OD", [BR, S], BF16)
        self.CT = sc("CT", [128, S], F32)
        self.ST = sc("ST", [128, S], F32)
        self.r_X = [rl(self.NT), rl(self.NT)]
        self.r_X1 = rl(self.NT)
        self.r_att_in = Res()
        self.r_o = {k: rl(self.NT) for k in ("OA", "OB", "OC", "OD")}
        self.r_rope = Res()

        self.ar = Arena(nc, 16640, 229000)
        ar = self.ar
        self.ones_f = ar.take([128, 128], F32)
        self.utri_f = ar.take([128, 128], F32)
        self.ones_b = ar.take([128, 128], BF16)
        self.mchunk_f = ar.take([128, 128], F32)
        self.mchunk_b = ar.take([128, 128], BF16)
        self.mstrict_f = ar.take([128, 128], F32)
        self.mstrict_b = ar.take([128, 128], BF16)
        self.rperm_b = ar.take([128, 128], BF16)
        self.zeros_b = ar.take([128, 128], BF16)
        self.cvec = ar.take([128, 2], F32)
        self.ln_p = ar.take([128, L * 6 * NCD], F32)
        self.col_p = ar.take([128, L * 4 * 8], F32)
        self.convw = ar.take([128, L * 8 * CONV_K], F32)
        self.halo = ar.take([128, 8, HALO], F32)
        self.qmax = ar.take([128, 8], F32)
        self.kmax = ar.take([128, 8], F32)
        self.negm = ar.take([128, 8], F32)
        self.neglam = ar.take([128, 1], F32)
        self.gscale = ar.take([128, 8], F32)
        self.wm = [ar.take([128, 128], BF16) for _ in range(4)]
        self.bs1 = [ar.take([128, 128], F32) for _ in range(4)]
        self.r_const = Res()
        self.r_halo = Res()
        self.r_qk = Res()
        self.NSLOT = 4
        self.slots = [ar.take([128, 8192], BF16) for _ in range(self.NSLOT)]
        self.r_slot = rl(self.NSLOT)
        self.slot_i = 0
        self.bank = [nc.alloc_psum_tensor(f"bank{i}", [128, 512], F32) for i in range(8)]
        self.r_bank = [Res(excl=True) for _ in range(8)]
        self.unit_i = 0
        self.base = ar.off

        s = self.s
        tmpf = ar.take([128, 128], F32)
        rt = Res()
        for nm, dst_f, dst_b in [("ones_f", self.ones_f, self.ones_b), ("utri_f", self.utri_f, None),
                                 ("mchunk", self.mchunk_f, self.mchunk_b), ("mstrict", self.mstrict_f, self.mstrict_b),
                                 ("rperm", None, self.rperm_b)]:
            tgt = dst_f if dst_f is not None else tmpf
            self.dma(tgt[:, :], self.c_in[nm][:, :], [], [rt])
            if dst_b is not None:
                s.op("dve", lambda e, o=dst_b, i=tgt: e.tensor_copy(out=o[:, :], in_=i[:, :]), reads=[rt], writes=[self.r_const])
        s.op("dve", lambda e: e.memset(self.zeros_b[:, :], 0.0), writes=[self.r_const])
        self.dma(self.cvec[:, :], self.c_in["vec"][:, :], [], [self.r_const])
        self.dma(self.ln_p[:, :], self.p_ln.ap().rearrange("p l i c -> p (l i c)"), [], [self.r_const])
        self.dma(self.col_p[:, :], self.p_col.ap().rearrange("p l i c -> p (l i c)"), [], [self.r_const])
        self.dma(self.convw[:, :], self.p_convw.ap().rearrange("p l c j -> p (l c j)"), [], [self.r_const])
        s.barrier()
        self.ar.off = self.base = ar.off

    def lnp(self, l, i, c):
        o = (l * 6 + i) * NCD + c
        return self.ln_p[:, o:o + 1]

    def colp(self, l, i, c):
        o = (l * 4 + i) * 8 + c
        return self.col_p[:, o:o + 1]

    def load_panel(self, W2d, k0, nk, c0, ncols):
        i = self.slot_i
        self.slot_i = (i + 1) % self.NSLOT
        view = self.slots[i][:, 0:nk * ncols].rearrange("p (k c) -> p k c", c=ncols)
        src = W2d[k0 * 128:(k0 + nk) * 128, c0:c0 + ncols].rearrange("(k p) c -> p k c", p=128)
        self.dma(view, src, [], [self.r_slot[i]], q="pool")
        return (i, view, k0, nk)

    def next_unit(self):
        u = self.unit_i
        self.unit_i = (u + 1) % 3
        return 2 * u, 2 * u + 1

    def run_jobs(self, jobs):
        loaded = {}

        def issue(j):
            loaded[j] = [self.load_panel(*p) for p in jobs[j][0]]

        issue(0)
        for j in range(len(jobs)):
            if j + 1 < len(jobs):
                issue(j + 1)
            pan = loaded.pop(j)
            for banks, evac in jobs[j][1]:
                bidx = self.next_unit()
                aps = []
                for bi, bk in enumerate(banks):
                    b = bidx[bi]
                    n = bk["n"]
                    total = sum(pan[pi][3] for pi, _ in bk["segs"])
                    cnt = 0
                    for pi, coff in bk["segs"]:
                        slot_i, view, k0, nk = pan[pi]
                        for kc in range(nk):
                            a_ap, a_res = bk["act"](k0 + kc)
                            first, last = cnt == 0, cnt == total - 1
                            cnt += 1
                            if bk["mode"] == "fm":
                                lhsT, rhs = view[:, kc, coff:coff + 128], a_ap
                            else:
                                lhsT, rhs = a_ap, view[:, kc, coff:coff + n]
                            self.s.op("pe", lambda e, o=self.bank[b][:, 0:n], l=lhsT, r=rhs, f=first, la=last:
                                      e.matmul(o, l, r, start=f, stop=la),
                                      reads=[self.r_slot[slot_i], a_res], writes=[self.r_bank[b]])
                    aps.append((self.bank[b], self.r_bank[b]))
                evac(aps)

    def ln_fm(self, resid, r_res, nchunk, eps, gfn, bfn, out_fn, scr, func=AF.Identity):
        s = self.s
        sq, r_sq, mean, rstd, tmp, r_tmp, r_stat = scr
        n = float(nchunk * 128)
        b6, b7 = self.bank[6], self.bank[7]
        for c in range(nchunk):
            i = c % 2
            s.op("act", lambda e, o=sq[i], x=resid[:, c, :]: e.activation(out=o[:, :], in_=x, func=AF.Square),
                 reads=[r_res[c]], writes=[r_sq[i]])
            s.op("pe", lambda e, x=resid[:, c, :], f=(c == 0), la=(c == nchunk - 1):
                 e.matmul(b6[:, :], self.ones_f[:, :], x, start=f, stop=la), reads=[r_res[c], self.r_const], writes=[self.r_bank[6]])
            s.op("pe", lambda e, x=sq[i], f=(c == 0), la=(c == nchunk - 1):
                 e.matmul(b7[:, :], self.ones_f[:, :], x[:, :], start=f, stop=la), reads=[r_sq[i], self.r_const], writes=[self.r_bank[7]])
        s.op("dve", lambda e: e.tensor_scalar(out=mean[:, :], in0=b6[:, :], scalar1=1.0 / n, scalar2=None, op0=ALU.mult),
             reads=[self.r_bank[6]], writes=[r_stat[0]])
        s.op("dve", lambda e: e.tensor_tensor(out=tmp[0][:, :], in0=mean[:, :], in1=mean[:, :], op=ALU.mult),
             reads=[r_stat[0]], writes=[r_tmp[0]])
        s.op("dve", lambda e: e.scalar_tensor_tensor(out=tmp[0][:, :], in0=b7[:, :], scalar=1.0 / n, in1=tmp[0][:, :],
                                                     op0=ALU.mult, op1=ALU.subtract),
             reads=[self.r_bank[7], r_tmp[0]], writes=[r_tmp[0]])
        s.op("dve", lambda e: e.tensor_scalar(out=tmp[0][:, :], in0=tmp[0][:, :], scalar1=eps, scalar2=None, op0=ALU.add),
             reads=[r_tmp[0]], writes=[r_tmp[0]])
        s.op("act", lambda e: e.activation(out=tmp[0][:, :], in_=tmp[0][:, :], func=AF.Sqrt), reads=[r_tmp[0]], writes=[r_tmp[0]])
        s.op("dve", lambda e: e.reciprocal(out=rstd[:, :], in_=tmp[0][:, :]), reads=[r_tmp[0]], writes=[r_stat[1]])
        for c in range(nchunk):
            i = c % 2
            s.op("dve", lambda e, o=tmp[i], x=resid[:, c, :]: e.tensor_tensor(out=o[:, :], in0=x, in1=mean[:, :], op=ALU.subtract),
                 reads=[r_res[c], r_stat[0]], writes=[r_tmp[i]])
            s.op("dve", lambda e, o=tmp[i]: e.tensor_tensor(out=o[:, :], in0=o[:, :], in1=rstd[:, :], op=ALU.mult),
                 reads=[r_tmp[i], r_stat[1]], writes=[r_tmp[i]])
            s.op("act", lambda e, o=resid[:, c, :], x=tmp[i], g=gfn(c), b=bfn(c): e.activation(out=o, in_=x[:, :], func=func, bias=b, scale=g),
                 reads=[r_tmp[i], self.r_const], writes=[r_res[c]])
            if out_fn is not None:
                o_ap, o_res = out_fn(c)
                s.op("pool", lambda e, o=o_ap, x=resid[:, c, :]: e.tensor_copy(out=o, in_=x), reads=[r_res[c]], writes=[o_res])

    def ln_scratch(self):
        ar = self.ar
        sq = [ar.take([128, T], F32) for _ in range(2)]
        mean = ar.take([128, T], F32)
        rstd = ar.take([128, T], F32)
        tmp = [ar.take([128, T], F32) for _ in range(2)]
        return (sq, rl(2), mean, rstd, tmp, rl(2), rl(2))

    def ffn_ln(self, l, w_in, w_out, lni, resid, r_res, xT, r_xT, hT, r_hT, scr):
        s = self.s
        W1 = w_in[l]
        W2 = w_out[l]
        silt = [self.ar.take([128, T], F32) for _ in range(2)]
        r_silt = rl(2)
        st = {"i": 0}
        jobs = []
        for jb in range(NCF // 4):
            panels = [(W1, 0, NCD, jb * 512, 512), (W1, 0, NCD, FF + jb * 512, 512)]
            units = []
            for c in range(4):
                j = jb * 4 + c

                def evac(aps, j=j):
                    (A, rA), (G, rG) = aps
                    i = st["i"]
                    st["i"] = 1 - i
                    s.op("act", lambda e: e.activation(out=silt[i][:, :], in_=A[:, :], func=AF.Silu), reads=[rA], writes=[r_silt[i]])
                    s.op("dve", lambda e: e.tensor_tensor(out=hT[:, j, :], in0=silt[i][:, :], in1=G[:, :], op=ALU.mult),
                         reads=[r_silt[i], rG], writes=[r_hT[j]])
                act = lambda kc: (xT[:, kc, :], r_xT[kc])
                units.append(([dict(segs=[(0, c * 128)], act=act, mode="fm", n=T),
                               dict(segs=[(1, c * 128)], act=act, mode="fm", n=T)], evac))
            jobs.append((panels, units))
        self.run_jobs(jobs)
        jobs = []
        for u in range(NCD // 2):
            panels = [(W2, 0, 32, u * 256, 256), (W2, 32, NCF - 32, u * 256, 256)]

            def evac(aps, u=u):
                for bi, (B, rB) in enumerate(aps):
                    c = 2 * u + bi
                    s.op("dve", lambda e, B=B, c=c: e.scalar_tensor_tensor(out=resid[:, c, :], in0=B[:, :], scalar=0.5 / ALPHA,
                                                                           in1=resid[:, c, :], op0=ALU.mult, op1=ALU.add),
                         reads=[rB, r_res[c]], writes=[r_res[c]])
            act = lambda kc: (hT[:, kc, :], r_hT[kc])
            jobs.append((panels, [([dict(segs=[(0, 0), (1, 0)], act=act, mode="fm", n=T),
                                    dict(segs=[(0, 128), (1, 128)], act=act, mode="fm", n=T)], evac)]))
        self.run_jobs(jobs)
        self.ln_fm(resid, r_res, NCD, LN_EPS / ALPHA ** 2, lambda c: self.lnp(l, lni, c), lambda c: self.lnp(l, lni + 1, c),
                   lambda c: (xT[:, c, :], r_xT[c]), scr)

    def load_resid(self, X, rX, t, resid, r_res, xT, r_xT):
        s = self.s
        for c in range(NCD):
            self.dma(resid[:, c, :], X[c * 128:(c + 1) * 128, t * T:(t + 1) * T], [rX], [r_res[c]])
            s.op("pool", lambda e, c=c: e.tensor_copy(out=xT[:, c, :], in_=resid[:, c, :]), reads=[r_res[c]], writes=[r_xT[c]])

    def store_resid(self, X, rX, t, resid, r_res):
        for c in range(NCD):
            self.dma(X[c * 128:(c + 1) * 128, t * T:(t + 1) * T], resid[:, c, :], [r_res[c]], [rX])

    def proj_fm(self, W, c0, nchunks, xT, r_xT, evac_chunk, K=NCD):
        jobs = []
        act = lambda kc: (xT[:, kc, :], r_xT[kc])
        for jb in range(nchunks // 4):
            units = []
            for u in range(2):
                def evac(aps, jb=jb, u=u):
                    for bi, (B, rB) in enumerate(aps):
                        evac_chunk(jb * 4 + u * 2 + bi, B, rB)
                units.append(([dict(segs=[(0, (2 * u + bi) * 128)], act=act, mode="fm", n=T) for bi in range(2)], evac))
            jobs.append(([(W, 0, K, c0 + jb * 512, 512)], units))
        self.run_jobs(jobs)

    def proj_fm_pair(self, W, c0a, c0b, nchunks, xT, r_xT, evac_pair):
        jobs = []
        act = lambda kc: (xT[:, kc, :], r_xT[kc])
        for jb in range(nchunks // 4):
            units = []
            for c in range(4):
                def evac(aps, j=jb * 4 + c):
                    evac_pair(j, aps[0], aps[1])
                units.append(([dict(segs=[(0, c * 128)], act=act, mode="fm", n=T),
                               dict(segs=[(1, c * 128)], act=act, mode="fm", n=T)], evac))
            jobs.append(([(W, 0, NCD, c0a + jb * 512, 512), (W, 0, NCD, c0b + jb * 512, 512)], units))
        self.run_jobs(jobs)

    def proj_tm(self, W, c0, ncols, xT, r_xT, evac):
        jobs = []
        for cg in range(ncols // 512):
            units = []
            for u in range(2):
                def ev(aps, cg=cg, u=u):
                    for bi, (B, rB) in enumerate(aps):
                        evac(2 * u + bi, cg, B, rB)
                bks = []
                for bi in range(2):
                    tb = 2 * u + bi
                    bks.append(dict(segs=[(0, 0)], act=(lambda kc, tb=tb: (xT[:, kc, tb * 128:(tb + 1) * 128], r_xT[kc])), mode="tm", n=512))
                units.append((bks, ev))
            jobs.append(([(W, 0, NCD, c0 + cg * 512, 512)], units))
        self.run_jobs(jobs)

    def rope_tables(self):
        s, ar = self.s, self.ar
        mark = ar.off
        posi = ar.take([128, T], I32)
        ang = ar.take([128, T], F32)
        twopi = ar.take([128, T], F32)
        ni = ar.take([128, T], I32)
        a1 = ar.take([128, T], F32)
        a2 = ar.take([128, T], F32)
        r = rl(5)
        for t in range(self.NT):
            self.dma(posi[:, :], self.pos[0:1, t * T:(t + 1) * T].partition_broadcast(128), [], [r[0]])
            s.op("dve", lambda e: e.tensor_copy(out=ang[:, :], in_=posi[:, :]), reads=[r[0]], writes=[r[1]])
            s.op("dve", lambda e: e.tensor_scalar(out=ang[:, :], in0=ang[:, :], scalar1=self.cvec[:, 0:1], scalar2=None, op0=ALU.mult),
                 reads=[r[1], self.r_const], writes=[r[1]])
            for which, (dst, shift) in enumerate(((self.ST, 0.0), (self.CT, 0.25))):
                a = a1 if which == 0 else a2
                ra = r[3 + which]
                s.op("dve", lambda e, a=a, sh=shift: e.tensor_scalar(out=a[:, :], in0=ang[:, :], scalar1=1.0 / (2.0 * math.pi), scalar2=sh, op0=ALU.mult, op1=ALU.add),
                     reads=[r[1]], writes=[ra])
                s.op("dve", lambda e, a=a: e.tensor_copy(out=ni[:, :], in_=a[:, :]), reads=[ra], writes=[r[2]])
                s.op("dve", lambda e, a=a: e.tensor_copy(out=twopi[:, :], in_=ni[:, :]), reads=[r[2]], writes=[r[2]])
                s.op("dve", lambda e, a=a: e.tensor_tensor(out=a[:, :], in0=a[:, :], in1=twopi[:, :], op=ALU.subtract), reads=[ra, r[2]], writes=[ra])
                s.op("dve", lambda e, a=a: e.tensor_scalar(out=twopi[:, :], in0=a[:, :], scalar1=0.5, scalar2=None, op0=ALU.is_gt), reads=[ra, r[2]], writes=[r[2]])
                s.op("dve", lambda e, a=a: e.tensor_tensor(out=a[:, :], in0=a[:, :], in1=twopi[:, :], op=ALU.subtract), reads=[ra, r[2]], writes=[ra])
                s.op("dve", lambda e, a=a: e.tensor_scalar(out=twopi[:, :], in0=a[:, :], scalar1=-0.5, scalar2=None, op0=ALU.is_lt), reads=[ra, r[2]], writes=[r[2]])
                s.op("dve", lambda e, a=a: e.tensor_tensor(out=a[:, :], in0=a[:, :], in1=twopi[:, :], op=ALU.add), reads=[ra, r[2]], writes=[ra])
                s.op("dve", lambda e, a=a: e.tensor_scalar(out=a[:, :], in0=a[:, :], scalar1=2.0 * math.pi - 1e-6, scalar2=None, op0=ALU.mult), reads=[ra], writes=[ra])
                s.op("act", lambda e, a=a: e.activation(out=a[:, :], in_=a[:, :], func=AF.Sin), reads=[ra], writes=[ra])
                if which == 0:
                    s.op("dve", lambda e, a=a: e.tensor_scalar(out=a[:, :], in0=a[:, :], scalar1=self.cvec[:, 1:2], scalar2=None, op0=ALU.mult),
                         reads=[ra, self.r_const], writes=[ra])
                self.dma(dst[:, t * T:(t + 1) * T], a[:, :], [ra], [self.r_rope])
        s.barrier()
        ar.off = mark

    def layer_prep(self, l):
        s, ar = self.s, self.ar
        lam_init = 0.8 - 0.6 * math.exp(-0.3 * l)
        mark = ar.off
        lt = ar.take([128, 256], F32)
        pr = ar.take([128, 64], F32)
        e12 = ar.take([128, 2], F32)
        wtmp = ar.take([128, 128], F32)
        r = rl(4)
        self.dma(lt[:, :], self.p_lam[l:l + 1, :, :].rearrange("o a b -> o (a b)").partition_broadcast(128), [], [r[0]])
        for i in range(2):
            s.op("dve", lambda e, i=i: e.tensor_tensor(out=pr[:, :], in0=lt[:, i * 128:i * 128 + 64], in1=lt[:, i * 128 + 64:i * 128 + 128], op=ALU.mult),
                 reads=[r[0]], writes=[r[1]])
            s.op("dve", lambda e, i=i: e.reduce_sum(out=e12[:, i:i + 1], in_=pr[:, :], axis=mybir.AxisListType.X), reads=[r[1]], writes=[r[2]])
        s.op("act", lambda e: e.activation(out=e12[:, :], in_=e12[:, :], func=AF.Exp), reads=[r[2]], writes=[r[2]])
        s.op("dve", lambda e: e.tensor_tensor(out=self.neglam[:, :], in0=e12[:, 1:2], in1=e12[:, 0:1], op=ALU.subtract), reads=[r[2]], writes=[self.r_qk])
        s.op("dve", lambda e: e.tensor_scalar(out=self.neglam[:, :], in0=self.neglam[:, :], scalar1=-lam_init, scalar2=None, op0=ALU.add),
             reads=[self.r_qk], writes=[self.r_qk])
        o = (l * 4 + 3) * 8
        s.op("dve", lambda e: e.tensor_scalar(out=self.gscale[:, :], in0=self.col_p[:, o:o + 8], scalar1=1.0 - lam_init, scalar2=None, op0=ALU.mult),
             reads=[self.r_const], writes=[self.r_qk])
        s.op("dve", lambda e: e.memset(self.qmax[:, :], 0.0), writes=[self.r_qk])
        s.op("dve", lambda e: e.memset(self.kmax[:, :], 0.0), writes=[self.r_qk])
        s.op("dve", lambda e: e.memset(self.halo[:, :, :], 0.0), writes=[self.r_halo])
        for g in range(4):
            self.dma(wtmp[:, :], self.p_wsT[l, g, :, :], [], [r[3]])
            s.op("dve", lambda e, g=g: e.tensor_tensor(out=self.wm[g][:, :], in0=wtmp[:, :], in1=self.mchunk_f[:, :], op=ALU.mult),
                 reads=[r[3], self.r_const], writes=[self.r_qk])
            self.dma(self.bs1[g][:, :], self.p_bs[l, g:g + 1, :].partition_broadcast(128), [], [self.r_qk])
        s.barrier()
        ar.off = mark

    def stage_a(self, l, t, Xin, rXin):
        s, ar = self.s, self.ar
        mark = ar.off
        S = self.S
        resid = ar.take([128, NCD, T], F32); r_res = rl(NCD)
        xT = ar.take([128, NCD, T], BF16); r_xT = rl(NCD)
        scr = self.ln_scratch()
        ph = ar.off
        hT = ar.take([128, NCF, T], BF16); r_hT = rl(NCF)
        self.load_resid(Xin, rXin, t, resid, r_res, xT, r_xT)
        self.ffn_ln(l, self.w["ffn1_w_in"], self.w["ffn1_w_out"], 0, resid, r_res, xT, r_xT, hT, r_hT, scr)
        self.store_resid(self.X1, self.r_X1[t], t, resid, r_res)
        Wi = self.w["w_in"][l]
        cols = slice(t * T, (t + 1) * T)
        if getattr(self, 'upto', 9) < 1:
            s.barrier(); ar.off = mark; return
        s.barrier()
        ar.off = ph
        ua = ar.take([128, 8, T], BF16); r_ua = rl(8)
        v_tm = [ar.take([128, BR], F32) for _ in range(4)]; r_v = rl(4)
        vn = [ar.take([128, BR], BF16) for _ in range(4)]; r_vn = rl(4)
        grow = ar.take([128, BR], F32); brow = ar.take([128, BR], F32); r_gb = Res()
        st = ar.take([128, 8], F32); r_st = Res()
        sqv = ar.take([128, BR], F32); r_sqv = Res()
        oa = [ar.take([128, T], BF16) for _ in range(2)]; r_oa = rl(2)
        otmp = [ar.take([128, T], F32) for _ in range(2)]; r_ot = rl(2)
        self.dma(grow[:, :], self.p_row[l, 0:1, :].partition_broadcast(128), [], [r_gb])
        self.dma(brow[:, :], self.p_row[l, 1:2, :].partition_broadcast(128), [], [r_gb])

        def ev_ua(c, B, rB):
            s.op("act", lambda e: e.activation(out=ua[:, c, :], in_=B[:, :], func=AF.Copy), reads=[rB], writes=[r_ua[c]])
        self.proj_fm(Wi, 0, 8, xT, r_xT, ev_ua)

        if getattr(self, 'gstep', 9) < 1:
            s.barrier(); ar.off = mark; return
        def ev_va(tb, cg, B, rB):
            s.op("act", lambda e: e.activation(out=v_tm[tb][:, cg * 512:(cg + 1) * 512], in_=B[:, :], func=AF.Copy), reads=[rB], writes=[r_v[tb]])
        self.proj_tm(Wi, 1024, 1024, xT, r_xT, ev_va)
        if getattr(self, 'gstep', 9) < 2:
            s.barrier(); ar.off = mark; return
        for tb in range(4):
            v = v_tm[tb]
            s.op("dve", lambda e, v=v: e.reduce_sum(out=st[:, 0:1], in_=v[:, :], axis=mybir.AxisListType.X), reads=[r_v[tb]], writes=[r_st])
            s.op("act", lambda e, v=v: e.activation(out=sqv[:, :], in_=v[:, :], func=AF.Square), reads=[r_v[tb]], writes=[r_sqv])
            s.op("dve", lambda e: e.reduce_sum(out=st[:, 1:2], in_=sqv[:, :], axis=mybir.AxisListType.X), reads=[r_sqv], writes=[r_st])
            s.op("dve", lambda e: e.tensor_scalar(out=st[:, 0:2], in0=st[:, 0:2], scalar1=1.0 / BR, scalar2=None, op0=ALU.mult), reads=[r_st], writes=[r_st])
            s.op("dve", lambda e: e.tensor_tensor(out=st[:, 2:3], in0=st[:, 0:1], in1=st[:, 0:1], op=ALU.mult), reads=[r_st], writes=[r_st])
            s.op("dve", lambda e: e.tensor_tensor(out=st[:, 2:3], in0=st[:, 1:2], in1=st[:, 2:3], op=ALU.subtract), reads=[r_st], writes=[r_st])
            s.op("dve", lambda e: e.tensor_scalar(out=st[:, 2:3], in0=st[:, 2:3], scalar1=LN_EPS, scalar2=None, op0=ALU.add), reads=[r_st], writes=[r_st])
            s.op("act", lambda e: e.activation(out=st[:, 2:3], in_=st[:, 2:3], func=AF.Sqrt), reads=[r_st], writes=[r_st])
            s.op("dve", lambda e: e.reciprocal(out=st[:, 3:4], in_=st[:, 2:3]), reads=[r_st], writes=[r_st])
            s.op("dve", lambda e, v=v: e.tensor_scalar(out=v[:, :], in0=v[:, :], scalar1=st[:, 0:1], scalar2=st[:, 3:4], op0=ALU.subtract, op1=ALU.mult),
                 reads=[r_v[tb], r_st], writes=[r_v[tb]])
            s.op("dve", lambda e, v=v: e.tensor_tensor(out=v[:, :], in0=v[:, :], in1=grow[:, :], op=ALU.mult), reads=[r_v[tb], r_gb], writes=[r_v[tb]])
            s.op("dve", lambda e, v=v, tb=tb: e.tensor_tensor(out=vn[tb][:, :], in0=v[:, :], in1=brow[:, :], op=ALU.add),
                 reads=[r_v[tb], r_gb], writes=[r_vn[tb]])
        if getattr(self, 'gstep', 9) < 3:
            s.barrier(); ar.off = mark; return
        for c in range(8):
            b = 6 + (c % 2)
            g = c // 2
            for tb in range(4):
                s.op("pe", lambda e, b=b, tb=tb, c=c, g=g: e.matmul(self.bank[b][:, tb * 128:(tb + 1) * 128], vn[tb][:, c * 128:(c + 1) * 128],
                                                                    self.wm[g][:, :], start=True, stop=True),
                     reads=[r_vn[tb], self.r_qk], writes=[self.r_bank[b]])
            i = c % 2
            for tb in range(4):
                s.op("dve", lambda e, b=b, tb=tb, g=g, i=i: e.tensor_tensor(out=otmp[i][:, tb * 128:(tb + 1) * 128], in0=self.bank[b][:, tb * 128:(tb + 1) * 128],
                                                                          in1=self.bs1[g][:, :], op=ALU.add),
                     reads=[self.r_bank[b], self.r_qk], writes=[r_ot[i]])
            s.op("dve", lambda e, i=i, c=c: e.tensor_tensor(out=oa[i][:, :], in0=otmp[i][:, :], in1=ua[:, c, :], op=ALU.mult),
                 reads=[r_ot[i], r_ua[c]], writes=[r_oa[i]])
            self.dma(self.OA[c * 128:(c + 1) * 128, cols], oa[i][:, :], [r_oa[i]], [self.r_o["OA"][t]])
        if getattr(self, 'upto', 9) < 2:
            s.barrier(); ar.off = mark; return
        s.barrier()
        ar.off = ph
        Ct = ar.take([128, T], F32); St = ar.take([128, T], F32); r_cs = Res()
        qb = [ar.take([128, T], BF16) for _ in range(2)]; r_qb = rl(2)
        t1 = [ar.take([128, T], F32) for _ in range(2)]; r_t1 = rl(2)
        t2 = [ar.take([128, T], F32) for _ in range(2)]; r_t2 = rl(2)
        qr = [ar.take([128, T], BF16) for _ in range(2)]; r_qr = rl(2)
        sqb = [ar.take([128, T], BF16) for _ in range(2)]; r_sqb = rl(2)
        m1 = ar.take([128, 2], F32); r_m1 = rl(2)
        vt = [ar.take([128, 512], BF16) for _ in range(2)]; r_vt = rl(2)
        self.dma(Ct[:, :], self.CT[:, cols], [self.r_rope], [r_cs])
        self.dma(St[:, :], self.ST[:, cols], [self.r_rope], [r_cs])
        cnt = {"i": 0}

        def mk_rope(dst, mx):
            def ev(h, B, rB):
                i = cnt["i"]; cnt["i"] = 1 - i
                b = 6 + i
                qs = getattr(self, 'qstep', 9)
                s.op("act", lambda e: e.activation(out=qb[i][:, :], in_=B[:, :], func=AF.Copy), reads=[rB], writes=[r_qb[i]])
                if qs == 1:
                    self.dma(dst[h, :, cols], qb[i][:, :], [r_qb[i]], [self.r_att_in]); return
                s.op("pe", lambda e: e.matmul(self.bank[b][:, :], self.rperm_b[:, :], qb[i][:, :], start=True, stop=True),
                     reads=[r_qb[i], self.r_const], writes=[self.r_bank[b]])
                if qs == 2:
                    self.dma(dst[h, :, cols], qb[i][:, :], [r_qb[i]], [self.r_att_in]); return
                s.op("dve", lambda e: e.tensor_tensor(out=t1[i][:, :], in0=B[:, :], in1=Ct[:, :], op=ALU.mult), reads=[rB, r_cs, r_qb[i]], writes=[r_t1[i]])
                if qs == 3:
                    self.dma(dst[h, :, cols], qb[i][:, :], [r_qb[i]], [self.r_att_in]); return
                s.op("dve", lambda e: e.tensor_tensor(out=t2[i][:, :], in0=self.bank[b][:, :], in1=St[:, :], op=ALU.mult),
                     reads=[self.r_bank[b], r_cs], writes=[r_t2[i]])
                s.op("dve", lambda e: e.tensor_tensor(out=qr[i][:, :], in0=t1[i][:, :], in1=t2[i][:, :], op=ALU.add),
                     reads=[r_t1[i], r_t2[i]], writes=[r_qr[i]])
                self.dma(dst[h, :, cols], qr[i][:, :], [r_qr[i]], [self.r_att_in])
                if qs == 4:
                    return
                s.op("act", lambda e: e.activation(out=sqb[i][:, :], in_=qr[i][:, :], func=AF.Square), reads=[r_qr[i]], writes=[r_sqb[i]])
                s.op("pe", lambda e: e.matmul(self.bank[b][:, :], self.ones_b[:, :], sqb[i][:, :], start=True, stop=True),
                     reads=[r_sqb[i], self.r_const], writes=[self.r_bank[b]])
                s.op("dve", lambda e: e.tensor_reduce(out=m1[:, i:i + 1], in_=self.bank[b][:, :], axis=mybir.AxisListType.X, op=ALU.max),
                     reads=[self.r_bank[b]], writes=[r_m1[i]])
                s.op("dve", lambda e: e.tensor_tensor(out=mx[:, h:h + 1], in0=mx[:, h:h + 1], in1=m1[:, i:i + 1], op=ALU.max),
                     reads=[r_m1[i], self.r_qk], writes=[self.r_qk])
            return ev
        if getattr(self, 'qstep', 9) >= 1:
            self.proj_fm(Wi, 2048, 8, xT, r_xT, mk_rope(self.QD, self.qmax))
            self.proj_fm(Wi, 3072, 8, xT, r_xT, mk_rope(self.KD, self.kmax))

        def mk_v(dst):
            def ev(tb, cg, B, rB):
                i = cnt["i"]; cnt["i"] = 1 - i
                s.op("act", lambda e: e.activation(out=vt[i][:, :], in_=B[:, :], func=AF.Copy), reads=[rB], writes=[r_vt[i]])
                for hh in range(4):
                    self.dma(dst[cg * 4 + hh, :, t * 4 + tb, :], vt[i][:, hh * 128:(hh + 1) * 128], [r_vt[i]], [self.r_att_in])
            return ev
        self.proj_tm(Wi, 4096, 1024, xT, r_xT, mk_v(self.VD))
        if getattr(self, 'upto', 9) < 3:
            s.barrier(); ar.off = mark; return

        def mk_plain(dst):
            def ev(h, B, rB):
                i = cnt["i"]; cnt["i"] = 1 - i
                s.op("act", lambda e: e.activation(out=qb[i][:, :], in_=B[:, :], func=AF.Copy), reads=[rB], writes=[r_qb[i]])
                self.dma(dst[h, :, cols], qb[i][:, :], [r_qb[i]], [self.r_att_in])
            return ev
        self.proj_fm(Wi, 7168, 8, xT, r_xT, mk_plain(self.SQ))
        self.proj_fm(Wi, 8192, 8, xT, r_xT, mk_plain(self.SK))
        self.proj_tm(Wi, 9216, 1024, xT, r_xT, mk_v(self.SV))
        if getattr(self, 'upto', 9) < 4:
            s.barrier(); ar.off = mark; return
        s.barrier()
        ar.off = ph
        hc = ar.take([128, 8, T + HALO], F32); r_hc = rl(8)
        acc = ar.take([128, 8, T], F32); r_acc = rl(8)
        sg = [ar.take([128, T], F32) for _ in range(2)]; r_sg = rl(2)
        ocb = ar.take([128, 8, T], BF16); r_ocb = rl(8)

        def ev_c(c, A, G):
            (A, rA), (G, rG) = A, G
            i = c % 2
            s.op("pool", lambda e: e.tensor_copy(out=hc[:, c, 0:HALO], in_=self.halo[:, c, :]), reads=[self.r_halo], writes=[r_hc[c]])
            s.op("act", lambda e: e.activation(out=sg[i][:, :], in_=G[:, :], func=AF.Sigmoid), reads=[rG], writes=[r_sg[i]])
            s.op("dve", lambda e: e.tensor_tensor(out=hc[:, c, HALO:HALO + T], in0=A[:, :], in1=sg[i][:, :], op=ALU.mult),
                 reads=[rA, r_sg[i], r_hc[c]], writes=[r_hc[c]])
            eng = "dve"
            wo = (l * 8 + c) * CONV_K
            s.op(eng, lambda e: e.tensor_scalar(out=acc[:, c, :], in0=hc[:, c, 0:T], scalar1=self.convw[:, wo:wo + 1], scalar2=self.colp(l, 0, c),
                                                op0=ALU.mult, op1=ALU.add), reads=[r_hc[c], self.r_const], writes=[r_acc[c]])
            for j in range(1, CONV_K):
                s.op(eng, lambda e, j=j: e.scalar_tensor_tensor(out=acc[:, c, :], in0=hc[:, c, j:j + T], scalar=self.convw[:, wo + j:wo + j + 1],
                                                                in1=acc[:, c, :], op0=ALU.mult, op1=ALU.add),
                     reads=[r_hc[c], r_acc[c], self.r_const], writes=[r_acc[c]])
        self.proj_fm_pair(Wi, 5120, 6144, 8, xT, r_xT, ev_c)
        for c in range(8):
            s.op("pool", lambda e, c=c: e.tensor_copy(out=self.halo[:, c, :], in_=hc[:, c, T:T + HALO]), reads=[r_hc[c]], writes=[self.r_halo])
        self.ln_fm(acc, r_acc, 8, LN_EPS, lambda c: self.colp(l, 1, c), lambda c: self.colp(l, 2, c),
                   lambda c: (ocb[:, c, :], r_ocb[c]), scr, func=AF.Silu)
        for c in range(8):
            self.dma(self.OC[c * 128:(c + 1) * 128, cols], ocb[:, c, :], [r_ocb[c]], [self.r_o["OC"][t]])
        s.barrier()
        ar.off = mark

    def stage_att(self, l):
        s, ar = self.s, self.ar
        S = self.S
        NG = S // T
        NB = S // 128
        mark = ar.off
        tq = ar.take([128, 8], F32); r_tq = Res()
        s.op("dve", lambda e: e.tensor_tensor(out=tq[:, :], in0=self.qmax[:, :], in1=self.kmax[:, :], op=ALU.mult), reads=[self.r_qk], writes=[r_tq])
        s.op("act", lambda e: e.activation(out=tq[:, :], in_=tq[:, :], func=AF.Sqrt), reads=[r_tq], writes=[r_tq])
        s.op("dve", lambda e: e.tensor_scalar(out=self.negm[:, :], in0=tq[:, :], scalar1=-1.02 / 8.0, scalar2=None, op0=ALU.mult), reads=[r_tq], writes=[self.r_qk])
        kT = [ar.take([128, S], BF16) for _ in range(2)]
        qT = [ar.take([128, S], BF16) for _ in range(2)]
        V = [ar.take([128, NB, 128], BF16) for _ in range(2)]
        r_in = rl(2)
        PT = [ar.take([128, T], BF16) for _ in range(3)]; r_PT = rl(3)
        f = [ar.take([128, T], F32) for _ in range(6)]; r_f = rl(6)
        ob = [ar.take([128, T], BF16) for _ in range(2)]; r_ob = rl(2)
        Trun = ar.take([128, T], F32); r_T = Res()
        bank, r_bank = self.bank, self.r_bank
        hb = 0
        pti = 0
        for kind in ("diff", "sb"):
            Qd, Kd, Vd = (self.QD, self.KD, self.VD) if kind == "diff" else (self.SQ, self.SK, self.SV)
            for h in range(8):
                bi = hb % 2
                hb += 1
                k_, q_, v_ = kT[bi], qT[bi], V[bi]
                rin = r_in[bi]
                self.dma(k_[:, :], Kd[h, :, :], [self.r_att_in], [rin])
                self.dma(q_[:, :], Qd[h, :, :], [self.r_att_in], [rin])
                self.dma(v_[:, :, :], Vd[h, :, :, :], [self.r_att_in], [rin])
                for G in range(NG):
                    gc = slice(G * T, (G + 1) * T)
                    if kind == "diff":
                        O = (3, 4); R = (5, 6)
                        nkb = 4 * G + 4
                        sbi = 0
                        for kb in range(nkb):
                            qlo = max(0, kb - 4 * G) * 128
                            for n in range(2):
                                sb_ = sbi % 3; sbi += 1
                                pi = pti % 3; pti += 1
                                pr = slice(n * 64, (n + 1) * 64)
                                s.op("pe", lambda e, sb_=sb_, pr=pr, kb=kb, qlo=qlo, k_=k_, q_=q_, G=G:
                                     e.matmul(bank[sb_][:, qlo:T], k_[pr, kb * 128:(kb + 1) * 128], q_[pr, G * T + qlo:(G + 1) * T], start=True, stop=True),
                                     reads=[rin], writes=[r_bank[sb_]])
                                s.op("act", lambda e, sb_=sb_, pi=pi, qlo=qlo, h=h:
                                     e.activation(out=PT[pi][:, qlo:T], in_=bank[sb_][:, qlo:T], func=AF.Exp, bias=self.negm[:, h:h + 1], scale=0.125),
                                     reads=[r_bank[sb_], self.r_qk], writes=[r_PT[pi]])
                                if kb >= 4 * G:
                                    s.op("dve", lambda e, pi=pi, qlo=qlo: e.tensor_tensor(out=PT[pi][:, qlo:qlo + 128], in0=PT[pi][:, qlo:qlo + 128],
                                                                                           in1=self.mchunk_b[:, :], op=ALU.mult),
                                         reads=[r_PT[pi], self.r_const], writes=[r_PT[pi]])
                                s.op("pe", lambda e, n=n, pi=pi, qlo=qlo, kb=kb, v_=v_, nkb=nkb:
                                     e.matmul(bank[O[n]][:, qlo:T], v_[:, kb, :], PT[pi][:, qlo:T], start=(kb == 0), stop=(kb == nkb - 1)),
                                     reads=[rin, r_PT[pi]], writes=[r_bank[O[n]]])
                                s.op("pe", lambda e, n=n, pi=pi, qlo=qlo, kb=kb, nkb=nkb:
                                     e.matmul(bank[R[n]][:, qlo:T], self.ones_b[:, :], PT[pi][:, qlo:T], start=(kb == 0), stop=(kb == nkb - 1)),
                                     reads=[r_PT[pi], self.r_const], writes=[r_bank[R[n]]])
                        oi = G % 2
                        for n in range(2):
                            s.op("dve", lambda e, n=n: e.reciprocal(out=f[n][:, :], in_=bank[R[n]][:, :]), reads=[r_bank[R[n]]], writes=[r_f[n]])
                            s.op("dve", lambda e, n=n: e.tensor_tensor(out=f[2 + n][:, :], in0=bank[O[n]][:, :], in1=f[n][:, :], op=ALU.mult),
                                 reads=[r_bank[O[n]], r_f[n]], writes=[r_f[2 + n]])
                        s.op("dve", lambda e: e.scalar_tensor_tensor(out=f[2][:, :], in0=f[3][:, :], scalar=self.neglam[:, 0:1], in1=f[2][:, :],
                                                                     op0=ALU.mult, op1=ALU.add), reads=[r_f[2], r_f[3], self.r_qk], writes=[r_f[2]])
                        s.op("act", lambda e: e.activation(out=f[4][:, :], in_=f[2][:, :], func=AF.Square), reads=[r_f[2]], writes=[r_f[4]])
                        s.op("pe", lambda e: e.matmul(bank[7][:, :], self.ones_f[:, :], f[4][:, :], start=True, stop=True),
                             reads=[r_f[4], self.r_const], writes=[r_bank[7]])
                        s.op("dve", lambda e: e.tensor_scalar(out=f[5][:, :], in0=bank[7][:, :], scalar1=1.0 / 128, scalar2=LN_EPS, op0=ALU.mult, op1=ALU.add),
                             reads=[r_bank[7]], writes=[r_f[5]])
                        s.op("act", lambda e: e.activation(out=f[5][:, :], in_=f[5][:, :], func=AF.Sqrt), reads=[r_f[5]], writes=[r_f[5]])
                        s.op("dve", lambda e: e.reciprocal(out=f[5][:, :], in_=f[5][:, :]), reads=[r_f[5]], writes=[r_f[5]])
                        s.op("dve", lambda e: e.tensor_tensor(out=f[2][:, :], in0=f[2][:, :], in1=f[5][:, :], op=ALU.mult), reads=[r_f[2], r_f[5]], writes=[r_f[2]])
                        s.op("act", lambda e, oi=oi, h=h: e.activation(out=ob[oi][:, :], in_=f[2][:, :], func=AF.Identity, scale=self.gscale[:, h:h + 1]),
                             reads=[r_f[2], self.r_qk], writes=[r_ob[oi]])
                        self.dma(self.OB[h * 128:(h + 1) * 128, gc], ob[oi][:, :], [r_ob[oi]], [self.r_o["OB"][G]])
                    else:
                        scale = 128 ** -0.5
                        s.op("pe", lambda e, q_=q_: e.matmul(bank[6][:, :], self.zeros_b[:, :], q_[:, 0:T], start=True, stop=False),
                             reads=[self.r_const, rin], writes=[r_bank[6]])
                        s.op("dve", lambda e: e.memset(Trun[:, :], 0.0), writes=[r_T])
                        it = 0
                        for kb in range(4 * G + 3, -1, -1):
                            qlo = max(0, kb - 4 * G) * 128
                            zb = it % 2; cb = 2 + it % 2; bb = 4 + it % 2
                            fi = it % 2; pi = pti % 3; pti += 1
                            it += 1
                            sp = f[fi]; tmp = f[2 + fi]
                            s.op("pe", lambda e, zb=zb, kb=kb, qlo=qlo, k_=k_, q_=q_, G=G:
                                 e.matmul(bank[zb][:, qlo:T], k_[:, kb * 128:(kb + 1) * 128], q_[:, G * T + qlo:(G + 1) * T], start=True, stop=True),
                                 reads=[rin], writes=[r_bank[zb]])
                            s.op("act", lambda e, zb=zb, sp=sp, qlo=qlo: e.activation(out=sp[:, qlo:T], in_=bank[zb][:, qlo:T], func=AF.Exp, scale=scale),
                                 reads=[r_bank[zb]], writes=[r_f[fi]])
                            s.op("act", lambda e, sp=sp, qlo=qlo: e.activation(out=sp[:, qlo:T], in_=sp[:, qlo:T], func=AF.Ln, bias=1.0),
                                 reads=[r_f[fi]], writes=[r_f[fi]])
                            if kb >= 4 * G:
                                s.op("dve", lambda e, sp=sp, qlo=qlo: e.tensor_tensor(out=sp[:, qlo:qlo + 128], in0=sp[:, qlo:qlo + 128], in1=self.mstrict_f[:, :], op=ALU.mult),
                                     reads=[r_f[fi], self.r_const], writes=[r_f[fi]])
                            s.op("pe", lambda e, cb=cb, sp=sp, qlo=qlo: e.matmul(bank[cb][:, qlo:T], self.utri_f[:, :], sp[:, qlo:T], start=True, stop=True),
                                 reads=[r_f[fi], self.r_const], writes=[r_bank[cb]])
                            s.op("pe", lambda e, bb=bb, sp=sp, qlo=qlo: e.matmul(bank[bb][:, qlo:T], self.ones_f[:, :], sp[:, qlo:T], start=True, stop=True),
                                 reads=[r_f[fi], self.r_const], writes=[r_bank[bb]])
                            s.op("dve", lambda e, cb=cb, tmp=tmp, qlo=qlo: e.tensor_tensor(out=tmp[:, qlo:T], in0=bank[cb][:, qlo:T], in1=Trun[:, qlo:T], op=ALU.add),
                                 reads=[r_bank[cb], r_T], writes=[r_f[2 + fi]])
                            s.op("dve", lambda e, zb=zb, tmp=tmp, qlo=qlo: e.scalar_tensor_tensor(out=tmp[:, qlo:T], in0=bank[zb][:, qlo:T], scalar=scale, in1=tmp[:, qlo:T],
                                                                                                  op0=ALU.mult, op1=ALU.subtract),
                                 reads=[r_bank[zb], r_f[2 + fi]], writes=[r_f[2 + fi]])
                            s.op("act", lambda e, pi=pi, tmp=tmp, qlo=qlo: e.activation(out=PT[pi][:, qlo:T], in_=tmp[:, qlo:T], func=AF.Exp),
                                 reads=[r_f[2 + fi]], writes=[r_PT[pi]])
                            if kb >= 4 * G:
                                s.op("dve", lambda e, pi=pi, qlo=qlo: e.tensor_tensor(out=PT[pi][:, qlo:qlo + 128], in0=PT[pi][:, qlo:qlo + 128],
                                                                                       in1=self.mstrict_b[:, :], op=ALU.mult),
                                     reads=[r_PT[pi], self.r_const], writes=[r_PT[pi]])
                            s.op("pe", lambda e, pi=pi, qlo=qlo, kb=kb, v_=v_: e.matmul(bank[6][:, qlo:T], v_[:, kb, :], PT[pi][:, qlo:T], start=False, stop=(kb == 0)),
                                 reads=[rin, r_PT[pi]], writes=[r_bank[6]])
                            s.op("dve", lambda e, bb=bb, qlo=qlo: e.tensor_tensor(out=Trun[:, qlo:T], in0=Trun[:, qlo:T], in1=bank[bb][:, qlo:T], op=ALU.add),
                                 reads=[r_bank[bb], r_T], writes=[r_T])
                        oi = G % 2
                        s.op("act", lambda e, oi=oi: e.activation(out=ob[oi][:, :], in_=bank[6][:, :], func=AF.Copy), reads=[r_bank[6]], writes=[r_ob[oi]])
                        self.dma(self.OD[h * 128:(h + 1) * 128, gc], ob[oi][:, :], [r_ob[oi]], [self.r_o["OD"][G]])
        s.barrier()
        ar.off = mark

    def stage_b(self, l, t, Xout, rXout):
        s, ar = self.s, self.ar
        mark = ar.off
        cols = slice(t * T, (t + 1) * T)
        resid = ar.take([128, NCD, T], F32); r_res = rl(NCD)
        xT = ar.take([128, NCD, T], BF16); r_xT = rl(NCD)
        scr = self.ln_scratch()
        ph = ar.off
        ot = {k: ar.take([128, 8, T], BF16) for k in ("OA", "OB", "OC", "OD")}
        r_ot = {k: rl(8) for k in ot}
        mg = ar.take([128, NCD, T], BF16); r_mg = rl(NCD)
        acc = [ar.take([128, T], F32) for _ in range(4)]; r_acc = rl(4)
        sg = [ar.take([128, T], F32) for _ in range(2)]; r_sg = rl(2)
        pj = [ar.take([128, T], F32) for _ in range(2)]; r_pj = rl(2)
        self.load_resid(self.X1, self.r_X1[t], t, resid, r_res, xT, r_xT)
        for k, dsrc in (("OA", self.OA), ("OB", self.OB), ("OC", self.OC), ("OD", self.OD)):
            for c in range(8):
                self.dma(ot[k][:, c, :], dsrc[c * 128:(c + 1) * 128, cols], [self.r_o[k][t]], [r_ot[k][c]])
        Wi = self.w["w_in"][l]
        cnt = {"i": 0}
        jobs = []
        names = ("OA", "OB", "OC", "OD")
        for dp in range(4):
            for n in range(4):
                Wb = self.w["w_branch"][l, n]
                panels = [(Wb, 0, 8, dp * 512, 512), (Wi, 0, NCD, 10240 + n * D + dp * 512, 512)]
                units = []
                for c in range(4):
                    def evac(aps, n=n, c=c, dp=dp):
                        (P, rP), (Gt, rG) = aps
                        i = cnt["i"]; cnt["i"] = 1 - i
                        ch = dp * 4 + c
                        s.op("act", lambda e: e.activation(out=sg[i][:, :], in_=Gt[:, :], func=AF.Sigmoid), reads=[rG], writes=[r_sg[i]])
                        if n == 0:
                            s.op("dve", lambda e: e.tensor_tensor(out=acc[c][:, :], in0=P[:, :], in1=sg[i][:, :], op=ALU.mult),
                                 reads=[rP, r_sg[i]], writes=[r_acc[c]])
                        else:
                            s.op("dve", lambda e: e.tensor_tensor(out=pj[i][:, :], in0=P[:, :], in1=sg[i][:, :], op=ALU.mult),
                                 reads=[rP, r_sg[i]], writes=[r_pj[i]])
                            if n < 3:
                                s.op("dve", lambda e: e.tensor_tensor(out=acc[c][:, :], in0=acc[c][:, :], in1=pj[i][:, :], op=ALU.add),
                                     reads=[r_acc[c], r_pj[i]], writes=[r_acc[c]])
                            else:
                                s.op("dve", lambda e: e.tensor_tensor(out=mg[:, ch, :], in0=acc[c][:, :], in1=pj[i][:, :], op=ALU.add),
                                     reads=[r_acc[c], r_pj[i]], writes=[r_mg[ch]])
                    nm = names[n]
                    units.append(([dict(segs=[(0, c * 128)], act=(lambda kc, nm=nm: (ot[nm][:, kc, :], r_ot[nm][kc])), mode="fm", n=T),
                                   dict(segs=[(1, c * 128)], act=(lambda kc: (xT[:, kc, :], r_xT[kc])), mode="fm", n=T)], evac))
                jobs.append((panels, units))
        self.run_jobs(jobs)

        def ev_wo(c, B, rB):
            s.op("dve", lambda e: e.scalar_tensor_tensor(out=resid[:, c, :], in0=B[:, :], scalar=1.0 / ALPHA, in1=resid[:, c, :], op0=ALU.mult, op1=ALU.add),
                 reads=[rB, r_res[c]], writes=[r_res[c]])
        self.proj_fm(self.w["w_out"][l], 0, NCD, mg, r_mg, ev_wo)
        self.ln_fm(resid, r_res, NCD, LN_EPS / ALPHA ** 2, lambda c: self.lnp(l, 2, c), lambda c: self.lnp(l, 3, c),
                   lambda c: (xT[:, c, :], r_xT[c]), scr)
        s.barrier()
        ar.off = ph
        hT = ar.take([128, NCF, T], BF16); r_hT = rl(NCF)
        self.ffn_ln(l, self.w["ffn2_w_in"], self.w["ffn2_w_out"], 4, resid, r_res, xT, r_xT, hT, r_hT, scr)
        self.store_resid(Xout, rXout, t, resid, r_res)
        s.barrier()
        ar.off = mark

    def build_all(self):
        self.setup()
        self.rope_tables()
        Xin, rXin = self.xT, [Res() for _ in range(self.NT)]
        for l in range(self.L):
            self.layer_prep(l)
            for t in range(self.NT):
                self.stage_a(l, t, Xin, rXin[t])
            self.stage_att(l)
            last = l == self.L - 1
            Xo = self.out if last else self.X[l % 2]
            rXo = [Res() for _ in range(self.NT)]
            for t in range(self.NT):
                self.stage_b(l, t, Xo, rXo[t])
            Xin, rXin = Xo, rXo
        self.s.emit()


def host_inputs(b, inp, S, L, bi):
    f = np.float32
    m = {}
    m["xT"] = np.ascontiguousarray(inp["x"][bi, :S].T)
    m["pos"] = np.ascontiguousarray(inp["positions"][bi:bi + 1, :S]).astype(np.int32)
    for k in ("ffn1_w_in", "ffn1_w_out", "w_in", "w_branch", "w_out", "ffn2_w_in", "ffn2_w_out"):
        m[k] = np.ascontiguousarray(inp[k][:L])
    ln = np.stack([inp[k][:L] for k in ("ln1_g", "ln1_b", "ln2_g", "ln2_b", "ln3_g", "ln3_b")], axis=1)
    m["p_ln"] = np.ascontiguousarray(ln.reshape(L, 6, NCD, 128).transpose(3, 0, 1, 2)).astype(f)
    row = np.stack([inp["gmlp_ln_g"][:L], inp["gmlp_ln_b"][:L], inp["conv_b"][:L], inp["diff_norm_g"][:L], inp["conv_ln_g"][:L]], axis=1)
    m["p_row"] = np.ascontiguousarray(row).astype(f)
    col = np.stack([inp[k][:L] for k in ("conv_b", "conv_ln_g", "conv_ln_b", "diff_norm_g")], axis=1)
    m["p_col"] = np.ascontiguousarray(col.reshape(L, 4, 8, 128).transpose(3, 0, 1, 2)).astype(f)
    m["p_convw"] = np.ascontiguousarray(inp["conv_w"][:L].reshape(L, CONV_K, 8, 128).transpose(3, 0, 2, 1)).astype(f)
    m["p_wsT"] = np.ascontiguousarray(inp["gmlp_ws"][:L].transpose(0, 1, 3, 2)).astype(f)
    m["p_bs"] = np.ascontiguousarray(inp["gmlp_bs"][:L]).astype(f)
    m["p_lam"] = np.ascontiguousarray(np.stack([inp["diff_lq1"][:L], inp["diff_lk1"][:L], inp["diff_lq2"][:L], inp["diff_lk2"][:L]], axis=1)).astype(f)
    for k, v in build_consts().items():
        m["c_" + k] = v
    return m


_CACHE = {}


def run(inputs, S=4096, L=DEPTH, n_cores=4, dbg=None):
    inputs = {k: np.asarray(v) for k, v in inputs.items()}
    B = inputs["x"].shape[0]
    b = Builder(S, L, dbg)
    b.build_all()
    in_maps = [host_inputs(b, inputs, S, L, c % B) for c in range(n_cores)]
    res = run_bass_kernel_spmd(b.nc, in_maps, core_ids=list(range(n_cores)))
    out = np.stack([np.ascontiguousarray(res.results[c]["outT"].T) for c in range(B)], axis=0)
    return out.astype(np.float32), res


def kernel(**inputs):
    out, _ = run(inputs)
    return out
```

```python
import math
import numpy as np
import concourse.bass as bass
import concourse.mybir as mybir
from concourse.bass_utils import run_bass_kernel_spmd

F32 = mybir.dt.float32
BF16 = mybir.dt.bfloat16
I32 = mybir.dt.int32
ALU = mybir.AluOpType
AF = mybir.ActivationFunctionType

D = 2048
FF = 5632
NCD = 16
NCF = 44
BR = 1024
IN_COLS = 18432
DEPTH = 4
ALPHA = (2 * DEPTH) ** 0.25
LN_EPS = 1e-5
T = 512
CONV_K = 31
HALO = CONV_K - 1
ROPE_THETA = 500000.0
RDMA = 12


class Res:
    __slots__ = ("w", "r", "excl")

    def __init__(self, excl=False):
        self.w = {}
        self.r = {}
        self.excl = excl


def rl(n):
    return [Res() for _ in range(n)]


class Sched:
    def __init__(self, nc):
        self.nc = nc
        self.streams = {k: [] for k in ("pe", "act", "dve", "pool", "sp")}
        self.count = {k: 0 for k in self.streams}
        self.dcount = {"sp": 0, "pool": 0, "act": 0}
        self.seen = {k: {} for k in self.streams}
        self.latest = {}
        self.barrier_tok = {}

    def barrier(self):
        self.barrier_tok = {k: v for k, v in self.latest.items() if not (k[0] == "d" and k[1] == "pool")}

    def op(self, stream, fn, reads=(), writes=(), dma=False, awrites=()):
        deps = dict(self.barrier_tok)

        def add(d):
            for k, v in d.items():
                if deps.get(k, 0) < v:
                    deps[k] = v

        for r in reads:
            add(r.w)
            if r.excl:
                add(r.r)
        for w in writes:
            add(w.w)
            add(w.r)
        for w in awrites:
            add(w.r)
        if dma:
            k = self.dcount[stream]
            self.dcount[stream] = k + 1
            sem = ("d", stream, k % RDMA)
            val = 16 * (k // RDMA + 1)
            if k >= RDMA:
                if deps.get(sem, 0) < val - 16:
                    deps[sem] = val - 16
            inc = 16
        else:
            self.count[stream] += 1
            sem = ("e", stream)
            val = self.count[stream]
            inc = 1
        seen = self.seen[stream]
        waits = []
        for k, v in deps.items():
            if k == ("e", "pe") and stream == "pe":
                continue
            if seen.get(k, 0) < v:
                seen[k] = v
                waits.append((k, v))
        self.streams[stream].append((waits, fn, sem, inc))
        self.latest[sem] = val
        for r in reads:
            if r.r.get(sem, 0) < val:
                r.r[sem] = val
        for w in writes:
            w.w = {sem: val}
            w.r = {}
        for w in awrites:
            if w.w.get(sem, 0) < val:
                w.w[sem] = val

    def emit(self, final_waits_stream="sp"):
        nc = self.nc
        keys = set()
        for st in self.streams.values():
            for waits, fn, sem, inc in st:
                keys.add(sem)
        import contextlib
        with contextlib.ExitStack() as es:
            semobj = {}
            for k in sorted(keys):
                semobj[k] = es.enter_context(nc.semaphore("s_" + "_".join(str(x) for x in k)))
            block = es.enter_context(nc.Block())
            latest = dict(self.latest)

            def runner(name, final):
                lst = self.streams[name]

                def run(e):
                    for waits, fn, sem, inc in lst:
                        for k, v in waits:
                            e.wait_ge(semobj[k], v)
                        fn(e).then_inc(semobj[sem], inc)
                    if final:
                        for k, v in latest.items():
                            e.wait_ge(semobj[k], v)
                return run

            block.tensor(runner("pe", False))
            block.scalar(runner("act", False))
            block.vector(runner("dve", False))
            block.gpsimd(runner("pool", False))
            block.sync(runner("sp", True))


class Arena:
    def __init__(self, nc, base, limit):
        self.nc = nc
        self.off = base
        self.limit = limit
        self.n = 0

    def take(self, shape, dtype):
        nbytes = int(np.prod(shape[1:])) * (4 if dtype in (F32, I32) else 2)
        nbytes = (nbytes + 63) // 64 * 64
        assert self.off + nbytes <= self.limit, ("SBUF overflow", self.off, nbytes, self.limit)
        self.n += 1
        t = self.nc.alloc_sbuf_tensor_at(f"t{self.n}_{self.off}", list(shape), dtype, offset=self.off)
        self.off += nbytes
        return t


def build_consts():
    c = {}
    s = np.arange(128)
    c["ones_f"] = np.ones((128, 128), np.float32)
    c["utri_f"] = (s[:, None] >= s[None, :]).astype(np.float32)
    c["mchunk"] = ((s[:, None] // 64) <= (s[None, :] // 64)).astype(np.float32)
    c["mstrict"] = (s[:, None] < s[None, :]).astype(np.float32)
    rp = np.zeros((128, 128), np.float32)
    invf = np.zeros((128,), np.float32)
    sgn = np.zeros((128,), np.float32)
    freqs = (ROPE_THETA ** (-np.arange(0, 16, 2, dtype=np.float32) / 16)).astype(np.float32)
    for base in (0, 64):
        for i in range(8):
            a, b = base + i, base + 8 + i
            rp[b, a] = 1.0
            rp[a, b] = 1.0
            invf[a] = freqs[i]
            invf[b] = freqs[i]
            sgn[a] = -1.0
            sgn[b] = 1.0
    c["rperm"] = rp
    e2 = np.zeros((128, 4), np.float32); e2[:, 0] = 1.0; e2[:, 3] = 1.0
    c["e2"] = e2
    sel = np.zeros((128, 256), np.float32); sel[0, 0:128] = 1.0; sel[1, 128:256] = 1.0
    c["sel"] = sel
    c["vec"] = np.stack([invf, sgn], axis=1).astype(np.float32)
    return c


class Builder:
    def __init__(self, S, L, dbg=None, hints=None, no_pre=()):
        self.hints = hints
        self.no_pre = set(no_pre)
        self.first_log = []
        self.conv_pending = []
        self.no_pre_log = set()
        self.rj_idx = 0
        self.pre_loaded = None
        self.S = S
        self.L = L
        self.NT = S // T
        self.dbg = dbg or set()
        nc = self.nc = bass.Bass("TRN2", target_bir_lowering=False)
        self.s = Sched(nc)
        self.inputs = {}
        self.outputs = {}

    def din(self, name, shape, dtype=F32):
        h = self.nc.dram_tensor(name, list(shape), dtype, kind="ExternalInput")
        self.inputs[name] = h
        return h

    def dscr(self, name, shape, dtype):
        kind = "ExternalOutput" if name in self.dbg else "Internal"
        h = self.nc.dram_tensor(name, list(shape), dtype, kind=kind)
        if kind == "ExternalOutput":
            self.outputs[name] = h
        return h

    def dma(self, out, in_, reads, writes, q="sp", awrites=()):
        self.s.op(q, lambda e: e.dma_start(out=out, in_=in_), reads=reads, writes=writes, dma=True, awrites=awrites)

    WSHAPES = {"ffn1_w_in": (D, 2 * FF), "ffn1_w_out": (FF, D), "w_in": (D, IN_COLS), "w_branch": (4 * BR, D),
               "w_out": (D, D), "ffn2_w_in": (D, 2 * FF), "ffn2_w_out": (FF, D)}

    def wv(self, name, l):
        return (name, l)

    def wres(self, name, l):
        return (self.wb[name][l % 2], self.r_wb[name][l % 2])

    def convert_layer(self, l):
        for name, (rows, ncols) in self.WSHAPES.items():
            src = self.w[name]
            if name == "w_branch":
                src2 = src[l].rearrange("n r c -> (n r) c")
            else:
                src2 = src[l]
            dst, res = self.wres(name, l)
            for rb in range(rows // 128):
                c = 0
                while c < ncols:
                    w = min(4096, ncols - c)
                    self.conv_pending.append((dst[rb * 128:(rb + 1) * 128, c:c + w], src2[rb * 128:(rb + 1) * 128, c:c + w], res))
                    c += w

    def conv_tick(self, gate, n):
        reads = [gate] if gate is not None else []
        while n > 0 and self.conv_pending:
            o, i, res = self.conv_pending.pop(0)
            self.dma(o, i, reads, [], q="pool", awrites=[res])
            n -= 1

    def setup(self):
        nc, S, L = self.nc, self.S, self.L
        d = self.din
        self.xT = d("xT", [D, S])
        self.pos = d("pos", [1, S], I32)
        self.w = {}
        for nm, shp in [("ffn1_w_in", [L, D, 2 * FF]), ("ffn1_w_out", [L, FF, D]),
                        ("w_in", [L, D, IN_COLS]), ("w_branch", [L, 4, BR, D]), ("w_out", [L, D, D]),
                        ("ffn2_w_in", [L, D, 2 * FF]), ("ffn2_w_out", [L, FF, D])]:
            self.w[nm] = d(nm, shp)
        self.p_ln = d("p_ln", [128, L, 6, NCD])
        self.p_row = d("p_row", [L, 5, BR])
        self.p_col = d("p_col", [128, L, 4, 8])
        self.p_convw = d("p_convw", [128, L, 8, CONV_K])
        self.p_wsT = d("p_wsT", [L, 4, 128, 128])
        self.p_bs = d("p_bs", [L, 4, 128])
        self.p_lam = d("p_lam", [L, 4, 64])
        self.c_in = {k: d("c_" + k, list(v.shape)) for k, v in build_consts().items()}
        self.out = nc.dram_tensor("outT", [D, S], F32, kind="ExternalOutput")
        self.outputs["outT"] = self.out
        sc = self.dscr
        self.X = [sc("X0", [D, S], F32), sc("X1s", [D, S], F32)]
        self.X1 = sc("X1", [D, S], F32)
        self.QD = sc("QD", [8, 128, S], BF16)
        self.KD = sc("KD", [8, 128, S], BF16)
        self.VD = sc("VD", [8, 128, S // 128, 128], BF16)
        self.SQ = sc("SQ", [8, 128, S], BF16)
        self.SK = sc("SK", [8, 128, S], BF16)
        self.SV = sc("SV", [8, 128, S // 128, 128], BF16)
        self.OA = sc("OA", [BR, S], BF16)
        self.OB = sc("OB", [BR, S], BF16)
        self.OC = sc("OC", [BR, S], BF16)
        self.OD = sc("OD", [BR, S], BF16)
        self.CT = sc("CT", [128, S], F32)
        self.ST = sc("ST", [128, S], F32)
        self.wb = {}
        self.r_wb = {}
        for name, (rows, ncols) in self.WSHAPES.items():
            h = sc("WB_" + name, [2, rows, ncols], BF16)
            self.wb[name] = [h[0], h[1]]
            self.r_wb[name] = [Res(), Res()]
        self.r_X = [rl(self.NT), rl(self.NT)]
        self.r_X1 = rl(self.NT)
        self.r_att_in = Res()
        self.r_o = {k: rl(self.NT) for k in ("OA", "OB", "OC", "OD")}
        self.r_rope = Res()

        self.ar = Arena(nc, 16640, 229000)
        ar = self.ar
        self.ones_f = ar.take([128, 128], F32)
        self.utri_f = ar.take([128, 128], F32)
        self.ones_b = ar.take([128, 128], BF16)
        self.utri_b = ar.take([128, 128], BF16)
        self.mchunk_f = ar.take([128, 128], F32)
        self.mchunk_b = ar.take([128, 128], BF16)
        self.mstrict_f = ar.take([128, 128], F32)
        self.mstrict_b = ar.take([128, 128], BF16)
        self.rperm_b = ar.take([128, 128], BF16)
        self.zeros_b = ar.take([128, 128], BF16)
        self.e2_b = ar.take([128, 4], BF16)
        self.sel_f = ar.take([128, 256], F32)
        self.cvec = ar.take([128, 2], F32)
        self.ln_p = ar.take([128, L * 6 * NCD], F32)
        self.col_p = ar.take([128, L * 4 * 8], F32)
        self.convw = ar.take([128, L * 8 * CONV_K], F32)
        self.halo = ar.take([128, 8, HALO], F32)
        self.qmax = ar.take([128, 8], F32)
        self.kmax = ar.take([128, 8], F32)
        self.negm = ar.take([128, 8], F32)
        self.neglam = ar.take([128, 1], F32)
        self.gscale = ar.take([128, 8], F32)
        self.wm = [ar.take([128, 128], BF16) for _ in range(4)]
        self.bs1 = [ar.take([128, 128], F32) for _ in range(4)]
        self.r_const = Res()
        self.r_halo = Res()
        self.r_qk = Res()
        self.NSLOT = 4
        self.slot_base = ar.off
        self.slots = [ar.take([128, 8192], BF16) for _ in range(self.NSLOT)]
        self.ar2 = Arena(nc, self.slot_base, ar.off)
        self.ar2.n = 100000
        self.r_slot = rl(self.NSLOT)
        self.slot_i = 0
        self.bank = [nc.alloc_psum_tensor(f"bank{i}", [128, 512], F32) for i in range(8)]
        self.r_bank = [Res(excl=True) for _ in range(8)]
        self.unit_i = 0
        self.base = ar.off

        s = self.s
        tmpf = ar.take([128, 128], F32)
        rt = Res()
        for nm, dst_f, dst_b in [("ones_f", self.ones_f, self.ones_b), ("utri_f", self.utri_f, self.utri_b),
                                 ("mchunk", self.mchunk_f, self.mchunk_b), ("mstrict", self.mstrict_f, self.mstrict_b),
                                 ("rperm", None, self.rperm_b)]:
            tgt = dst_f if dst_f is not None else tmpf
            self.dma(tgt[:, :], self.c_in[nm][:, :], [], [rt])
            if dst_b is not None:
                s.op("dve", lambda e, o=dst_b, i=tgt: e.tensor_copy(out=o[:, :], in_=i[:, :]), reads=[rt], writes=[self.r_const])
        s.op("dve", lambda e: e.memset(self.zeros_b[:, :], 0.0), writes=[self.r_const])
        self.dma(tmpf[:, 0:4], self.c_in["e2"][:, :], [], [rt])
        s.op("dve", lambda e: e.tensor_copy(out=self.e2_b[:, :], in_=tmpf[:, 0:4]), reads=[rt], writes=[self.r_const])
        self.dma(self.sel_f[:, :], self.c_in["sel"][:, :], [], [self.r_const])
        self.dma(self.cvec[:, :], self.c_in["vec"][:, :], [], [self.r_const])
        self.dma(self.ln_p[:, :], self.p_ln.ap().rearrange("p l i c -> p (l i c)"), [], [self.r_const])
        self.dma(self.col_p[:, :], self.p_col.ap().rearrange("p l i c -> p (l i c)"), [], [self.r_const])
        self.dma(self.convw[:, :], self.p_convw.ap().rearrange("p l c j -> p (l c j)"), [], [self.r_const])
        s.barrier()
        self.ar.off = self.base = ar.off

    def lnp(self, l, i, c):
        o = (l * 6 + i) * NCD + c
        return self.ln_p[:, o:o + 1]

    def colp(self, l, i, c):
        o = (l * 4 + i) * 8 + c
        return self.col_p[:, o:o + 1]

    def load_panel(self, W, k0, nk, c0, ncols, r0=0):
        W2d, wres = self.wres(*W)
        i = self.slot_i
        self.slot_i = (i + 1) % self.NSLOT
        view = self.slots[i][:, 0:nk * ncols].rearrange("p (k c) -> p k c", c=ncols)
        src = W2d[r0 + k0 * 128:r0 + (k0 + nk) * 128, c0:c0 + ncols].rearrange("(k p) c -> p k c", p=128)
        self.dma(view, src, [wres], [self.r_slot[i]], q="sp")
        return (i, view, k0, nk)

    def next_unit(self):
        u = self.unit_i
        self.unit_i = (u + 1) % 3
        return 2 * u, 2 * u + 1

    def run_jobs(self, jobs):
        loaded = {}

        def issue(j):
            loaded[j] = [self.load_panel(*p) for p in jobs[j][0]]

        k = self.rj_idx
        self.rj_idx += 1
        self.first_log.append(list(jobs[0][0]))
        if self.pre_loaded is not None:
            loaded[0] = self.pre_loaded
            self.pre_loaded = None
        else:
            issue(0)
        for j in range(len(jobs)):
            if j + 1 < len(jobs):
                issue(j + 1)
            elif self.hints is not None and k + 1 < len(self.hints) and (k + 1) not in self.no_pre:
                self.pre_loaded = [self.load_panel(*p) for p in self.hints[k + 1]]
            pan = loaded.pop(j)
            for banks, evac in jobs[j][1]:
                bidx = self.next_unit()
                aps = []
                for bi, bk in enumerate(banks):
                    b = bidx[bi]
                    n = bk["n"]
                    total = sum(pan[pi][3] for pi, _ in bk["segs"])
                    cnt = 0
                    for pi, coff in bk["segs"]:
                        slot_i, view, k0, nk = pan[pi]
                        for kc in range(nk):
                            a_ap, a_res = bk["act"](k0 + kc)
                            first, last = cnt == 0, cnt == total - 1
                            cnt += 1
                            if bk["mode"] == "fm":
                                lhsT, rhs = view[:, kc, coff:coff + 128], a_ap
                            else:
                                lhsT, rhs = a_ap, view[:, kc, coff:coff + n]
                            self.s.op("pe", lambda e, o=self.bank[b][:, 0:n], l=lhsT, r=rhs, f=first, la=last:
                                      e.matmul(o, l, r, start=f, stop=la),
                                      reads=[self.r_slot[slot_i], a_res], writes=[self.r_bank[b]])
                    aps.append((self.bank[b], self.r_bank[b]))
                evac(aps)

    def ln_fm(self, resid, r_res, nchunk, eps, gfn, bfn, out_fn, scr, func=AF.Identity):
        s = self.s
        sq, r_sq, mean, rstd, tmp, r_tmp, r_stat = scr
        n = float(nchunk * 128)
        b6, b7 = self.bank[6], self.bank[7]
        ybf, r_ybf, sqb, r_sqb = self.lnb
        for c in range(nchunk):
            i = c % 2
            s.op("dve", lambda e, o=ybf[i], x=resid[:, c, :]: e.tensor_copy(out=o[:, :], in_=x), reads=[r_res[c]], writes=[r_ybf[i]])
            s.op("act", lambda e, o=sqb[i], x=resid[:, c, :]: e.activation(out=o[:, :], in_=x, func=AF.Square),
                 reads=[r_res[c]], writes=[r_sqb[i]])
            s.op("pe", lambda e, x=ybf[i], f=(c == 0), la=(c == nchunk - 1):
                 e.matmul(b6[:, :], self.ones_b[:, :], x[:, :], start=f, stop=la), reads=[r_ybf[i], self.r_const], writes=[self.r_bank[6]])
            s.op("pe", lambda e, x=sqb[i], f=(c == 0), la=(c == nchunk - 1):
                 e.matmul(b7[:, :], self.ones_b[:, :], x[:, :], start=f, stop=la), reads=[r_sqb[i], self.r_const], writes=[self.r_bank[7]])
        s.op("dve", lambda e: e.tensor_scalar(out=mean[:, :], in0=b6[:, :], scalar1=1.0 / n, scalar2=None, op0=ALU.mult),
             reads=[self.r_bank[6]], writes=[r_stat[0]])
        s.op("dve", lambda e: e.tensor_tensor(out=tmp[0][:, :], in0=mean[:, :], in1=mean[:, :], op=ALU.mult),
             reads=[r_stat[0]], writes=[r_tmp[0]])
        s.op("dve", lambda e: e.scalar_tensor_tensor(out=tmp[0][:, :], in0=b7[:, :], scalar=1.0 / n, in1=tmp[0][:, :],
                                                     op0=ALU.mult, op1=ALU.subtract),
             reads=[self.r_bank[7], r_tmp[0]], writes=[r_tmp[0]])
        s.op("dve", lambda e: e.tensor_scalar(out=tmp[0][:, :], in0=tmp[0][:, :], scalar1=eps, scalar2=None, op0=ALU.add),
             reads=[r_tmp[0]], writes=[r_tmp[0]])
        s.op("act", lambda e: e.activation(out=tmp[0][:, :], in_=tmp[0][:, :], func=AF.Sqrt), reads=[r_tmp[0]], writes=[r_tmp[0]])
        s.op("dve", lambda e: e.reciprocal(out=rstd[:, :], in_=tmp[0][:, :]), reads=[r_tmp[0]], writes=[r_stat[1]])
        for c in range(nchunk):
            i = c % 2
            s.op("dve", lambda e, o=tmp[i], x=resid[:, c, :]: e.tensor_tensor(out=o[:, :], in0=x, in1=mean[:, :], op=ALU.subtract),
                 reads=[r_res[c], r_stat[0]], writes=[r_tmp[i]])
            s.op("dve", lambda e, o=tmp[i]: e.tensor_tensor(out=o[:, :], in0=o[:, :], in1=rstd[:, :], op=ALU.mult),
                 reads=[r_tmp[i], r_stat[1]], writes=[r_tmp[i]])
            s.op("act", lambda e, o=resid[:, c, :], x=tmp[i], g=gfn(c), b=bfn(c): e.activation(out=o, in_=x[:, :], func=func, bias=b, scale=g),
                 reads=[r_tmp[i], self.r_const], writes=[r_res[c]])
            if out_fn is not None:
                o_ap, o_res = out_fn(c)
                s.op("act", lambda e, o=o_ap, x=resid[:, c, :]: e.activation(out=o, in_=x, func=AF.Copy), reads=[r_res[c]], writes=[o_res])

    def ln_scratch(self):
        ar = self.ar
        sq = [None, None]
        mean = ar.take([128, T], F32)
        rstd = ar.take([128, T], F32)
        tmp = [ar.take([128, T], F32) for _ in range(2)]
        self.lnb = ([ar.take([128, T], BF16) for _ in range(2)], rl(2), [ar.take([128, T], BF16) for _ in range(2)], rl(2))
        return (sq, rl(2), mean, rstd, tmp, rl(2), rl(2))

    def ffn_ln(self, l, w_in, w_out, lni, resid, r_res, xT, r_xT, hT, r_hT, scr):
        s = self.s
        W1 = w_in
        W2 = w_out
        silt = [self.ar.take([128, T], F32) for _ in range(2)]
        r_silt = rl(2)
        st = {"i": 0}
        jobs = []
        for jb in range(NCF // 4):
            panels = [(W1, 0, NCD, jb * 512, 512), (W1, 0, NCD, FF + jb * 512, 512)]
            units = []
            for c in range(4):
                j = jb * 4 + c

                def evac(aps, j=j):
                    (A, rA), (G, rG) = aps
                    i = st["i"]
                    st["i"] = 1 - i
                    s.op("act", lambda e: e.activation(out=silt[i][:, :], in_=A[:, :], func=AF.Silu), reads=[rA], writes=[r_silt[i]])
                    s.op("dve", lambda e: e.tensor_tensor(out=hT[:, j, :], in0=silt[i][:, :], in1=G[:, :], op=ALU.mult),
                         reads=[r_silt[i], rG], writes=[r_hT[j]])
                act = lambda kc: (xT[:, kc, :], r_xT[kc])
                units.append(([dict(segs=[(0, c * 128)], act=act, mode="fm", n=T),
                               dict(segs=[(1, c * 128)], act=act, mode="fm", n=T)], evac))
            jobs.append((panels, units))
        self.run_jobs(jobs)
        jobs = []
        for u in range(NCD // 2):
            panels = [(W2, 0, 32, u * 256, 256), (W2, 32, NCF - 32, u * 256, 256)]

            def evac(aps, u=u):
                for bi, (B, rB) in enumerate(aps):
                    c = 2 * u + bi
                    s.op("dve", lambda e, B=B, c=c: e.scalar_tensor_tensor(out=resid[:, c, :], in0=B[:, :], scalar=0.5 / ALPHA,
                                                                           in1=resid[:, c, :], op0=ALU.mult, op1=ALU.add),
                         reads=[rB, r_res[c]], writes=[r_res[c]])
            act = lambda kc: (hT[:, kc, :], r_hT[kc])
            jobs.append((panels, [([dict(segs=[(0, 0), (1, 0)], act=act, mode="fm", n=T),
                                    dict(segs=[(0, 128), (1, 128)], act=act, mode="fm", n=T)], evac)]))
        self.run_jobs(jobs)
        self.ln_fm(resid, r_res, NCD, LN_EPS / ALPHA ** 2, lambda c: self.lnp(l, lni, c), lambda c: self.lnp(l, lni + 1, c),
                   lambda c: (xT[:, c, :], r_xT[c]), scr)

    def load_resid(self, X, rX, t, resid, r_res, xT, r_xT):
        s = self.s
        for c in range(NCD):
            self.dma(resid[:, c, :], X[c * 128:(c + 1) * 128, t * T:(t + 1) * T], [rX], [r_res[c]])
            s.op("act", lambda e, c=c: e.activation(out=xT[:, c, :], in_=resid[:, c, :], func=AF.Copy), reads=[r_res[c]], writes=[r_xT[c]])

    def store_resid(self, X, rX, t, resid, r_res):
        for c in range(NCD):
            self.dma(X[c * 128:(c + 1) * 128, t * T:(t + 1) * T], resid[:, c, :], [r_res[c]], [], awrites=[rX])

    def proj_fm(self, W, c0, nchunks, xT, r_xT, evac_chunk, K=NCD):
        jobs = []
        act = lambda kc: (xT[:, kc, :], r_xT[kc])
        for jb in range(nchunks // 4):
            units = []
            for u in range(2):
                def evac(aps, jb=jb, u=u):
                    for bi, (B, rB) in enumerate(aps):
                        evac_chunk(jb * 4 + u * 2 + bi, B, rB)
                units.append(([dict(segs=[(0, (2 * u + bi) * 128)], act=act, mode="fm", n=T) for bi in range(2)], evac))
            jobs.append(([(W, 0, K, c0 + jb * 512, 512)], units))
        self.run_jobs(jobs)

    def proj_fm_pair(self, W, c0a, c0b, nchunks, xT, r_xT, evac_pair):
        jobs = []
        act = lambda kc: (xT[:, kc, :], r_xT[kc])
        for jb in range(nchunks // 4):
            units = []
            for c in range(4):
                def evac(aps, j=jb * 4 + c):
                    evac_pair(j, aps[0], aps[1])
                units.append(([dict(segs=[(0, c * 128)], act=act, mode="fm", n=T),
                               dict(segs=[(1, c * 128)], act=act, mode="fm", n=T)], evac))
            jobs.append(([(W, 0, NCD, c0a + jb * 512, 512), (W, 0, NCD, c0b + jb * 512, 512)], units))
        self.run_jobs(jobs)

    def proj_tm(self, W, c0, ncols, xT, r_xT, evac):
        jobs = []
        for cg in range(ncols // 512):
            units = []
            for u in range(2):
                def ev(aps, cg=cg, u=u):
                    for bi, (B, rB) in enumerate(aps):
                        evac(2 * u + bi, cg, B, rB)
                bks = []
                for bi in range(2):
                    tb = 2 * u + bi
                    bks.append(dict(segs=[(0, 0)], act=(lambda kc, tb=tb: (xT[:, kc, tb * 128:(tb + 1) * 128], r_xT[kc])), mode="tm", n=512))
                units.append((bks, ev))
            jobs.append(([(W, 0, NCD, c0 + cg * 512, 512)], units))
        self.run_jobs(jobs)

    def rope_tables(self):
        s, ar = self.s, self.ar
        mark = ar.off
        posi = ar.take([128, T], I32)
        ang = ar.take([128, T], F32)
        twopi = ar.take([128, T], F32)
        ni = ar.take([128, T], I32)
        a1 = ar.take([128, T], F32)
        a2 = ar.take([128, T], F32)
        r = rl(5)
        for t in range(self.NT):
            self.dma(posi[:, :], self.pos[0:1, t * T:(t + 1) * T].partition_broadcast(128), [], [r[0]])
            s.op("dve", lambda e: e.tensor_copy(out=ang[:, :], in_=posi[:, :]), reads=[r[0]], writes=[r[1]])
            s.op("dve", lambda e: e.tensor_scalar(out=ang[:, :], in0=ang[:, :], scalar1=self.cvec[:, 0:1], scalar2=None, op0=ALU.mult),
                 reads=[r[1], self.r_const], writes=[r[1]])
            for which, (dst, shift) in enumerate(((self.ST, 0.0), (self.CT, 0.25))):
                a = a1 if which == 0 else a2
                ra = r[3 + which]
                s.op("dve", lambda e, a=a, sh=shift: e.tensor_scalar(out=a[:, :], in0=ang[:, :], scalar1=1.0 / (2.0 * math.pi), scalar2=sh, op0=ALU.mult, op1=ALU.add),
                     reads=[r[1]], writes=[ra])
                s.op("dve", lambda e, a=a: e.tensor_copy(out=ni[:, :], in_=a[:, :]), reads=[ra], writes=[r[2]])
                s.op("dve", lambda e, a=a: e.tensor_copy(out=twopi[:, :], in_=ni[:, :]), reads=[r[2]], writes=[r[2]])
                s.op("dve", lambda e, a=a: e.tensor_tensor(out=a[:, :], in0=a[:, :], in1=twopi[:, :], op=ALU.subtract), reads=[ra, r[2]], writes=[ra])
                s.op("dve", lambda e, a=a: e.tensor_scalar(out=twopi[:, :], in0=a[:, :], scalar1=0.5, scalar2=None, op0=ALU.is_gt), reads=[ra, r[2]], writes=[r[2]])
                s.op("dve", lambda e, a=a: e.tensor_tensor(out=a[:, :], in0=a[:, :], in1=twopi[:, :], op=ALU.subtract), reads=[ra, r[2]], writes=[ra])
                s.op("dve", lambda e, a=a: e.tensor_scalar(out=twopi[:, :], in0=a[:, :], scalar1=-0.5, scalar2=None, op0=ALU.is_lt), reads=[ra, r[2]], writes=[r[2]])
                s.op("dve", lambda e, a=a: e.tensor_tensor(out=a[:, :], in0=a[:, :], in1=twopi[:, :], op=ALU.add), reads=[ra, r[2]], writes=[ra])
                s.op("dve", lambda e, a=a: e.tensor_scalar(out=a[:, :], in0=a[:, :], scalar1=2.0 * math.pi - 1e-6, scalar2=None, op0=ALU.mult), reads=[ra], writes=[ra])
                s.op("act", lambda e, a=a: e.activation(out=a[:, :], in_=a[:, :], func=AF.Sin), reads=[ra], writes=[ra])
                if which == 0:
                    s.op("dve", lambda e, a=a: e.tensor_scalar(out=a[:, :], in0=a[:, :], scalar1=self.cvec[:, 1:2], scalar2=None, op0=ALU.mult),
                         reads=[ra, self.r_const], writes=[ra])
                self.dma(dst[:, t * T:(t + 1) * T], a[:, :], [ra], [], awrites=[self.r_rope])
        s.barrier()
        ar.off = mark

    def layer_prep(self, l):
        s, ar = self.s, self.ar
        lam_init = 0.8 - 0.6 * math.exp(-0.3 * l)
        mark = ar.off
        lt = ar.take([128, 256], F32)
        pr = ar.take([128, 64], F32)
        e12 = ar.take([128, 2], F32)
        wtmp = ar.take([128, 128], F32)
        r = rl(4)
        self.dma(lt[:, :], self.p_lam[l:l + 1, :, :].rearrange("o a b -> o (a b)").partition_broadcast(128), [], [r[0]])
        for i in range(2):
            s.op("dve", lambda e, i=i: e.tensor_tensor(out=pr[:, :], in0=lt[:, i * 128:i * 128 + 64], in1=lt[:, i * 128 + 64:i * 128 + 128], op=ALU.mult),
                 reads=[r[0]], writes=[r[1]])
            s.op("dve", lambda e, i=i: e.reduce_sum(out=e12[:, i:i + 1], in_=pr[:, :], axis=mybir.AxisListType.X), reads=[r[1]], writes=[r[2]])
        s.op("act", lambda e: e.activation(out=e12[:, :], in_=e12[:, :], func=AF.Exp), reads=[r[2]], writes=[r[2]])
        s.op("dve", lambda e: e.tensor_tensor(out=self.neglam[:, :], in0=e12[:, 1:2], in1=e12[:, 0:1], op=ALU.subtract), reads=[r[2]], writes=[self.r_qk])
        s.op("dve", lambda e: e.tensor_scalar(out=self.neglam[:, :], in0=self.neglam[:, :], scalar1=-lam_init, scalar2=None, op0=ALU.add),
             reads=[self.r_qk], writes=[self.r_qk])
        o = (l * 4 + 3) * 8
        s.op("dve", lambda e: e.tensor_scalar(out=self.gscale[:, :], in0=self.col_p[:, o:o + 8], scalar1=1.0 - lam_init, scalar2=None, op0=ALU.mult),
             reads=[self.r_const], writes=[self.r_qk])
        s.op("dve", lambda e: e.memset(self.qmax[:, :], 0.0), writes=[self.r_qk])
        s.op("dve", lambda e: e.memset(self.kmax[:, :], 0.0), writes=[self.r_qk])
        s.op("dve", lambda e: e.memset(self.halo[:, :, :], 0.0), writes=[self.r_halo])
        for g in range(4):
            self.dma(wtmp[:, :], self.p_wsT[l, g, :, :], [], [r[3]])
            s.op("dve", lambda e, g=g: e.tensor_tensor(out=self.wm[g][:, :], in0=wtmp[:, :], in1=self.mchunk_f[:, :], op=ALU.mult),
                 reads=[r[3], self.r_const], writes=[self.r_qk])
            self.dma(self.bs1[g][:, :], self.p_bs[l, g:g + 1, :].partition_broadcast(128), [], [self.r_qk])
        s.barrier()
        ar.off = mark

    def stage_a(self, l, t, Xin, rXin):
        s, ar = self.s, self.ar
        mark = ar.off
        S = self.S
        resid = ar.take([128, NCD, T], F32); r_res = rl(NCD)
        xT = ar.take([128, NCD, T], BF16); r_xT = rl(NCD)
        scr = self.ln_scratch()
        ph = ar.off
        hT = ar.take([128, NCF, T], BF16); r_hT = rl(NCF)
        self.load_resid(Xin, rXin, t, resid, r_res, xT, r_xT)
        self.conv_tick(r_res[0], getattr(self, "conv_rate", 0))
        self.ffn_ln(l, self.wv("ffn1_w_in", l), self.wv("ffn1_w_out", l), 0, resid, r_res, xT, r_xT, hT, r_hT, scr)
        self.store_resid(self.X1, self.r_X1[t], t, resid, r_res)
        Wi = self.wv("w_in", l)
        cols = slice(t * T, (t + 1) * T)
        if getattr(self, 'upto', 9) < 1:
            s.barrier(); ar.off = mark; return
        s.barrier()
        ar.off = ph
        ua = ar.take([128, 8, T], BF16); r_ua = rl(8)
        v_tm = [ar.take([128, BR], F32) for _ in range(4)]; r_v = rl(4)
        vn = [ar.take([128, BR], BF16) for _ in range(4)]; r_vn = rl(4)
        grow = ar.take([128, BR], F32); brow = ar.take([128, BR], F32); r_gb = Res()
        st = ar.take([128, 8], F32); r_st = Res()
        sqv = ar.take([128, BR], F32); r_sqv = Res()
        oa = [ar.take([128, T], BF16) for _ in range(2)]; r_oa = rl(2)
        otmp = [ar.take([128, T], F32) for _ in range(2)]; r_ot = rl(2)
        self.dma(grow[:, :], self.p_row[l, 0:1, :].partition_broadcast(128), [], [r_gb])
        self.dma(brow[:, :], self.p_row[l, 1:2, :].partition_broadcast(128), [], [r_gb])

        def ev_ua(c, B, rB):
            s.op("act", lambda e: e.activation(out=ua[:, c, :], in_=B[:, :], func=AF.Copy), reads=[rB], writes=[r_ua[c]])
        self.proj_fm(Wi, 0, 8, xT, r_xT, ev_ua)

        if getattr(self, 'gstep', 9) < 1:
            s.barrier(); ar.off = mark; return
        def ev_va(tb, cg, B, rB):
            s.op("act", lambda e: e.activation(out=v_tm[tb][:, cg * 512:(cg + 1) * 512], in_=B[:, :], func=AF.Copy), reads=[rB], writes=[r_v[tb]])
        self.proj_tm(Wi, 1024, 1024, xT, r_xT, ev_va)
        if getattr(self, 'gstep', 9) < 2:
            s.barrier(); ar.off = mark; return
        for tb in range(4):
            v = v_tm[tb]
            s.op("dve", lambda e, v=v: e.reduce_sum(out=st[:, 0:1], in_=v[:, :], axis=mybir.AxisListType.X), reads=[r_v[tb]], writes=[r_st])
            s.op("act", lambda e, v=v: e.activation(out=sqv[:, :], in_=v[:, :], func=AF.Square), reads=[r_v[tb]], writes=[r_sqv])
            s.op("dve", lambda e: e.reduce_sum(out=st[:, 1:2], in_=sqv[:, :], axis=mybir.AxisListType.X), reads=[r_sqv], writes=[r_st])
            s.op("dve", lambda e: e.tensor_scalar(out=st[:, 0:2], in0=st[:, 0:2], scalar1=1.0 / BR, scalar2=None, op0=ALU.mult), reads=[r_st], writes=[r_st])
            s.op("dve", lambda e: e.tensor_tensor(out=st[:, 2:3], in0=st[:, 0:1], in1=st[:, 0:1], op=ALU.mult), reads=[r_st], writes=[r_st])
            s.op("dve", lambda e: e.tensor_tensor(out=st[:, 2:3], in0=st[:, 1:2], in1=st[:, 2:3], op=ALU.subtract), reads=[r_st], writes=[r_st])
            s.op("dve", lambda e: e.tensor_scalar(out=st[:, 2:3], in0=st[:, 2:3], scalar1=LN_EPS, scalar2=None, op0=ALU.add), reads=[r_st], writes=[r_st])
            s.op("act", lambda e: e.activation(out=st[:, 2:3], in_=st[:, 2:3], func=AF.Sqrt), reads=[r_st], writes=[r_st])
            s.op("dve", lambda e: e.reciprocal(out=st[:, 3:4], in_=st[:, 2:3]), reads=[r_st], writes=[r_st])
            s.op("dve", lambda e, v=v: e.tensor_scalar(out=v[:, :], in0=v[:, :], scalar1=st[:, 0:1], scalar2=st[:, 3:4], op0=ALU.subtract, op1=ALU.mult),
                 reads=[r_v[tb], r_st], writes=[r_v[tb]])
            s.op("dve", lambda e, v=v: e.tensor_tensor(out=v[:, :], in0=v[:, :], in1=grow[:, :], op=ALU.mult), reads=[r_v[tb], r_gb], writes=[r_v[tb]])
            s.op("dve", lambda e, v=v, tb=tb: e.tensor_tensor(out=vn[tb][:, :], in0=v[:, :], in1=brow[:, :], op=ALU.add),
                 reads=[r_v[tb], r_gb], writes=[r_vn[tb]])
        if getattr(self, 'gstep', 9) < 3:
            s.barrier(); ar.off = mark; return
        for c in range(8):
            b = 6 + (c % 2)
            g = c // 2
            for tb in range(4):
                s.op("pe", lambda e, b=b, tb=tb, c=c, g=g: e.matmul(self.bank[b][:, tb * 128:(tb + 1) * 128], vn[tb][:, c * 128:(c + 1) * 128],
                                                                    self.wm[g][:, :], start=True, stop=True),
                     reads=[r_vn[tb], self.r_qk], writes=[self.r_bank[b]])
            i = c % 2
            for tb in range(4):
                s.op("dve", lambda e, b=b, tb=tb, g=g, i=i: e.tensor_tensor(out=otmp[i][:, tb * 128:(tb + 1) * 128], in0=self.bank[b][:, tb * 128:(tb + 1) * 128],
                                                                          in1=self.bs1[g][:, :], op=ALU.add),
                     reads=[self.r_bank[b], self.r_qk], writes=[r_ot[i]])
            s.op("dve", lambda e, i=i, c=c: e.tensor_tensor(out=oa[i][:, :], in0=otmp[i][:, :], in1=ua[:, c, :], op=ALU.mult),
                 reads=[r_ot[i], r_ua[c]], writes=[r_oa[i]])
            self.dma(self.OA[c * 128:(c + 1) * 128, cols], oa[i][:, :], [r_oa[i]], [], awrites=[self.r_o["OA"][t]])
        if getattr(self, 'upto', 9) < 2:
            s.barrier(); ar.off = mark; return
        s.barrier()
        ar.off = ph
        Ct = ar.take([128, T], F32); St = ar.take([128, T], F32); r_cs = Res()
        qb = [ar.take([128, T], BF16) for _ in range(2)]; r_qb = rl(2)
        t1 = [ar.take([128, T], F32) for _ in range(2)]; r_t1 = rl(2)
        t2 = [ar.take([128, T], F32) for _ in range(2)]; r_t2 = rl(2)
        qr = [ar.take([128, T], BF16) for _ in range(2)]; r_qr = rl(2)
        sqb = [ar.take([128, T], BF16) for _ in range(2)]; r_sqb = rl(2)
        m1 = ar.take([128, 2], F32); r_m1 = rl(2)
        vt = [ar.take([128, 512], BF16) for _ in range(2)]; r_vt = rl(2)
        self.dma(Ct[:, :], self.CT[:, cols], [self.r_rope], [r_cs])
        self.dma(St[:, :], self.ST[:, cols], [self.r_rope], [r_cs])
        cnt = {"i": 0}

        def mk_rope(dst, mx):
            def ev(h, B, rB):
                i = cnt["i"]; cnt["i"] = 1 - i
                b = 6 + i
                qs = getattr(self, 'qstep', 9)
                s.op("act", lambda e: e.activation(out=qb[i][:, :], in_=B[:, :], func=AF.Copy), reads=[rB], writes=[r_qb[i]])
                if qs == 1:
                    self.dma(dst[h, :, cols], qb[i][:, :], [r_qb[i]], [], awrites=[self.r_att_in]); return
                s.op("pe", lambda e: e.matmul(self.bank[b][:, :], self.rperm_b[:, :], qb[i][:, :], start=True, stop=True),
                     reads=[r_qb[i], self.r_const], writes=[self.r_bank[b]])
                if qs == 2:
                    self.dma(dst[h, :, cols], qb[i][:, :], [r_qb[i]], [], awrites=[self.r_att_in]); return
                s.op("dve", lambda e: e.tensor_tensor(out=t1[i][:, :], in0=B[:, :], in1=Ct[:, :], op=ALU.mult), reads=[rB, r_cs, r_qb[i]], writes=[r_t1[i]])
                if qs == 3:
                    self.dma(dst[h, :, cols], qb[i][:, :], [r_qb[i]], [], awrites=[self.r_att_in]); return
                s.op("dve", lambda e: e.tensor_tensor(out=t2[i][:, :], in0=self.bank[b][:, :], in1=St[:, :], op=ALU.mult),
                     reads=[self.r_bank[b], r_cs], writes=[r_t2[i]])
                s.op("dve", lambda e: e.tensor_tensor(out=qr[i][:, :], in0=t1[i][:, :], in1=t2[i][:, :], op=ALU.add),
                     reads=[r_t1[i], r_t2[i]], writes=[r_qr[i]])
                self.dma(dst[h, :, cols], qr[i][:, :], [r_qr[i]], [], awrites=[self.r_att_in])
                if qs == 4:
                    return
                s.op("act", lambda e: e.activation(out=sqb[i][:, :], in_=qr[i][:, :], func=AF.Square), reads=[r_qr[i]], writes=[r_sqb[i]])
                s.op("pe", lambda e: e.matmul(self.bank[b][:, :], self.ones_b[:, :], sqb[i][:, :], start=True, stop=True),
                     reads=[r_sqb[i], self.r_const], writes=[self.r_bank[b]])
                s.op("dve", lambda e: e.tensor_reduce(out=m1[:, i:i + 1], in_=self.bank[b][:, :], axis=mybir.AxisListType.X, op=ALU.max),
                     reads=[self.r_bank[b]], writes=[r_m1[i]])
                s.op("dve", lambda e: e.tensor_tensor(out=mx[:, h:h + 1], in0=mx[:, h:h + 1], in1=m1[:, i:i + 1], op=ALU.max),
                     reads=[r_m1[i], self.r_qk], writes=[self.r_qk])
            return ev
        if getattr(self, 'qstep', 9) >= 1:
            self.proj_fm(Wi, 2048, 8, xT, r_xT, mk_rope(self.QD, self.qmax))
            self.proj_fm(Wi, 3072, 8, xT, r_xT, mk_rope(self.KD, self.kmax))

        def mk_v(dst):
            def ev(tb, cg, B, rB):
                i = cnt["i"]; cnt["i"] = 1 - i
                s.op("act", lambda e: e.activation(out=vt[i][:, :], in_=B[:, :], func=AF.Copy), reads=[rB], writes=[r_vt[i]])
                for hh in range(4):
                    self.dma(dst[cg * 4 + hh, :, t * 4 + tb, :], vt[i][:, hh * 128:(hh + 1) * 128], [r_vt[i]], [], awrites=[self.r_att_in])
            return ev
        self.proj_tm(Wi, 4096, 1024, xT, r_xT, mk_v(self.VD))
        if getattr(self, 'upto', 9) < 3:
            s.barrier(); ar.off = mark; return

        def mk_plain(dst):
            def ev(h, B, rB):
                i = cnt["i"]; cnt["i"] = 1 - i
                s.op("act", lambda e: e.activation(out=qb[i][:, :], in_=B[:, :], func=AF.Copy), reads=[rB], writes=[r_qb[i]])
                self.dma(dst[h, :, cols], qb[i][:, :], [r_qb[i]], [], awrites=[self.r_att_in])
            return ev
        self.proj_fm(Wi, 7168, 8, xT, r_xT, mk_plain(self.SQ))
        self.proj_fm(Wi, 8192, 8, xT, r_xT, mk_plain(self.SK))
        self.proj_tm(Wi, 9216, 1024, xT, r_xT, mk_v(self.SV))
        if getattr(self, 'upto', 9) < 4:
            s.barrier(); ar.off = mark; return
        s.barrier()
        ar.off = ph
        hc = ar.take([128, 8, T + HALO], F32); r_hc = rl(8)
        acc = ar.take([128, 8, T], F32); r_acc = rl(8)
        sg = [ar.take([128, T], F32) for _ in range(2)]; r_sg = rl(2)
        ocb = ar.take([128, 8, T], BF16); r_ocb = rl(8)

        def ev_c(c, A, G):
            (A, rA), (G, rG) = A, G
            i = c % 2
            s.op("act", lambda e: e.activation(out=hc[:, c, 0:HALO], in_=self.halo[:, c, :], func=AF.Copy), reads=[self.r_halo], writes=[r_hc[c]])
            s.op("act", lambda e: e.activation(out=sg[i][:, :], in_=G[:, :], func=AF.Sigmoid), reads=[rG], writes=[r_sg[i]])
            s.op("dve", lambda e: e.tensor_tensor(out=hc[:, c, HALO:HALO + T], in0=A[:, :], in1=sg[i][:, :], op=ALU.mult),
                 reads=[rA, r_sg[i], r_hc[c]], writes=[r_hc[c]])
            eng = "dve"
            wo = (l * 8 + c) * CONV_K
            s.op(eng, lambda e: e.tensor_scalar(out=acc[:, c, :], in0=hc[:, c, 0:T], scalar1=self.convw[:, wo:wo + 1], scalar2=self.colp(l, 0, c),
                                                op0=ALU.mult, op1=ALU.add), reads=[r_hc[c], self.r_const], writes=[r_acc[c]])
            for j in range(1, CONV_K):
                s.op(eng, lambda e, j=j: e.scalar_tensor_tensor(out=acc[:, c, :], in0=hc[:, c, j:j + T], scalar=self.convw[:, wo + j:wo + j + 1],
                                                                in1=acc[:, c, :], op0=ALU.mult, op1=ALU.add),
                     reads=[r_hc[c], r_acc[c], self.r_const], writes=[r_acc[c]])
        self.proj_fm_pair(Wi, 5120, 6144, 8, xT, r_xT, ev_c)
        for c in range(8):
            s.op("act", lambda e, c=c: e.activation(out=self.halo[:, c, :], in_=hc[:, c, T:T + HALO], func=AF.Copy), reads=[r_hc[c]], writes=[self.r_halo])
        self.ln_fm(acc, r_acc, 8, LN_EPS, lambda c: self.colp(l, 1, c), lambda c: self.colp(l, 2, c),
                   lambda c: (ocb[:, c, :], r_ocb[c]), scr, func=AF.Silu)
        for c in range(8):
            self.dma(self.OC[c * 128:(c + 1) * 128, cols], ocb[:, c, :], [r_ocb[c]], [], awrites=[self.r_o["OC"][t]])
        s.barrier()
        ar.off = mark

    def stage_att(self, l):
        self.no_pre_log.add(self.rj_idx)
        assert self.pre_loaded is None
        s, ar = self.s, self.ar
        S = self.S
        NG = S // T
        NB = S // 128
        mark = ar.off
        tq = ar.take([128, 8], F32); r_tq = Res()
        s.op("dve", lambda e: e.tensor_tensor(out=tq[:, :], in0=self.qmax[:, :], in1=self.kmax[:, :], op=ALU.mult), reads=[self.r_qk], writes=[r_tq])
        s.op("act", lambda e: e.activation(out=tq[:, :], in_=tq[:, :], func=AF.Sqrt), reads=[r_tq], writes=[r_tq])
        s.op("dve", lambda e: e.tensor_scalar(out=self.negm[:, :], in0=tq[:, :], scalar1=-1.02 / 8.0, scalar2=None, op0=ALU.mult), reads=[r_tq], writes=[self.r_qk])
        bufs = {}
        ar2 = self.ar2
        ar2.off = self.slot_base
        for kind in ("diff", "sb"):
            for par in range(2):
                a_ = ar2 if (par == 1 and 6 * S <= 16384 * 4 // 2 * 2 and ar2.off + 6 * S <= ar2.limit) else ar
                bufs[(kind, par)] = (a_.take([128, S], BF16), a_.take([128, S], BF16), a_.take([128, NB, 128], BF16), Res())
        bank, r_bank = self.bank, self.r_bank
        dPT = [ar.take([128, T], BF16) for _ in range(4)]; r_dPT = rl(4)
        rinv = ar.take([128, T], F32); r_rinv = Res()
        df = [ar.take([128, T], F32) for _ in range(4)]; r_df = rl(4)
        dob = [ar.take([128, T], BF16) for _ in range(2)]; r_dob = rl(2)
        e_t = [ar.take([128, T], F32) for _ in range(3)]; r_e = rl(3)
        sp_t = [ar.take([128, T], BF16) for _ in range(3)]; r_sp = rl(3)
        ec_t = [ar.take([128, T], F32) for _ in range(3)]; r_ec = rl(3)
        sAT = [ar.take([128, T], BF16) for _ in range(3)]; r_sAT = rl(3)
        spsum = ar.take([128, T], BF16); r_sps = Res()
        sob = [ar.take([128, T], BF16) for _ in range(2)]; r_sob = rl(2)

        def load(kind, h, par):
            Qd, Kd, Vd = (self.QD, self.KD, self.VD) if kind == "diff" else (self.SQ, self.SK, self.SV)
            k_, q_, v_, rin = bufs[(kind, par)]
            self.dma(k_[:, :], Kd[h, :, :], [self.r_att_in], [rin])
            self.dma(q_[:, :], Qd[h, :, :], [self.r_att_in], [], awrites=[rin])
            self.dma(v_[:, :, :], Vd[h, :, :, :], [self.r_att_in], [], awrites=[rin])

        def skewed(iters, nph):
            for step in range(len(iters) + nph - 1):
                for ph in range(nph):
                    j = step - ph
                    if 0 <= j < len(iters):
                        iters[j][ph]()
                yield

        def gen_diff(h, par):
            k_, q_, v_, rin = bufs[("diff", par)]
            iters = []
            cnt = 0
            for G in range(NG):
                nkb = 4 * G + 4
                for kb in range(nkb):
                    qlo = max(0, kb - 4 * G) * 128
                    pis = (cnt % 4, (cnt + 1) % 4)
                    cnt += 2

                    def ph0(G=G, kb=kb, qlo=qlo, pis=pis):
                        for n in range(2):
                            pi = pis[n]
                            pr = slice(n * 64, (n + 1) * 64)
                            s.op("pe", lambda e, n=n, pr=pr: e.matmul(bank[n][:, qlo:T], k_[pr, kb * 128:(kb + 1) * 128], q_[pr, G * T + qlo:(G + 1) * T],
                                                                     start=True, stop=True), reads=[rin], writes=[r_bank[n]])
                        for n in range(2):
                            pi = pis[n]
                            s.op("act", lambda e, n=n, pi=pi: e.activation(out=dPT[pi][:, qlo:T], in_=bank[n][:, qlo:T], func=AF.Exp,
                                                                           bias=self.negm[:, h:h + 1], scale=0.125),
                                 reads=[r_bank[n], self.r_qk], writes=[r_dPT[pi]])
                            if kb >= 4 * G:
                                s.op("dve", lambda e, pi=pi: e.tensor_tensor(out=dPT[pi][:, qlo:qlo + 128], in0=dPT[pi][:, qlo:qlo + 128],
                                                                             in1=self.mchunk_b[:, :], op=ALU.mult),
                                     reads=[r_dPT[pi], self.r_const], writes=[r_dPT[pi]])

                    def ph1(G=G, kb=kb, qlo=qlo, pis=pis, nkb=nkb):
                        for n in range(2):
                            pi = pis[n]
                            s.op("pe", lambda e, n=n, pi=pi: e.matmul(bank[2 + n][:, qlo:T], v_[:, kb, :], dPT[pi][:, qlo:T], start=(kb == 0), stop=(kb == nkb - 1)),
                                 reads=[rin, r_dPT[pi]], writes=[r_bank[2 + n]])
                            s.op("pe", lambda e, n=n, pi=pi: e.matmul(bank[4][0:2, qlo:T], self.e2_b[:, 2 * n:2 * n + 2], dPT[pi][:, qlo:T],
                                                                      start=(kb == 0 and n == 0), stop=(kb == nkb - 1 and n == 1)),
                                 reads=[r_dPT[pi], self.r_const], writes=[r_bank[4]])
                        if kb == nkb - 1:
                            group_end(G)

                    iters.append((ph0, ph1))

            def group_end(G):
                gc = slice(G * T, (G + 1) * T)
                oi = G % 2
                s.op("dve", lambda e: e.reciprocal(out=rinv[0:2, :], in_=bank[4][0:2, :]), reads=[r_bank[4]], writes=[r_rinv])
                for n in range(2):
                    s.op("pe", lambda e, n=n: e.matmul(bank[n][:, :], self.sel_f[0:2, n * 128:(n + 1) * 128], rinv[0:2, :], start=True, stop=True),
                         reads=[r_rinv, self.r_const], writes=[r_bank[n]])
                    s.op("act", lambda e, n=n: e.activation(out=df[n][:, :], in_=bank[n][:, :], func=AF.Copy), reads=[r_bank[n]], writes=[r_df[n]])
                    s.op("dve", lambda e, n=n: e.tensor_tensor(out=df[n][:, :], in0=bank[2 + n][:, :], in1=df[n][:, :], op=ALU.mult),
                         reads=[r_bank[2 + n], r_df[n]], writes=[r_df[n]])
                s.op("dve", lambda e: e.scalar_tensor_tensor(out=df[0][:, :], in0=df[1][:, :], scalar=self.neglam[:, 0:1], in1=df[0][:, :],
                                                             op0=ALU.mult, op1=ALU.add), reads=[r_df[0], r_df[1], self.r_qk], writes=[r_df[0]])
                s.op("act", lambda e: e.activation(out=df[2][:, :], in_=df[0][:, :], func=AF.Square), reads=[r_df[0]], writes=[r_df[2]])
                s.op("pe", lambda e: e.matmul(bank[0][:, :], self.ones_f[:, :], df[2][:, :], start=True, stop=True),
                     reads=[r_df[2], self.r_const], writes=[r_bank[0]])
                s.op("dve", lambda e: e.tensor_scalar(out=df[3][:, :], in0=bank[0][:, :], scalar1=1.0 / 128, scalar2=LN_EPS, op0=ALU.mult, op1=ALU.add),
                     reads=[r_bank[0]], writes=[r_df[3]])
                s.op("act", lambda e: e.activation(out=df[3][:, :], in_=df[3][:, :], func=AF.Sqrt), reads=[r_df[3]], writes=[r_df[3]])
                s.op("dve", lambda e: e.reciprocal(out=df[3][:, :], in_=df[3][:, :]), reads=[r_df[3]], writes=[r_df[3]])
                s.op("dve", lambda e: e.tensor_tensor(out=df[0][:, :], in0=df[0][:, :], in1=df[3][:, :], op=ALU.mult), reads=[r_df[0], r_df[3]], writes=[r_df[0]])
                s.op("act", lambda e: e.activation(out=dob[oi][:, :], in_=df[0][:, :], func=AF.Identity, scale=self.gscale[:, h:h + 1]),
                     reads=[r_df[0], self.r_qk], writes=[r_dob[oi]])
                self.dma(self.OB[h * 128:(h + 1) * 128, gc], dob[oi][:, :], [r_dob[oi]], [], awrites=[self.r_o["OB"][G]])

            return skewed(iters, 2)

        def gen_sb(h, par):
            k_, q_, v_, rin = bufs[("sb", par)]
            scale = 128 ** -0.5
            iters = []
            it = 0
            for G in range(NG):
                for kb in range(4 * G + 3, -1, -1):
                    qlo = max(0, kb - 4 * G) * 128
                    i = it % 3; it += 1
                    first = kb == 4 * G + 3

                    def ph0(G=G, kb=kb, qlo=qlo, i=i):
                        et, sp = e_t[i], sp_t[i]
                        s.op("pe", lambda e: e.matmul(bank[5][:, qlo:T], k_[:, kb * 128:(kb + 1) * 128], q_[:, G * T + qlo:(G + 1) * T], start=True, stop=True),
                             reads=[rin], writes=[r_bank[5]])
                        s.op("act", lambda e: e.activation(out=et[:, qlo:T], in_=bank[5][:, qlo:T], func=AF.Exp, scale=scale),
                             reads=[r_bank[5]], writes=[r_e[i]])
                        s.op("act", lambda e: e.activation(out=sp[:, qlo:T], in_=et[:, qlo:T], func=AF.Ln, bias=1.0), reads=[r_e[i]], writes=[r_sp[i]])
                        if kb >= 4 * G:
                            s.op("dve", lambda e: e.tensor_tensor(out=sp[:, qlo:qlo + 128], in0=sp[:, qlo:qlo + 128], in1=self.mstrict_b[:, :], op=ALU.mult),
                                 reads=[r_sp[i], self.r_const], writes=[r_sp[i]])

                    def ph1(G=G, kb=kb, qlo=qlo, i=i, first=first):
                        et, sp, ec, at = e_t[i], sp_t[i], ec_t[i], sAT[i]
                        if first:
                            s.op("dve", lambda e: e.memset(spsum[:, :], 0.0), writes=[r_sps])
                        s.op("pe", lambda e: e.matmul(bank[6][:, qlo:T], self.utri_b[:, :], sp[:, qlo:T], start=True, stop=False),
                             reads=[r_sp[i], self.r_const], writes=[r_bank[6]])
                        s.op("pe", lambda e: e.matmul(bank[6][:, qlo:T], self.ones_b[:, :], spsum[:, qlo:T], start=False, stop=True),
                             reads=[r_sps, self.r_const], writes=[r_bank[6]])
                        s.op("act", lambda e: e.activation(out=ec[:, qlo:T], in_=bank[6][:, qlo:T], func=AF.Exp, scale=-1.0),
                             reads=[r_bank[6]], writes=[r_ec[i]])
                        s.op("dve", lambda e: e.tensor_tensor(out=spsum[:, qlo:T], in0=spsum[:, qlo:T], in1=sp[:, qlo:T], op=ALU.add),
                             reads=[r_sp[i], r_sps], writes=[r_sps])
                        s.op("dve", lambda e: e.tensor_tensor(out=at[:, qlo:T], in0=et[:, qlo:T], in1=ec[:, qlo:T], op=ALU.mult),
                             reads=[r_e[i], r_ec[i]], writes=[r_sAT[i]])
                        if kb >= 4 * G:
                            s.op("dve", lambda e: e.tensor_tensor(out=at[:, qlo:qlo + 128], in0=at[:, qlo:qlo + 128], in1=self.mstrict_b[:, :], op=ALU.mult),
                                 reads=[r_sAT[i], self.r_const], writes=[r_sAT[i]])

                    def ph2(G=G, kb=kb, qlo=qlo, i=i, first=first):
                        at = sAT[i]
                        if first:
                            s.op("pe", lambda e: e.matmul(bank[7][:, :], self.zeros_b[:, :], q_[:, 0:T], start=True, stop=False),
                                 reads=[self.r_const, rin], writes=[r_bank[7]])
                        s.op("pe", lambda e: e.matmul(bank[7][:, qlo:T], v_[:, kb, :], at[:, qlo:T], start=False, stop=(kb == 0)),
                             reads=[rin, r_sAT[i]], writes=[r_bank[7]])
                        if kb == 0:
                            gc = slice(G * T, (G + 1) * T)
                            oi = G % 2
                            s.op("act", lambda e: e.activation(out=sob[oi][:, :], in_=bank[7][:, :], func=AF.Copy), reads=[r_bank[7]], writes=[r_sob[oi]])
                            self.dma(self.OD[h * 128:(h + 1) * 128, gc], sob[oi][:, :], [r_sob[oi]], [], awrites=[self.r_o["OD"][G]])

                    iters.append((ph0, ph1, ph2))
            return skewed(iters, 3)

        load("diff", 0, 0)
        load("sb", 0, 0)
        for h in range(8):
            par = h % 2
            if h + 1 < 8:
                load("diff", h + 1, 1 - par)
                load("sb", h + 1, 1 - par)
            gens = [gen_diff(h, par), gen_sb(h, par)]
            while gens:
                for g in list(gens):
                    try:
                        next(g)
                    except StopIteration:
                        gens.remove(g)
        s.barrier()
        ar.off = mark

    def stage_b(self, l, t, Xout, rXout):
        s, ar = self.s, self.ar
        mark = ar.off
        cols = slice(t * T, (t + 1) * T)
        resid = ar.take([128, NCD, T], F32); r_res = rl(NCD)
        xT = ar.take([128, NCD, T], BF16); r_xT = rl(NCD)
        scr = self.ln_scratch()
        ph = ar.off
        ot = {k: ar.take([128, 8, T], BF16) for k in ("OA", "OB", "OC", "OD")}
        r_ot = {k: rl(8) for k in ot}
        mg = ar.take([128, NCD, T], BF16); r_mg = rl(NCD)
        acc = [ar.take([128, T], F32) for _ in range(4)]; r_acc = rl(4)
        sg = [ar.take([128, T], F32) for _ in range(2)]; r_sg = rl(2)
        pj = [ar.take([128, T], F32) for _ in range(2)]; r_pj = rl(2)
        self.load_resid(self.X1, self.r_X1[t], t, resid, r_res, xT, r_xT)
        self.conv_tick(r_res[0], getattr(self, "conv_rate", 0))
        for k, dsrc in (("OA", self.OA), ("OB", self.OB), ("OC", self.OC), ("OD", self.OD)):
            for c in range(8):
                self.dma(ot[k][:, c, :], dsrc[c * 128:(c + 1) * 128, cols], [self.r_o[k][t]], [r_ot[k][c]])
        Wi = self.wv("w_in", l)
        cnt = {"i": 0}
        jobs = []
        names = ("OA", "OB", "OC", "OD")
        for dp in range(4):
            for n in range(4):
                Wb = self.wv("w_branch", l)
                panels = [(Wb, 0, 8, dp * 512, 512, n * BR), (Wi, 0, NCD, 10240 + n * D + dp * 512, 512)]
                units = []
                for c in range(4):
                    def evac(aps, n=n, c=c, dp=dp):
                        (P, rP), (Gt, rG) = aps
                        i = cnt["i"]; cnt["i"] = 1 - i
                        ch = dp * 4 + c
                        s.op("act", lambda e: e.activation(out=sg[i][:, :], in_=Gt[:, :], func=AF.Sigmoid), reads=[rG], writes=[r_sg[i]])
                        if n == 0:
                            s.op("dve", lambda e: e.tensor_tensor(out=acc[c][:, :], in0=P[:, :], in1=sg[i][:, :], op=ALU.mult),
                                 reads=[rP, r_sg[i]], writes=[r_acc[c]])
                        else:
                            s.op("dve", lambda e: e.tensor_tensor(out=pj[i][:, :], in0=P[:, :], in1=sg[i][:, :], op=ALU.mult),
                                 reads=[rP, r_sg[i]], writes=[r_pj[i]])
                            if n < 3:
                                s.op("dve", lambda e: e.tensor_tensor(out=acc[c][:, :], in0=acc[c][:, :], in1=pj[i][:, :], op=ALU.add),
                                     reads=[r_acc[c], r_pj[i]], writes=[r_acc[c]])
                            else:
                                s.op("dve", lambda e: e.tensor_tensor(out=mg[:, ch, :], in0=acc[c][:, :], in1=pj[i][:, :], op=ALU.add),
                                     reads=[r_acc[c], r_pj[i]], writes=[r_mg[ch]])
                    nm = names[n]
                    units.append(([dict(segs=[(0, c * 128)], act=(lambda kc, nm=nm: (ot[nm][:, kc, :], r_ot[nm][kc])), mode="fm", n=T),
                                   dict(segs=[(1, c * 128)], act=(lambda kc: (xT[:, kc, :], r_xT[kc])), mode="fm", n=T)], evac))
                jobs.append((panels, units))
        self.run_jobs(jobs)

        def ev_wo(c, B, rB):
            s.op("dve", lambda e: e.scalar_tensor_tensor(out=resid[:, c, :], in0=B[:, :], scalar=1.0 / ALPHA, in1=resid[:, c, :], op0=ALU.mult, op1=ALU.add),
                 reads=[rB, r_res[c]], writes=[r_res[c]])
        self.proj_fm(self.wv("w_out", l), 0, NCD, mg, r_mg, ev_wo)
        self.ln_fm(resid, r_res, NCD, LN_EPS / ALPHA ** 2, lambda c: self.lnp(l, 2, c), lambda c: self.lnp(l, 3, c),
                   lambda c: (xT[:, c, :], r_xT[c]), scr)
        s.barrier()
        ar.off = ph
        hT = ar.take([128, NCF, T], BF16); r_hT = rl(NCF)
        self.ffn_ln(l, self.wv("ffn2_w_in", l), self.wv("ffn2_w_out", l), 4, resid, r_res, xT, r_xT, hT, r_hT, scr)
        self.store_resid(Xout, rXout, t, resid, r_res)
        s.barrier()
        ar.off = mark

    def build_all(self, emit=True):
        self.setup()
        self.convert_layer(0)
        self.conv_tick(None, 10 ** 9)
        self.rope_tables()
        Xin, rXin = self.xT, [Res() for _ in range(self.NT)]
        for l in range(self.L):
            self.conv_tick(None, 10 ** 9)
            self.layer_prep(l)
            if l + 1 < self.L:
                self.convert_layer(l + 1)
                self.conv_rate = (len(self.conv_pending) + 2 * self.NT - 1) // (2 * self.NT)
            for t in range(self.NT):
                self.stage_a(l, t, Xin, rXin[t])
            self.stage_att(l)
            last = l == self.L - 1
            Xo = self.out if last else self.X[l % 2]
            rXo = [Res() for _ in range(self.NT)]
            for t in range(self.NT):
                self.stage_b(l, t, Xo, rXo[t])
            Xin, rXin = Xo, rXo
        if emit:
            self.s.emit()


def host_inputs(b, inp, S, L, bi):
    f = np.float32
    m = {}
    m["xT"] = np.ascontiguousarray(inp["x"][bi, :S].T)
    m["pos"] = np.ascontiguousarray(inp["positions"][bi:bi + 1, :S]).astype(np.int32)
    for k in ("ffn1_w_in", "ffn1_w_out", "w_in", "w_branch", "w_out", "ffn2_w_in", "ffn2_w_out"):
        m[k] = np.ascontiguousarray(inp[k][:L])
    ln = np.stack([inp[k][:L] for k in ("ln1_g", "ln1_b", "ln2_g", "ln2_b", "ln3_g", "ln3_b")], axis=1)
    m["p_ln"] = np.ascontiguousarray(ln.reshape(L, 6, NCD, 128).transpose(3, 0, 1, 2)).astype(f)
    row = np.stack([inp["gmlp_ln_g"][:L], inp["gmlp_ln_b"][:L], inp["conv_b"][:L], inp["diff_norm_g"][:L], inp["conv_ln_g"][:L]], axis=1)
    m["p_row"] = np.ascontiguousarray(row).astype(f)
    col = np.stack([inp[k][:L] for k in ("conv_b", "conv_ln_g", "conv_ln_b", "diff_norm_g")], axis=1)
    m["p_col"] = np.ascontiguousarray(col.reshape(L, 4, 8, 128).transpose(3, 0, 1, 2)).astype(f)
    m["p_convw"] = np.ascontiguousarray(inp["conv_w"][:L].reshape(L, CONV_K, 8, 128).transpose(3, 0, 2, 1)).astype(f)
    m["p_wsT"] = np.ascontiguousarray(inp["gmlp_ws"][:L].transpose(0, 1, 3, 2)).astype(f)
    m["p_bs"] = np.ascontiguousarray(inp["gmlp_bs"][:L]).astype(f)
    m["p_lam"] = np.ascontiguousarray(np.stack([inp["diff_lq1"][:L], inp["diff_lk1"][:L], inp["diff_lq2"][:L], inp["diff_lk2"][:L]], axis=1)).astype(f)
    for k, v in build_consts().items():
        m["c_" + k] = v
    return m


_CACHE = {}


def run(inputs, S=4096, L=DEPTH, n_cores=4, dbg=None):
    inputs = {k: np.asarray(v) for k, v in inputs.items()}
    B = inputs["x"].shape[0]
    b0 = Builder(S, L, dbg)
    b0.build_all(emit=False)
    b = Builder(S, L, dbg, hints=b0.first_log, no_pre=b0.no_pre_log)
    b.build_all()
    in_maps = [host_inputs(b, inputs, S, L, c % B) for c in range(n_cores)]
    res = run_bass_kernel_spmd(b.nc, in_maps, core_ids=list(range(n_cores)))
    out = np.stack([np.ascontiguousarray(res.results[c]["outT"].T) for c in range(B)], axis=0)
    return out.astype(np.float32), res


def kernel(**inputs):
    out, _ = run(inputs)
    return out
```

```python
import math
import numpy as np
import concourse.bass as bass
import concourse.mybir as mybir
from concourse.bass_utils import run_bass_kernel_spmd

F32 = mybir.dt.float32
BF16 = mybir.dt.bfloat16
I32 = mybir.dt.int32
ALU = mybir.AluOpType
AF = mybir.ActivationFunctionType

D = 2048
FF = 5632
NCD = 16
NCF = 44
BR = 1024
IN_COLS = 18432
DEPTH = 4
ALPHA = (2 * DEPTH) ** 0.25
LN_EPS = 1e-5
T = 512
CONV_K = 31
HALO = CONV_K - 1
ROPE_THETA = 500000.0
RDMA = 12


class Res:
    __slots__ = ("w", "r", "excl")

    def __init__(self, excl=False):
        self.w = {}
        self.r = {}
        self.excl = excl


def rl(n):
    return [Res() for _ in range(n)]


class Sched:
    def __init__(self, nc):
        self.nc = nc
        self.streams = {k: [] for k in ("pe", "act", "dve", "pool", "sp")}
        self.count = {k: 0 for k in self.streams}
        self.dcount = {"sp": 0, "pool": 0, "act": 0}
        self.seen = {k: {} for k in self.streams}
        self.latest = {}
        self.barrier_tok = {}

    def barrier(self):
        self.barrier_tok = {k: v for k, v in self.latest.items() if not (k[0] == "d" and k[1] == "pool")}

    def op(self, stream, fn, reads=(), writes=(), dma=False, awrites=()):
        deps = dict(self.barrier_tok)

        def add(d):
            for k, v in d.items():
                if deps.get(k, 0) < v:
                    deps[k] = v

        for r in reads:
            add(r.w)
            if r.excl:
                add(r.r)
        for w in writes:
            add(w.w)
            add(w.r)
        for w in awrites:
            add(w.r)
        if dma:
            k = self.dcount[stream]
            self.dcount[stream] = k + 1
            sem = ("d", stream, k % RDMA)
            val = 16 * (k // RDMA + 1)
            if k >= RDMA:
                if deps.get(sem, 0) < val - 16:
                    deps[sem] = val - 16
            inc = 16
        else:
            self.count[stream] += 1
            sem = ("e", stream)
            val = self.count[stream]
            inc = 1
        seen = self.seen[stream]
        waits = []
        for k, v in deps.items():
            if k == ("e", "pe") and stream == "pe":
                continue
            if seen.get(k, 0) < v:
                seen[k] = v
                waits.append((k, v))
        self.streams[stream].append((waits, fn, sem, inc))
        self.latest[sem] = val
        for r in reads:
            if r.r.get(sem, 0) < val:
                r.r[sem] = val
        for w in writes:
            w.w = {sem: val}
            w.r = {}
        for w in awrites:
            if w.w.get(sem, 0) < val:
                w.w[sem] = val

    def emit(self, final_waits_stream="sp"):
        nc = self.nc
        keys = set()
        for st in self.streams.values():
            for waits, fn, sem, inc in st:
                keys.add(sem)
        import contextlib
        with contextlib.ExitStack() as es:
            semobj = {}
            for k in sorted(keys):
                semobj[k] = es.enter_context(nc.semaphore("s_" + "_".join(str(x) for x in k)))
            block = es.enter_context(nc.Block())
            latest = dict(self.latest)

            def runner(name, final):
                lst = self.streams[name]

                def run(e):
                    for waits, fn, sem, inc in lst:
                        for k, v in waits:
                            e.wait_ge(semobj[k], v)
                        fn(e).then_inc(semobj[sem], inc)
                    if final:
                        for k, v in latest.items():
                            e.wait_ge(semobj[k], v)
                return run

            block.tensor(runner("pe", False))
            block.scalar(runner("act", False))
            block.vector(runner("dve", False))
            block.gpsimd(runner("pool", False))
            block.sync(runner("sp", True))


class Arena:
    def __init__(self, nc, base, limit):
        self.nc = nc
        self.off = base
        self.limit = limit
        self.n = 0

    def take(self, shape, dtype):
        nbytes = int(np.prod(shape[1:])) * (4 if dtype in (F32, I32) else 2)
        nbytes = (nbytes + 63) // 64 * 64
        assert self.off + nbytes <= self.limit, ("SBUF overflow", self.off, nbytes, self.limit)
        self.n += 1
        t = self.nc.alloc_sbuf_tensor_at(f"t{self.n}_{self.off}", list(shape), dtype, offset=self.off)
        self.off += nbytes
        return t


def build_consts():
    c = {}
    s = np.arange(128)
    c["ones_f"] = np.ones((128, 128), np.float32)
    c["utri_f"] = (s[:, None] >= s[None, :]).astype(np.float32)
    c["mchunk"] = ((s[:, None] // 64) <= (s[None, :] // 64)).astype(np.float32)
    c["mstrict"] = (s[:, None] < s[None, :]).astype(np.float32)
    rp = np.zeros((128, 128), np.float32)
    invf = np.zeros((128,), np.float32)
    sgn = np.zeros((128,), np.float32)
    freqs = (ROPE_THETA ** (-np.arange(0, 16, 2, dtype=np.float32) / 16)).astype(np.float32)
    for base in (0, 64):
        for i in range(8):
            a, b = base + i, base + 8 + i
            rp[b, a] = 1.0
            rp[a, b] = 1.0
            invf[a] = freqs[i]
            invf[b] = freqs[i]
            sgn[a] = -1.0
            sgn[b] = 1.0
    c["rperm"] = rp
    e2 = np.zeros((128, 4), np.float32); e2[:, 0] = 1.0; e2[:, 3] = 1.0
    c["e2"] = e2
    sel = np.zeros((128, 256), np.float32); sel[0, 0:128] = 1.0; sel[1, 128:256] = 1.0
    c["sel"] = sel
    c["vec"] = np.stack([invf, sgn], axis=1).astype(np.float32)
    return c


class Builder:
    def __init__(self, S, L, dbg=None, hints=None, no_pre=()):
        self.hints = hints
        self.no_pre = set(no_pre)
        self.first_log = []
        self.conv_pending = []
        self.bg = []
        self.bg_rate = 0
        self.no_pre_log = set()
        self.rj_idx = 0
        self.pre_loaded = None
        self.S = S
        self.L = L
        self.NT = S // T
        self.dbg = dbg or set()
        nc = self.nc = bass.Bass("TRN2", target_bir_lowering=False)
        self.s = Sched(nc)
        self.inputs = {}
        self.outputs = {}

    def din(self, name, shape, dtype=F32):
        h = self.nc.dram_tensor(name, list(shape), dtype, kind="ExternalInput")
        self.inputs[name] = h
        return h

    def dscr(self, name, shape, dtype):
        kind = "ExternalOutput" if name in self.dbg else "Internal"
        h = self.nc.dram_tensor(name, list(shape), dtype, kind=kind)
        if kind == "ExternalOutput":
            self.outputs[name] = h
        return h

    def dma(self, out, in_, reads, writes, q="sp", awrites=()):
        self.s.op(q, lambda e: e.dma_start(out=out, in_=in_), reads=reads, writes=writes, dma=True, awrites=awrites)

    WSHAPES = {"ffn1_w_in": (D, 2 * FF), "ffn1_w_out": (FF, D), "w_in": (D, IN_COLS), "w_branch": (4 * BR, D),
               "w_out": (D, D), "ffn2_w_in": (D, 2 * FF), "ffn2_w_out": (FF, D)}

    def wv(self, name, l):
        return (name, l)

    def wres(self, name, l):
        return (self.wb[name][l % 2], self.r_wb[name][l % 2])

    def convert_layer(self, l):
        for name, (rows, ncols) in self.WSHAPES.items():
            src = self.w[name]
            if name == "w_branch":
                src2 = src[l].rearrange("n r c -> (n r) c")
            else:
                src2 = src[l]
            dst, res = self.wres(name, l)
            for rb in range(rows // 128):
                c = 0
                while c < ncols:
                    w = min(4096, ncols - c)
                    self.conv_pending.append((dst[rb * 128:(rb + 1) * 128, c:c + w], src2[rb * 128:(rb + 1) * 128, c:c + w], res))
                    c += w

    def conv_tick(self, gate, n):
        reads = [gate] if gate is not None else []
        while n > 0 and self.conv_pending:
            o, i, res = self.conv_pending.pop(0)
            self.dma(o, i, reads, [], q="pool", awrites=[res])
            n -= 1

    def setup(self):
        nc, S, L = self.nc, self.S, self.L
        d = self.din
        self.xT = d("xT", [D, S])
        self.pos = d("pos", [1, S], I32)
        self.w = {}
        for nm, shp in [("ffn1_w_in", [L, D, 2 * FF]), ("ffn1_w_out", [L, FF, D]),
                        ("w_in", [L, D, IN_COLS]), ("w_branch", [L, 4, BR, D]), ("w_out", [L, D, D]),
                        ("ffn2_w_in", [L, D, 2 * FF]), ("ffn2_w_out", [L, FF, D])]:
            self.w[nm] = d(nm, shp)
        self.p_ln = d("p_ln", [128, L, 6, NCD])
        self.p_row = d("p_row", [L, 5, BR])
        self.p_col = d("p_col", [128, L, 4, 8])
        self.p_convw = d("p_convw", [128, L, 8, CONV_K])
        self.p_wsT = d("p_wsT", [L, 4, 128, 128])
        self.p_bs = d("p_bs", [L, 4, 128])
        self.p_lam = d("p_lam", [L, 4, 64])
        self.c_in = {k: d("c_" + k, list(v.shape)) for k, v in build_consts().items()}
        self.out = nc.dram_tensor("outT", [D, S], F32, kind="ExternalOutput")
        self.outputs["outT"] = self.out
        sc = self.dscr
        self.X = [sc("X0", [D, S], F32), sc("X1s", [D, S], F32)]
        self.X1 = sc("X1", [D, S], F32)
        self.QD = sc("QD", [8, 128, S], BF16)
        self.KD = sc("KD", [8, 128, S], BF16)
        self.VD = sc("VD", [8, 128, S // 128, 128], BF16)
        self.SQ = sc("SQ", [8, 128, S], BF16)
        self.SK = sc("SK", [8, 128, S], BF16)
        self.SV = sc("SV", [8, 128, S // 128, 128], BF16)
        self.OA = sc("OA", [BR, S], BF16)
        self.OB = sc("OB", [BR, S], BF16)
        self.OC = sc("OC", [BR, S], BF16)
        self.OD = sc("OD", [BR, S], BF16)
        self.CT = sc("CT", [128, S], F32)
        self.ST = sc("ST", [128, S], F32)
        self.wb = {}
        self.r_wb = {}
        for name, (rows, ncols) in self.WSHAPES.items():
            h = sc("WB_" + name, [2, rows, ncols], BF16)
            self.wb[name] = [h[0], h[1]]
            self.r_wb[name] = [Res(), Res()]
        self.r_X = [rl(self.NT), rl(self.NT)]
        self.r_X1 = rl(self.NT)
        self.r_att_in = Res()
        self.r_o = {k: rl(self.NT) for k in ("OA", "OB", "OC", "OD")}
        self.r_rope = Res()

        self.ar = Arena(nc, 16640, 229000)
        ar = self.ar
        self.ones_f = ar.take([128, 128], F32)
        self.utri_f = ar.take([128, 128], F32)
        self.ones_b = ar.take([128, 128], BF16)
        self.utri_b = ar.take([128, 128], BF16)
        self.mchunk_f = ar.take([128, 128], F32)
        self.mchunk_b = ar.take([128, 128], BF16)
        self.mstrict_f = ar.take([128, 128], F32)
        self.mstrict_b = ar.take([128, 128], BF16)
        self.rperm_b = ar.take([128, 128], BF16)
        self.zeros_b = ar.take([128, 128], BF16)
        self.e2_b = ar.take([128, 4], BF16)
        self.sel_f = ar.take([128, 256], F32)
        self.cvec = ar.take([128, 2], F32)
        self.ln_p = ar.take([128, L * 6 * NCD], F32)
        self.col_p = ar.take([128, L * 4 * 8], F32)
        self.convw = ar.take([128, L * 8 * CONV_K], F32)
        self.halo = ar.take([128, 8, HALO], F32)
        self.qmax = ar.take([128, 8], F32)
        self.kmax = ar.take([128, 8], F32)
        self.negm = ar.take([128, 8], F32)
        self.neglam = ar.take([128, 1], F32)
        self.gscale = ar.take([128, 8], F32)
        self.wm = [ar.take([128, 128], BF16) for _ in range(4)]
        self.bs1 = [ar.take([128, 128], F32) for _ in range(4)]
        self.r_const = Res()
        self.r_halo = Res()
        self.r_qk = Res()
        self.NSLOT = 4
        self.slot_base = ar.off
        self.slots = [ar.take([128, 8192], BF16) for _ in range(self.NSLOT)]
        self.ar2 = Arena(nc, self.slot_base, ar.off)
        self.ar2.n = 100000
        self.r_slot = rl(self.NSLOT)
        self.slot_i = 0
        self.bank = [nc.alloc_psum_tensor(f"bank{i}", [128, 512], F32) for i in range(8)]
        self.r_bank = [Res(excl=True) for _ in range(8)]
        self.unit_i = 0
        self.base = ar.off

        s = self.s
        tmpf = ar.take([128, 128], F32)
        rt = Res()
        for nm, dst_f, dst_b in [("ones_f", self.ones_f, self.ones_b), ("utri_f", self.utri_f, self.utri_b),
                                 ("mchunk", self.mchunk_f, self.mchunk_b), ("mstrict", self.mstrict_f, self.mstrict_b),
                                 ("rperm", None, self.rperm_b)]:
            tgt = dst_f if dst_f is not None else tmpf
            self.dma(tgt[:, :], self.c_in[nm][:, :], [], [rt])
            if dst_b is not None:
                s.op("dve", lambda e, o=dst_b, i=tgt: e.tensor_copy(out=o[:, :], in_=i[:, :]), reads=[rt], writes=[self.r_const])
        s.op("dve", lambda e: e.memset(self.zeros_b[:, :], 0.0), writes=[self.r_const])
        self.dma(tmpf[:, 0:4], self.c_in["e2"][:, :], [], [rt])
        s.op("dve", lambda e: e.tensor_copy(out=self.e2_b[:, :], in_=tmpf[:, 0:4]), reads=[rt], writes=[self.r_const])
        self.dma(self.sel_f[:, :], self.c_in["sel"][:, :], [], [self.r_const])
        self.dma(self.cvec[:, :], self.c_in["vec"][:, :], [], [self.r_const])
        self.dma(self.ln_p[:, :], self.p_ln.ap().rearrange("p l i c -> p (l i c)"), [], [self.r_const])
        self.dma(self.col_p[:, :], self.p_col.ap().rearrange("p l i c -> p (l i c)"), [], [self.r_const])
        self.dma(self.convw[:, :], self.p_convw.ap().rearrange("p l c j -> p (l c j)"), [], [self.r_const])
        s.barrier()
        self.ar.off = self.base = ar.off

    def lnp(self, l, i, c):
        o = (l * 6 + i) * NCD + c
        return self.ln_p[:, o:o + 1]

    def colp(self, l, i, c):
        o = (l * 4 + i) * 8 + c
        return self.col_p[:, o:o + 1]

    def load_panel(self, W, k0, nk, c0, ncols, r0=0):
        W2d, wres = self.wres(*W)
        i = self.slot_i
        self.slot_i = (i + 1) % self.NSLOT
        view = self.slots[i][:, 0:nk * ncols].rearrange("p (k c) -> p k c", c=ncols)
        src = W2d[r0 + k0 * 128:r0 + (k0 + nk) * 128, c0:c0 + ncols].rearrange("(k p) c -> p k c", p=128)
        self.dma(view, src, [wres], [self.r_slot[i]], q="sp")
        return (i, view, k0, nk)

    def next_unit(self):
        u = self.unit_i
        self.unit_i = (u + 1) % 3
        return 2 * u, 2 * u + 1

    def run_jobs(self, jobs):
        loaded = {}

        def issue(j):
            loaded[j] = [self.load_panel(*p) for p in jobs[j][0]]

        k = self.rj_idx
        self.rj_idx += 1
        self.first_log.append(list(jobs[0][0]))
        if self.pre_loaded is not None:
            loaded[0] = self.pre_loaded
            self.pre_loaded = None
        else:
            issue(0)
        for j in range(len(jobs)):
            if j + 1 < len(jobs):
                issue(j + 1)
            elif self.hints is not None and k + 1 < len(self.hints) and (k + 1) not in self.no_pre:
                self.pre_loaded = [self.load_panel(*p) for p in self.hints[k + 1]]
            pan = loaded.pop(j)
            for banks, evac in jobs[j][1]:
                bidx = self.next_unit()
                aps = []
                for bi, bk in enumerate(banks):
                    b = bidx[bi]
                    n = bk["n"]
                    total = sum(pan[pi][3] for pi, _ in bk["segs"])
                    cnt = 0
                    for pi, coff in bk["segs"]:
                        slot_i, view, k0, nk = pan[pi]
                        for kc in range(nk):
                            a_ap, a_res = bk["act"](k0 + kc)
                            first, last = cnt == 0, cnt == total - 1
                            cnt += 1
                            if bk["mode"] == "fm":
                                lhsT, rhs = view[:, kc, coff:coff + 128], a_ap
                            else:
                                lhsT, rhs = a_ap, view[:, kc, coff:coff + n]
                            self.s.op("pe", lambda e, o=self.bank[b][:, 0:n], l=lhsT, r=rhs, f=first, la=last:
                                      e.matmul(o, l, r, start=f, stop=la),
                                      reads=[self.r_slot[slot_i], a_res], writes=[self.r_bank[b]])
                    aps.append((self.bank[b], self.r_bank[b]))
                evac(aps)
                self.drain_bg(self.bg_rate)

    def drain_bg(self, n):
        while n > 0 and self.bg:
            self.bg.pop(0)()
            n -= 1

    def ln_fm(self, resid, r_res, nchunk, eps, gfn, bfn, out_fn, scr, func=AF.Identity):
        s = self.s
        sq, r_sq, mean, rstd, tmp, r_tmp, r_stat = scr
        n = float(nchunk * 128)
        b6, b7 = self.bank[6], self.bank[7]
        ybf, r_ybf, sqb, r_sqb = self.lnb
        for c in range(nchunk):
            i = c % 2
            s.op("dve", lambda e, o=ybf[i], x=resid[:, c, :]: e.tensor_copy(out=o[:, :], in_=x), reads=[r_res[c]], writes=[r_ybf[i]])
            s.op("act", lambda e, o=sqb[i], x=resid[:, c, :]: e.activation(out=o[:, :], in_=x, func=AF.Square),
                 reads=[r_res[c]], writes=[r_sqb[i]])
            s.op("pe", lambda e, x=ybf[i], f=(c == 0), la=(c == nchunk - 1):
                 e.matmul(b6[:, :], self.ones_b[:, :], x[:, :], start=f, stop=la), reads=[r_ybf[i], self.r_const], writes=[self.r_bank[6]])
            s.op("pe", lambda e, x=sqb[i], f=(c == 0), la=(c == nchunk - 1):
                 e.matmul(b7[:, :], self.ones_b[:, :], x[:, :], start=f, stop=la), reads=[r_sqb[i], self.r_const], writes=[self.r_bank[7]])
        s.op("dve", lambda e: e.tensor_scalar(out=mean[:, :], in0=b6[:, :], scalar1=1.0 / n, scalar2=None, op0=ALU.mult),
             reads=[self.r_bank[6]], writes=[r_stat[0]])
        s.op("dve", lambda e: e.tensor_tensor(out=tmp[0][:, :], in0=mean[:, :], in1=mean[:, :], op=ALU.mult),
             reads=[r_stat[0]], writes=[r_tmp[0]])
        s.op("dve", lambda e: e.scalar_tensor_tensor(out=tmp[0][:, :], in0=b7[:, :], scalar=1.0 / n, in1=tmp[0][:, :],
                                                     op0=ALU.mult, op1=ALU.subtract),
             reads=[self.r_bank[7], r_tmp[0]], writes=[r_tmp[0]])
        s.op("dve", lambda e: e.tensor_scalar(out=tmp[0][:, :], in0=tmp[0][:, :], scalar1=eps, scalar2=None, op0=ALU.add),
             reads=[r_tmp[0]], writes=[r_tmp[0]])
        s.op("act", lambda e: e.activation(out=tmp[0][:, :], in_=tmp[0][:, :], func=AF.Sqrt), reads=[r_tmp[0]], writes=[r_tmp[0]])
        s.op("dve", lambda e: e.reciprocal(out=rstd[:, :], in_=tmp[0][:, :]), reads=[r_tmp[0]], writes=[r_stat[1]])
        for c in range(nchunk):
            i = c % 2
            s.op("dve", lambda e, o=tmp[i], x=resid[:, c, :]: e.tensor_tensor(out=o[:, :], in0=x, in1=mean[:, :], op=ALU.subtract),
                 reads=[r_res[c], r_stat[0]], writes=[r_tmp[i]])
            s.op("dve", lambda e, o=tmp[i]: e.tensor_tensor(out=o[:, :], in0=o[:, :], in1=rstd[:, :], op=ALU.mult),
                 reads=[r_tmp[i], r_stat[1]], writes=[r_tmp[i]])
            s.op("act", lambda e, o=resid[:, c, :], x=tmp[i], g=gfn(c), b=bfn(c): e.activation(out=o, in_=x[:, :], func=func, bias=b, scale=g),
                 reads=[r_tmp[i], self.r_const], writes=[r_res[c]])
            if out_fn is not None:
                o_ap, o_res = out_fn(c)
                s.op("act", lambda e, o=o_ap, x=resid[:, c, :]: e.activation(out=o, in_=x, func=AF.Copy), reads=[r_res[c]], writes=[o_res])

    def ln_scratch(self):
        ar = self.ar
        sq = [None, None]
        mean = ar.take([128, T], F32)
        rstd = ar.take([128, T], F32)
        tmp = [ar.take([128, T], F32) for _ in range(2)]
        self.lnb = ([ar.take([128, T], BF16) for _ in range(2)], rl(2), [ar.take([128, T], BF16) for _ in range(2)], rl(2))
        return (sq, rl(2), mean, rstd, tmp, rl(2), rl(2))

    def ffn_ln(self, l, w_in, w_out, lni, resid, r_res, xT, r_xT, hT, r_hT, scr):
        s = self.s
        W1 = w_in
        W2 = w_out
        silt = [self.ar.take([128, T], F32) for _ in range(2)]
        r_silt = rl(2)
        st = {"i": 0}
        jobs = []
        for jb in range(NCF // 4):
            panels = [(W1, 0, NCD, jb * 512, 512), (W1, 0, NCD, FF + jb * 512, 512)]
            units = []
            for c in range(4):
                j = jb * 4 + c

                def evac(aps, j=j):
                    (A, rA), (G, rG) = aps
                    i = st["i"]
                    st["i"] = 1 - i
                    s.op("act", lambda e: e.activation(out=silt[i][:, :], in_=A[:, :], func=AF.Silu), reads=[rA], writes=[r_silt[i]])
                    s.op("dve", lambda e: e.tensor_tensor(out=hT[:, j, :], in0=silt[i][:, :], in1=G[:, :], op=ALU.mult),
                         reads=[r_silt[i], rG], writes=[r_hT[j]])
                act = lambda kc: (xT[:, kc, :], r_xT[kc])
                units.append(([dict(segs=[(0, c * 128)], act=act, mode="fm", n=T),
                               dict(segs=[(1, c * 128)], act=act, mode="fm", n=T)], evac))
            jobs.append((panels, units))
        self.run_jobs(jobs)
        jobs = []
        for u in range(NCD // 2):
            panels = [(W2, 0, 32, u * 256, 256), (W2, 32, NCF - 32, u * 256, 256)]

            def evac(aps, u=u):
                for bi, (B, rB) in enumerate(aps):
                    c = 2 * u + bi
                    s.op("dve", lambda e, B=B, c=c: e.scalar_tensor_tensor(out=resid[:, c, :], in0=B[:, :], scalar=0.5 / ALPHA,
                                                                           in1=resid[:, c, :], op0=ALU.mult, op1=ALU.add),
                         reads=[rB, r_res[c]], writes=[r_res[c]])
            act = lambda kc: (hT[:, kc, :], r_hT[kc])
            jobs.append((panels, [([dict(segs=[(0, 0), (1, 0)], act=act, mode="fm", n=T),
                                    dict(segs=[(0, 128), (1, 128)], act=act, mode="fm", n=T)], evac)]))
        self.run_jobs(jobs)
        self.ln_fm(resid, r_res, NCD, LN_EPS / ALPHA ** 2, lambda c: self.lnp(l, lni, c), lambda c: self.lnp(l, lni + 1, c),
                   lambda c: (xT[:, c, :], r_xT[c]), scr)

    def load_resid(self, X, rX, t, resid, r_res, xT, r_xT):
        s = self.s
        for c in range(NCD):
            self.dma(resid[:, c, :], X[c * 128:(c + 1) * 128, t * T:(t + 1) * T], [rX], [r_res[c]])
            s.op("act", lambda e, c=c: e.activation(out=xT[:, c, :], in_=resid[:, c, :], func=AF.Copy), reads=[r_res[c]], writes=[r_xT[c]])

    def store_resid(self, X, rX, t, resid, r_res):
        for c in range(NCD):
            self.dma(X[c * 128:(c + 1) * 128, t * T:(t + 1) * T], resid[:, c, :], [r_res[c]], [], awrites=[rX])

    def proj_fm(self, W, c0, nchunks, xT, r_xT, evac_chunk, K=NCD):
        jobs = []
        act = lambda kc: (xT[:, kc, :], r_xT[kc])
        for jb in range(nchunks // 4):
            units = []
            for u in range(2):
                def evac(aps, jb=jb, u=u):
                    for bi, (B, rB) in enumerate(aps):
                        evac_chunk(jb * 4 + u * 2 + bi, B, rB)
                units.append(([dict(segs=[(0, (2 * u + bi) * 128)], act=act, mode="fm", n=T) for bi in range(2)], evac))
            jobs.append(([(W, 0, K, c0 + jb * 512, 512)], units))
        self.run_jobs(jobs)

    def proj_fm_pair(self, W, c0a, c0b, nchunks, xT, r_xT, evac_pair):
        jobs = []
        act = lambda kc: (xT[:, kc, :], r_xT[kc])
        for jb in range(nchunks // 4):
            units = []
            for c in range(4):
                def evac(aps, j=jb * 4 + c):
                    evac_pair(j, aps[0], aps[1])
                units.append(([dict(segs=[(0, c * 128)], act=act, mode="fm", n=T),
                               dict(segs=[(1, c * 128)], act=act, mode="fm", n=T)], evac))
            jobs.append(([(W, 0, NCD, c0a + jb * 512, 512), (W, 0, NCD, c0b + jb * 512, 512)], units))
        self.run_jobs(jobs)

    def proj_tm(self, W, c0, ncols, xT, r_xT, evac):
        jobs = []
        for cg in range(ncols // 512):
            units = []
            for u in range(2):
                def ev(aps, cg=cg, u=u):
                    for bi, (B, rB) in enumerate(aps):
                        evac(2 * u + bi, cg, B, rB)
                bks = []
                for bi in range(2):
                    tb = 2 * u + bi
                    bks.append(dict(segs=[(0, 0)], act=(lambda kc, tb=tb: (xT[:, kc, tb * 128:(tb + 1) * 128], r_xT[kc])), mode="tm", n=512))
                units.append((bks, ev))
            jobs.append(([(W, 0, NCD, c0 + cg * 512, 512)], units))
        self.run_jobs(jobs)

    def rope_tables(self):
        s, ar = self.s, self.ar
        mark = ar.off
        posi = ar.take([128, T], I32)
        ang = ar.take([128, T], F32)
        twopi = ar.take([128, T], F32)
        ni = ar.take([128, T], I32)
        a1 = ar.take([128, T], F32)
        a2 = ar.take([128, T], F32)
        r = rl(5)
        for t in range(self.NT):
            self.dma(posi[:, :], self.pos[0:1, t * T:(t + 1) * T].partition_broadcast(128), [], [r[0]])
            s.op("dve", lambda e: e.tensor_copy(out=ang[:, :], in_=posi[:, :]), reads=[r[0]], writes=[r[1]])
            s.op("dve", lambda e: e.tensor_scalar(out=ang[:, :], in0=ang[:, :], scalar1=self.cvec[:, 0:1], scalar2=None, op0=ALU.mult),
                 reads=[r[1], self.r_const], writes=[r[1]])
            for which, (dst, shift) in enumerate(((self.ST, 0.0), (self.CT, 0.25))):
                a = a1 if which == 0 else a2
                ra = r[3 + which]
                s.op("dve", lambda e, a=a, sh=shift: e.tensor_scalar(out=a[:, :], in0=ang[:, :], scalar1=1.0 / (2.0 * math.pi), scalar2=sh, op0=ALU.mult, op1=ALU.add),
                     reads=[r[1]], writes=[ra])
                s.op("dve", lambda e, a=a: e.tensor_copy(out=ni[:, :], in_=a[:, :]), reads=[ra], writes=[r[2]])
                s.op("dve", lambda e, a=a: e.tensor_copy(out=twopi[:, :], in_=ni[:, :]), reads=[r[2]], writes=[r[2]])
                s.op("dve", lambda e, a=a: e.tensor_tensor(out=a[:, :], in0=a[:, :], in1=twopi[:, :], op=ALU.subtract), reads=[ra, r[2]], writes=[ra])
                s.op("dve", lambda e, a=a: e.tensor_scalar(out=twopi[:, :], in0=a[:, :], scalar1=0.5, scalar2=None, op0=ALU.is_gt), reads=[ra, r[2]], writes=[r[2]])
                s.op("dve", lambda e, a=a: e.tensor_tensor(out=a[:, :], in0=a[:, :], in1=twopi[:, :], op=ALU.subtract), reads=[ra, r[2]], writes=[ra])
                s.op("dve", lambda e, a=a: e.tensor_scalar(out=twopi[:, :], in0=a[:, :], scalar1=-0.5, scalar2=None, op0=ALU.is_lt), reads=[ra, r[2]], writes=[r[2]])
                s.op("dve", lambda e, a=a: e.tensor_tensor(out=a[:, :], in0=a[:, :], in1=twopi[:, :], op=ALU.add), reads=[ra, r[2]], writes=[ra])
                s.op("dve", lambda e, a=a: e.tensor_scalar(out=a[:, :], in0=a[:, :], scalar1=2.0 * math.pi - 1e-6, scalar2=None, op0=ALU.mult), reads=[ra], writes=[ra])
                s.op("act", lambda e, a=a: e.activation(out=a[:, :], in_=a[:, :], func=AF.Sin), reads=[ra], writes=[ra])
                if which == 0:
                    s.op("dve", lambda e, a=a: e.tensor_scalar(out=a[:, :], in0=a[:, :], scalar1=self.cvec[:, 1:2], scalar2=None, op0=ALU.mult),
                         reads=[ra, self.r_const], writes=[ra])
                self.dma(dst[:, t * T:(t + 1) * T], a[:, :], [ra], [], awrites=[self.r_rope])
        s.barrier()
        ar.off = mark

    def layer_prep(self, l):
        s, ar = self.s, self.ar
        lam_init = 0.8 - 0.6 * math.exp(-0.3 * l)
        mark = ar.off
        lt = ar.take([128, 256], F32)
        pr = ar.take([128, 64], F32)
        e12 = ar.take([128, 2], F32)
        wtmp = ar.take([128, 128], F32)
        r = rl(4)
        self.dma(lt[:, :], self.p_lam[l:l + 1, :, :].rearrange("o a b -> o (a b)").partition_broadcast(128), [], [r[0]])
        for i in range(2):
            s.op("dve", lambda e, i=i: e.tensor_tensor(out=pr[:, :], in0=lt[:, i * 128:i * 128 + 64], in1=lt[:, i * 128 + 64:i * 128 + 128], op=ALU.mult),
                 reads=[r[0]], writes=[r[1]])
            s.op("dve", lambda e, i=i: e.reduce_sum(out=e12[:, i:i + 1], in_=pr[:, :], axis=mybir.AxisListType.X), reads=[r[1]], writes=[r[2]])
        s.op("act", lambda e: e.activation(out=e12[:, :], in_=e12[:, :], func=AF.Exp), reads=[r[2]], writes=[r[2]])
        s.op("dve", lambda e: e.tensor_tensor(out=self.neglam[:, :], in0=e12[:, 1:2], in1=e12[:, 0:1], op=ALU.subtract), reads=[r[2]], writes=[self.r_qk])
        s.op("dve", lambda e: e.tensor_scalar(out=self.neglam[:, :], in0=self.neglam[:, :], scalar1=-lam_init, scalar2=None, op0=ALU.add),
             reads=[self.r_qk], writes=[self.r_qk])
        o = (l * 4 + 3) * 8
        s.op("dve", lambda e: e.tensor_scalar(out=self.gscale[:, :], in0=self.col_p[:, o:o + 8], scalar1=1.0 - lam_init, scalar2=None, op0=ALU.mult),
             reads=[self.r_const], writes=[self.r_qk])
        s.op("dve", lambda e: e.memset(self.qmax[:, :], 0.0), writes=[self.r_qk])
        s.op("dve", lambda e: e.memset(self.kmax[:, :], 0.0), writes=[self.r_qk])
        s.op("dve", lambda e: e.memset(self.halo[:, :, :], 0.0), writes=[self.r_halo])
        for g in range(4):
            self.dma(wtmp[:, :], self.p_wsT[l, g, :, :], [], [r[3]])
            s.op("dve", lambda e, g=g: e.tensor_tensor(out=self.wm[g][:, :], in0=wtmp[:, :], in1=self.mchunk_f[:, :], op=ALU.mult),
                 reads=[r[3], self.r_const], writes=[self.r_qk])
            self.dma(self.bs1[g][:, :], self.p_bs[l, g:g + 1, :].partition_broadcast(128), [], [self.r_qk])
        s.barrier()
        ar.off = mark

    def stage_a(self, l, t, Xin, rXin):
        s, ar = self.s, self.ar
        mark = ar.off
        S = self.S
        resid = ar.take([128, NCD, T], F32); r_res = rl(NCD)
        xT = ar.take([128, NCD, T], BF16); r_xT = rl(NCD)
        scr = self.ln_scratch()
        ph = ar.off
        hT = ar.take([128, NCF, T], BF16); r_hT = rl(NCF)
        self.load_resid(Xin, rXin, t, resid, r_res, xT, r_xT)
        self.conv_tick(r_res[0], getattr(self, "conv_rate", 0))
        self.ffn_ln(l, self.wv("ffn1_w_in", l), self.wv("ffn1_w_out", l), 0, resid, r_res, xT, r_xT, hT, r_hT, scr)
        self.store_resid(self.X1, self.r_X1[t], t, resid, r_res)
        Wi = self.wv("w_in", l)
        cols = slice(t * T, (t + 1) * T)
        if getattr(self, 'upto', 9) < 1:
            s.barrier(); ar.off = mark; return
        s.barrier()
        ar.off = ph
        ua = ar.take([128, 8, T], BF16); r_ua = rl(8)
        v_tm = [ar.take([128, BR], F32) for _ in range(4)]; r_v = rl(4)
        vn = [ar.take([128, BR], BF16) for _ in range(4)]; r_vn = rl(4)
        grow = ar.take([128, BR], F32); brow = ar.take([128, BR], F32); r_gb = Res()
        st = ar.take([128, 8], F32); r_st = Res()
        sqv = ar.take([128, BR], F32); r_sqv = Res()
        oa = [ar.take([128, T], BF16) for _ in range(2)]; r_oa = rl(2)
        otmp = [ar.take([128, T], F32) for _ in range(2)]; r_ot = rl(2)
        self.dma(grow[:, :], self.p_row[l, 0:1, :].partition_broadcast(128), [], [r_gb])
        self.dma(brow[:, :], self.p_row[l, 1:2, :].partition_broadcast(128), [], [r_gb])

        def ev_ua(c, B, rB):
            s.op("act", lambda e: e.activation(out=ua[:, c, :], in_=B[:, :], func=AF.Copy), reads=[rB], writes=[r_ua[c]])
        self.proj_fm(Wi, 0, 8, xT, r_xT, ev_ua)

        if getattr(self, 'gstep', 9) < 1:
            s.barrier(); ar.off = mark; return
        def ev_va(tb, cg, B, rB):
            s.op("act", lambda e: e.activation(out=v_tm[tb][:, cg * 512:(cg + 1) * 512], in_=B[:, :], func=AF.Copy), reads=[rB], writes=[r_v[tb]])
        self.proj_tm(Wi, 1024, 1024, xT, r_xT, ev_va)
        if getattr(self, 'gstep', 9) < 2:
            s.barrier(); ar.off = mark; return
        for tb in range(4):
            v = v_tm[tb]
            s.op("dve", lambda e, v=v: e.reduce_sum(out=st[:, 0:1], in_=v[:, :], axis=mybir.AxisListType.X), reads=[r_v[tb]], writes=[r_st])
            s.op("act", lambda e, v=v: e.activation(out=sqv[:, :], in_=v[:, :], func=AF.Square), reads=[r_v[tb]], writes=[r_sqv])
            s.op("dve", lambda e: e.reduce_sum(out=st[:, 1:2], in_=sqv[:, :], axis=mybir.AxisListType.X), reads=[r_sqv], writes=[r_st])
            s.op("dve", lambda e: e.tensor_scalar(out=st[:, 0:2], in0=st[:, 0:2], scalar1=1.0 / BR, scalar2=None, op0=ALU.mult), reads=[r_st], writes=[r_st])
            s.op("dve", lambda e: e.tensor_tensor(out=st[:, 2:3], in0=st[:, 0:1], in1=st[:, 0:1], op=ALU.mult), reads=[r_st], writes=[r_st])
            s.op("dve", lambda e: e.tensor_tensor(out=st[:, 2:3], in0=st[:, 1:2], in1=st[:, 2:3], op=ALU.subtract), reads=[r_st], writes=[r_st])
            s.op("dve", lambda e: e.tensor_scalar(out=st[:, 2:3], in0=st[:, 2:3], scalar1=LN_EPS, scalar2=None, op0=ALU.add), reads=[r_st], writes=[r_st])
            s.op("act", lambda e: e.activation(out=st[:, 2:3], in_=st[:, 2:3], func=AF.Sqrt), reads=[r_st], writes=[r_st])
            s.op("dve", lambda e: e.reciprocal(out=st[:, 3:4], in_=st[:, 2:3]), reads=[r_st], writes=[r_st])
            s.op("dve", lambda e, v=v: e.tensor_scalar(out=v[:, :], in0=v[:, :], scalar1=st[:, 0:1], scalar2=st[:, 3:4], op0=ALU.subtract, op1=ALU.mult),
                 reads=[r_v[tb], r_st], writes=[r_v[tb]])
            s.op("dve", lambda e, v=v: e.tensor_tensor(out=v[:, :], in0=v[:, :], in1=grow[:, :], op=ALU.mult), reads=[r_v[tb], r_gb], writes=[r_v[tb]])
            s.op("dve", lambda e, v=v, tb=tb: e.tensor_tensor(out=vn[tb][:, :], in0=v[:, :], in1=brow[:, :], op=ALU.add),
                 reads=[r_v[tb], r_gb], writes=[r_vn[tb]])
        if getattr(self, 'gstep', 9) < 3:
            s.barrier(); ar.off = mark; return
        for c in range(8):
            b = 6 + (c % 2)
            g = c // 2
            for tb in range(4):
                s.op("pe", lambda e, b=b, tb=tb, c=c, g=g: e.matmul(self.bank[b][:, tb * 128:(tb + 1) * 128], vn[tb][:, c * 128:(c + 1) * 128],
                                                                    self.wm[g][:, :], start=True, stop=True),
                     reads=[r_vn[tb], self.r_qk], writes=[self.r_bank[b]])
            i = c % 2
            for tb in range(4):
                s.op("dve", lambda e, b=b, tb=tb, g=g, i=i: e.tensor_tensor(out=otmp[i][:, tb * 128:(tb + 1) * 128], in0=self.bank[b][:, tb * 128:(tb + 1) * 128],
                                                                          in1=self.bs1[g][:, :], op=ALU.add),
                     reads=[self.r_bank[b], self.r_qk], writes=[r_ot[i]])
            s.op("dve", lambda e, i=i, c=c: e.tensor_tensor(out=oa[i][:, :], in0=otmp[i][:, :], in1=ua[:, c, :], op=ALU.mult),
                 reads=[r_ot[i], r_ua[c]], writes=[r_oa[i]])
            self.dma(self.OA[c * 128:(c + 1) * 128, cols], oa[i][:, :], [r_oa[i]], [], awrites=[self.r_o["OA"][t]])
        if getattr(self, 'upto', 9) < 2:
            s.barrier(); ar.off = mark; return
        s.barrier()
        ar.off = ph
        Ct = ar.take([128, T], F32); St = ar.take([128, T], F32); r_cs = Res()
        qb = [ar.take([128, T], BF16) for _ in range(2)]; r_qb = rl(2)
        t1 = [ar.take([128, T], F32) for _ in range(2)]; r_t1 = rl(2)
        t2 = [ar.take([128, T], F32) for _ in range(2)]; r_t2 = rl(2)
        qr = [ar.take([128, T], BF16) for _ in range(2)]; r_qr = rl(2)
        sqb = [ar.take([128, T], BF16) for _ in range(2)]; r_sqb = rl(2)
        m1 = ar.take([128, 2], F32); r_m1 = rl(2)
        vt = [ar.take([128, 512], BF16) for _ in range(2)]; r_vt = rl(2)
        self.dma(Ct[:, :], self.CT[:, cols], [self.r_rope], [r_cs])
        self.dma(St[:, :], self.ST[:, cols], [self.r_rope], [r_cs])
        cnt = {"i": 0}
        hc = ar.take([128, 8, T + HALO], F32); r_hc = rl(8)
        save_off = ar.off
        ar.off = mark
        acc = ar.take([128, 8, T], F32); r_acc = rl(8)
        sg = [ar.take([128, T], F32) for _ in range(2)]; r_sg = rl(2)
        ocb = ar.take([128, 8, T], BF16); r_ocb = rl(8)
        assert ar.off <= mark + NCD * T * 4
        ar.off = save_off
        taps = [[] for _ in range(8)]

        def ev_c(c, A, G):
            (A, rA), (G, rG) = A, G
            i = c % 2
            s.op("act", lambda e: e.activation(out=hc[:, c, 0:HALO], in_=self.halo[:, c, :], func=AF.Copy), reads=[self.r_halo], writes=[r_hc[c]])
            s.op("act", lambda e: e.activation(out=sg[i][:, :], in_=G[:, :], func=AF.Sigmoid), reads=[rG], writes=[r_sg[i]])
            s.op("dve", lambda e: e.tensor_tensor(out=hc[:, c, HALO:HALO + T], in0=A[:, :], in1=sg[i][:, :], op=ALU.mult),
                 reads=[rA, r_sg[i], r_hc[c]], writes=[r_hc[c]])
            wo = (l * 8 + c) * CONV_K
            taps[c].append(lambda: s.op("dve", lambda e: e.tensor_scalar(out=acc[:, c, :], in0=hc[:, c, 0:T], scalar1=self.convw[:, wo:wo + 1],
                                                                        scalar2=self.colp(l, 0, c), op0=ALU.mult, op1=ALU.add),
                                        reads=[r_hc[c], self.r_const], writes=[r_acc[c]]))
            for j in range(1, CONV_K):
                taps[c].append(lambda j=j: s.op("dve", lambda e: e.scalar_tensor_tensor(out=acc[:, c, :], in0=hc[:, c, j:j + T],
                                                                                       scalar=self.convw[:, wo + j:wo + j + 1],
                                                                                       in1=acc[:, c, :], op0=ALU.mult, op1=ALU.add),
                                                reads=[r_hc[c], r_acc[c], self.r_const], writes=[r_acc[c]]))
        self.proj_fm_pair(Wi, 5120, 6144, 8, xT, r_xT, ev_c)
        for c in range(8):
            s.op("act", lambda e, c=c: e.activation(out=self.halo[:, c, :], in_=hc[:, c, T:T + HALO], func=AF.Copy), reads=[r_hc[c]], writes=[self.r_halo])
        for j in range(CONV_K):
            for c in range(8):
                self.bg.append(taps[c][j])
        self.bg_rate = 3

        def mk_rope(dst, mx):
            def ev(h, B, rB):
                i = cnt["i"]; cnt["i"] = 1 - i
                b = 6 + i
                qs = getattr(self, 'qstep', 9)
                s.op("act", lambda e: e.activation(out=qb[i][:, :], in_=B[:, :], func=AF.Copy), reads=[rB], writes=[r_qb[i]])
                if qs == 1:
                    self.dma(dst[h, :, cols], qb[i][:, :], [r_qb[i]], [], awrites=[self.r_att_in]); return
                s.op("pe", lambda e: e.matmul(self.bank[b][:, :], self.rperm_b[:, :], qb[i][:, :], start=True, stop=True),
                     reads=[r_qb[i], self.r_const], writes=[self.r_bank[b]])
                if qs == 2:
                    self.dma(dst[h, :, cols], qb[i][:, :], [r_qb[i]], [], awrites=[self.r_att_in]); return
                s.op("dve", lambda e: e.tensor_tensor(out=t1[i][:, :], in0=B[:, :], in1=Ct[:, :], op=ALU.mult), reads=[rB, r_cs, r_qb[i]], writes=[r_t1[i]])
                if qs == 3:
                    self.dma(dst[h, :, cols], qb[i][:, :], [r_qb[i]], [], awrites=[self.r_att_in]); return
                s.op("dve", lambda e: e.tensor_tensor(out=t2[i][:, :], in0=self.bank[b][:, :], in1=St[:, :], op=ALU.mult),
                     reads=[self.r_bank[b], r_cs], writes=[r_t2[i]])
                s.op("dve", lambda e: e.tensor_tensor(out=qr[i][:, :], in0=t1[i][:, :], in1=t2[i][:, :], op=ALU.add),
                     reads=[r_t1[i], r_t2[i]], writes=[r_qr[i]])
                self.dma(dst[h, :, cols], qr[i][:, :], [r_qr[i]], [], awrites=[self.r_att_in])
                if qs == 4:
                    return
                s.op("act", lambda e: e.activation(out=sqb[i][:, :], in_=qr[i][:, :], func=AF.Square), reads=[r_qr[i]], writes=[r_sqb[i]])
                s.op("pe", lambda e: e.matmul(self.bank[b][:, :], self.ones_b[:, :], sqb[i][:, :], start=True, stop=True),
                     reads=[r_sqb[i], self.r_const], writes=[self.r_bank[b]])
                s.op("dve", lambda e: e.tensor_reduce(out=m1[:, i:i + 1], in_=self.bank[b][:, :], axis=mybir.AxisListType.X, op=ALU.max),
                     reads=[self.r_bank[b]], writes=[r_m1[i]])
                s.op("dve", lambda e: e.tensor_tensor(out=mx[:, h:h + 1], in0=mx[:, h:h + 1], in1=m1[:, i:i + 1], op=ALU.max),
                     reads=[r_m1[i], self.r_qk], writes=[self.r_qk])
            return ev
        if getattr(self, 'qstep', 9) >= 1:
            self.proj_fm(Wi, 2048, 8, xT, r_xT, mk_rope(self.QD, self.qmax))
            self.proj_fm(Wi, 3072, 8, xT, r_xT, mk_rope(self.KD, self.kmax))

        def mk_v(dst):
            def ev(tb, cg, B, rB):
                i = cnt["i"]; cnt["i"] = 1 - i
                s.op("act", lambda e: e.activation(out=vt[i][:, :], in_=B[:, :], func=AF.Copy), reads=[rB], writes=[r_vt[i]])
                for hh in range(4):
                    self.dma(dst[cg * 4 + hh, :, t * 4 + tb, :], vt[i][:, hh * 128:(hh + 1) * 128], [r_vt[i]], [], awrites=[self.r_att_in])
            return ev
        self.bg_rate = 14
        self.proj_tm(Wi, 4096, 1024, xT, r_xT, mk_v(self.VD))
        if getattr(self, 'upto', 9) < 3:
            s.barrier(); ar.off = mark; return

        def mk_plain(dst):
            def ev(h, B, rB):
                i = cnt["i"]; cnt["i"] = 1 - i
                s.op("act", lambda e: e.activation(out=qb[i][:, :], in_=B[:, :], func=AF.Copy), reads=[rB], writes=[r_qb[i]])
                self.dma(dst[h, :, cols], qb[i][:, :], [r_qb[i]], [], awrites=[self.r_att_in])
            return ev
        self.proj_fm(Wi, 7168, 8, xT, r_xT, mk_plain(self.SQ))
        self.proj_fm(Wi, 8192, 8, xT, r_xT, mk_plain(self.SK))
        self.proj_tm(Wi, 9216, 1024, xT, r_xT, mk_v(self.SV))
        if getattr(self, 'upto', 9) < 4:
            s.barrier(); ar.off = mark; return
        self.drain_bg(10 ** 9)
        self.bg_rate = 0
        self.ln_fm(acc, r_acc, 8, LN_EPS, lambda c: self.colp(l, 1, c), lambda c: self.colp(l, 2, c),
                   lambda c: (ocb[:, c, :], r_ocb[c]), scr, func=AF.Silu)
        for c in range(8):
            self.dma(self.OC[c * 128:(c + 1) * 128, cols], ocb[:, c, :], [r_ocb[c]], [], awrites=[self.r_o["OC"][t]])
        s.barrier()
        ar.off = mark

    def stage_att(self, l):
        self.no_pre_log.add(self.rj_idx)
        assert self.pre_loaded is None
        s, ar = self.s, self.ar
        S = self.S
        NG = S // T
        NB = S // 128
        mark = ar.off
        tq = ar.take([128, 8], F32); r_tq = Res()
        s.op("dve", lambda e: e.tensor_tensor(out=tq[:, :], in0=self.qmax[:, :], in1=self.kmax[:, :], op=ALU.mult), reads=[self.r_qk], writes=[r_tq])
        s.op("act", lambda e: e.activation(out=tq[:, :], in_=tq[:, :], func=AF.Sqrt), reads=[r_tq], writes=[r_tq])
        s.op("dve", lambda e: e.tensor_scalar(out=self.negm[:, :], in0=tq[:, :], scalar1=-1.02 / 8.0, scalar2=None, op0=ALU.mult), reads=[r_tq], writes=[self.r_qk])
        bufs = {}
        ar2 = self.ar2
        ar2.off = self.slot_base
        for kind in ("diff", "sb"):
            for par in range(2):
                a_ = ar2 if (par == 1 and 6 * S <= 16384 * 4 // 2 * 2 and ar2.off + 6 * S <= ar2.limit) else ar
                bufs[(kind, par)] = (a_.take([128, S], BF16), a_.take([128, S], BF16), a_.take([128, NB, 128], BF16), Res())
        bank, r_bank = self.bank, self.r_bank
        dPT = [ar.take([128, T], BF16) for _ in range(4)]; r_dPT = rl(4)
        rinv = ar.take([128, T], F32); r_rinv = Res()
        df = [ar.take([128, T], F32) for _ in range(4)]; r_df = rl(4)
        dob = [ar.take([128, T], BF16) for _ in range(2)]; r_dob = rl(2)
        e_t = [ar.take([128, T], F32) for _ in range(3)]; r_e = rl(3)
        sp_t = [ar.take([128, T], BF16) for _ in range(3)]; r_sp = rl(3)
        ec_t = [ar.take([128, T], F32) for _ in range(3)]; r_ec = rl(3)
        sAT = [ar.take([128, T], BF16) for _ in range(3)]; r_sAT = rl(3)
        spsum = ar.take([128, T], BF16); r_sps = Res()
        sob = [ar.take([128, T], BF16) for _ in range(2)]; r_sob = rl(2)

        def load(kind, h, par):
            Qd, Kd, Vd = (self.QD, self.KD, self.VD) if kind == "diff" else (self.SQ, self.SK, self.SV)
            k_, q_, v_, rin = bufs[(kind, par)]
            self.dma(k_[:, :], Kd[h, :, :], [self.r_att_in], [rin])
            self.dma(q_[:, :], Qd[h, :, :], [self.r_att_in], [], awrites=[rin])
            self.dma(v_[:, :, :], Vd[h, :, :, :], [self.r_att_in], [], awrites=[rin])

        def skewed(iters, nph):
            for step in range(len(iters) + nph - 1):
                for ph in range(nph):
                    j = step - ph
                    if 0 <= j < len(iters):
                        iters[j][ph]()
                yield

        def gen_diff(h, par):
            k_, q_, v_, rin = bufs[("diff", par)]
            iters = []
            cnt = 0
            for G in range(NG):
                nkb = 4 * G + 4
                for kb in range(nkb):
                    qlo = max(0, kb - 4 * G) * 128
                    pis = (cnt % 4, (cnt + 1) % 4)
                    cnt += 2

                    def ph0(G=G, kb=kb, qlo=qlo, pis=pis):
                        for n in range(2):
                            pi = pis[n]
                            pr = slice(n * 64, (n + 1) * 64)
                            s.op("pe", lambda e, n=n, pr=pr: e.matmul(bank[n][:, qlo:T], k_[pr, kb * 128:(kb + 1) * 128], q_[pr, G * T + qlo:(G + 1) * T],
                                                                     start=True, stop=True), reads=[rin], writes=[r_bank[n]])
                        for n in range(2):
                            pi = pis[n]
                            s.op("act", lambda e, n=n, pi=pi: e.activation(out=dPT[pi][:, qlo:T], in_=bank[n][:, qlo:T], func=AF.Exp,
                                                                           bias=self.negm[:, h:h + 1], scale=0.125),
                                 reads=[r_bank[n], self.r_qk], writes=[r_dPT[pi]])
                            if kb >= 4 * G:
                                s.op("dve", lambda e, pi=pi: e.tensor_tensor(out=dPT[pi][:, qlo:qlo + 128], in0=dPT[pi][:, qlo:qlo + 128],
                                                                             in1=self.mchunk_b[:, :], op=ALU.mult),
                                     reads=[r_dPT[pi], self.r_const], writes=[r_dPT[pi]])

                    def ph1(G=G, kb=kb, qlo=qlo, pis=pis, nkb=nkb):
                        for n in range(2):
                            pi = pis[n]
                            s.op("pe", lambda e, n=n, pi=pi: e.matmul(bank[2 + n][:, qlo:T], v_[:, kb, :], dPT[pi][:, qlo:T], start=(kb == 0), stop=(kb == nkb - 1)),
                                 reads=[rin, r_dPT[pi]], writes=[r_bank[2 + n]])
                            s.op("pe", lambda e, n=n, pi=pi: e.matmul(bank[4][0:2, qlo:T], self.e2_b[:, 2 * n:2 * n + 2], dPT[pi][:, qlo:T],
                                                                      start=(kb == 0 and n == 0), stop=(kb == nkb - 1 and n == 1)),
                                 reads=[r_dPT[pi], self.r_const], writes=[r_bank[4]])
                        if kb == nkb - 1:
                            group_end(G)

                    iters.append((ph0, ph1))

            def group_end(G):
                gc = slice(G * T, (G + 1) * T)
                oi = G % 2
                s.op("dve", lambda e: e.reciprocal(out=rinv[0:2, :], in_=bank[4][0:2, :]), reads=[r_bank[4]], writes=[r_rinv])
                for n in range(2):
                    s.op("pe", lambda e, n=n: e.matmul(bank[n][:, :], self.sel_f[0:2, n * 128:(n + 1) * 128], rinv[0:2, :], start=True, stop=True),
                         reads=[r_rinv, self.r_const], writes=[r_bank[n]])
                    s.op("act", lambda e, n=n: e.activation(out=df[n][:, :], in_=bank[n][:, :], func=AF.Copy), reads=[r_bank[n]], writes=[r_df[n]])
                    s.op("dve", lambda e, n=n: e.tensor_tensor(out=df[n][:, :], in0=bank[2 + n][:, :], in1=df[n][:, :], op=ALU.mult),
                         reads=[r_bank[2 + n], r_df[n]], writes=[r_df[n]])
                s.op("dve", lambda e: e.scalar_tensor_tensor(out=df[0][:, :], in0=df[1][:, :], scalar=self.neglam[:, 0:1], in1=df[0][:, :],
                                                             op0=ALU.mult, op1=ALU.add), reads=[r_df[0], r_df[1], self.r_qk], writes=[r_df[0]])
                s.op("act", lambda e: e.activation(out=df[2][:, :], in_=df[0][:, :], func=AF.Square), reads=[r_df[0]], writes=[r_df[2]])
                s.op("pe", lambda e: e.matmul(bank[0][:, :], self.ones_f[:, :], df[2][:, :], start=True, stop=True),
                     reads=[r_df[2], self.r_const], writes=[r_bank[0]])
                s.op("dve", lambda e: e.tensor_scalar(out=df[3][:, :], in0=bank[0][:, :], scalar1=1.0 / 128, scalar2=LN_EPS, op0=ALU.mult, op1=ALU.add),
                     reads=[r_bank[0]], writes=[r_df[3]])
                s.op("act", lambda e: e.activation(out=df[3][:, :], in_=df[3][:, :], func=AF.Sqrt), reads=[r_df[3]], writes=[r_df[3]])
                s.op("dve", lambda e: e.reciprocal(out=df[3][:, :], in_=df[3][:, :]), reads=[r_df[3]], writes=[r_df[3]])
                s.op("dve", lambda e: e.tensor_tensor(out=df[0][:, :], in0=df[0][:, :], in1=df[3][:, :], op=ALU.mult), reads=[r_df[0], r_df[3]], writes=[r_df[0]])
                s.op("act", lambda e: e.activation(out=dob[oi][:, :], in_=df[0][:, :], func=AF.Identity, scale=self.gscale[:, h:h + 1]),
                     reads=[r_df[0], self.r_qk], writes=[r_dob[oi]])
                self.dma(self.OB[h * 128:(h + 1) * 128, gc], dob[oi][:, :], [r_dob[oi]], [], awrites=[self.r_o["OB"][G]])

            return skewed(iters, 2)

        def gen_sb(h, par):
            k_, q_, v_, rin = bufs[("sb", par)]
            scale = 128 ** -0.5
            iters = []
            it = 0
            for G in range(NG):
                for kb in range(4 * G + 3, -1, -1):
                    qlo = max(0, kb - 4 * G) * 128
                    i = it % 3; it += 1
                    first = kb == 4 * G + 3

                    def ph0(G=G, kb=kb, qlo=qlo, i=i):
                        et, sp = e_t[i], sp_t[i]
                        s.op("pe", lambda e: e.matmul(bank[5][:, qlo:T], k_[:, kb * 128:(kb + 1) * 128], q_[:, G * T + qlo:(G + 1) * T], start=True, stop=True),
                             reads=[rin], writes=[r_bank[5]])
                        s.op("act", lambda e: e.activation(out=et[:, qlo:T], in_=bank[5][:, qlo:T], func=AF.Exp, scale=scale),
                             reads=[r_bank[5]], writes=[r_e[i]])
                        s.op("act", lambda e: e.activation(out=sp[:, qlo:T], in_=et[:, qlo:T], func=AF.Ln, bias=1.0), reads=[r_e[i]], writes=[r_sp[i]])
                        if kb >= 4 * G:
                            s.op("dve", lambda e: e.tensor_tensor(out=sp[:, qlo:qlo + 128], in0=sp[:, qlo:qlo + 128], in1=self.mstrict_b[:, :], op=ALU.mult),
                                 reads=[r_sp[i], self.r_const], writes=[r_sp[i]])

                    def ph1(G=G, kb=kb, qlo=qlo, i=i, first=first):
                        et, sp, ec, at = e_t[i], sp_t[i], ec_t[i], sAT[i]
                        if first:
                            s.op("dve", lambda e: e.memset(spsum[:, :], 0.0), writes=[r_sps])
                        s.op("pe", lambda e: e.matmul(bank[6][:, qlo:T], self.utri_b[:, :], sp[:, qlo:T], start=True, stop=False),
                             reads=[r_sp[i], self.r_const], writes=[r_bank[6]])
                        s.op("pe", lambda e: e.matmul(bank[6][:, qlo:T], self.ones_b[:, :], spsum[:, qlo:T], start=False, stop=True),
                             reads=[r_sps, self.r_const], writes=[r_bank[6]])
                        s.op("act", lambda e: e.activation(out=ec[:, qlo:T], in_=bank[6][:, qlo:T], func=AF.Exp, scale=-1.0),
                             reads=[r_bank[6]], writes=[r_ec[i]])
                        s.op("dve", lambda e: e.tensor_tensor(out=spsum[:, qlo:T], in0=spsum[:, qlo:T], in1=sp[:, qlo:T], op=ALU.add),
                             reads=[r_sp[i], r_sps], writes=[r_sps])
                        s.op("dve", lambda e: e.tensor_tensor(out=at[:, qlo:T], in0=et[:, qlo:T], in1=ec[:, qlo:T], op=ALU.mult),
                             reads=[r_e[i], r_ec[i]], writes=[r_sAT[i]])
                        if kb >= 4 * G:
                            s.op("dve", lambda e: e.tensor_tensor(out=at[:, qlo:qlo + 128], in0=at[:, qlo:qlo + 128], in1=self.mstrict_b[:, :], op=ALU.mult),
                                 reads=[r_sAT[i], self.r_const], writes=[r_sAT[i]])

                    def ph2(G=G, kb=kb, qlo=qlo, i=i, first=first):
                        at = sAT[i]
                        if first:
                            s.op("pe", lambda e: e.matmul(bank[7][:, :], self.zeros_b[:, :], q_[:, 0:T], start=True, stop=False),
                                 reads=[self.r_const, rin], writes=[r_bank[7]])
                        s.op("pe", lambda e: e.matmul(bank[7][:, qlo:T], v_[:, kb, :], at[:, qlo:T], start=False, stop=(kb == 0)),
                             reads=[rin, r_sAT[i]], writes=[r_bank[7]])
                        if kb == 0:
                            gc = slice(G * T, (G + 1) * T)
                            oi = G % 2
                            s.op("act", lambda e: e.activation(out=sob[oi][:, :], in_=bank[7][:, :], func=AF.Copy), reads=[r_bank[7]], writes=[r_sob[oi]])
                            self.dma(self.OD[h * 128:(h + 1) * 128, gc], sob[oi][:, :], [r_sob[oi]], [], awrites=[self.r_o["OD"][G]])

                    iters.append((ph0, ph1, ph2))
            return skewed(iters, 3)

        load("diff", 0, 0)
        load("sb", 0, 0)
        for h in range(8):
            par = h % 2
            if h + 1 < 8:
                load("diff", h + 1, 1 - par)
                load("sb", h + 1, 1 - par)
            gens = [gen_diff(h, par), gen_sb(h, par)]
            while gens:
                for g in list(gens):
                    try:
                        next(g)
                    except StopIteration:
                        gens.remove(g)
        s.barrier()
        ar.off = mark

    def stage_b(self, l, t, Xout, rXout):
        s, ar = self.s, self.ar
        mark = ar.off
        cols = slice(t * T, (t + 1) * T)
        resid = ar.take([128, NCD, T], F32); r_res = rl(NCD)
        xT = ar.take([128, NCD, T], BF16); r_xT = rl(NCD)
        scr = self.ln_scratch()
        ph = ar.off
        ot = {k: ar.take([128, 8, T], BF16) for k in ("OA", "OB", "OC", "OD")}
        r_ot = {k: rl(8) for k in ot}
        mg = ar.take([128, NCD, T], BF16); r_mg = rl(NCD)
        acc = [ar.take([128, T], F32) for _ in range(4)]; r_acc = rl(4)
        sg = [ar.take([128, T], F32) for _ in range(2)]; r_sg = rl(2)
        pj = [ar.take([128, T], F32) for _ in range(2)]; r_pj = rl(2)
        self.load_resid(self.X1, self.r_X1[t], t, resid, r_res, xT, r_xT)
        self.conv_tick(r_res[0], getattr(self, "conv_rate", 0))
        for k, dsrc in (("OA", self.OA), ("OB", self.OB), ("OC", self.OC), ("OD", self.OD)):
            for c in range(8):
                self.dma(ot[k][:, c, :], dsrc[c * 128:(c + 1) * 128, cols], [self.r_o[k][t]], [r_ot[k][c]])
        Wi = self.wv("w_in", l)
        cnt = {"i": 0}
        jobs = []
        names = ("OA", "OB", "OC", "OD")
        for dp in range(4):
            for n in range(4):
                Wb = self.wv("w_branch", l)
                panels = [(Wb, 0, 8, dp * 512, 512, n * BR), (Wi, 0, NCD, 10240 + n * D + dp * 512, 512)]
                units = []
                for c in range(4):
                    def evac(aps, n=n, c=c, dp=dp):
                        (P, rP), (Gt, rG) = aps
                        i = cnt["i"]; cnt["i"] = 1 - i
                        ch = dp * 4 + c
                        s.op("act", lambda e: e.activation(out=sg[i][:, :], in_=Gt[:, :], func=AF.Sigmoid), reads=[rG], writes=[r_sg[i]])
                        if n == 0:
                            s.op("dve", lambda e: e.tensor_tensor(out=acc[c][:, :], in0=P[:, :], in1=sg[i][:, :], op=ALU.mult),
                                 reads=[rP, r_sg[i]], writes=[r_acc[c]])
                        else:
                            s.op("dve", lambda e: e.tensor_tensor(out=pj[i][:, :], in0=P[:, :], in1=sg[i][:, :], op=ALU.mult),
                                 reads=[rP, r_sg[i]], writes=[r_pj[i]])
                            if n < 3:
                                s.op("dve", lambda e: e.tensor_tensor(out=acc[c][:, :], in0=acc[c][:, :], in1=pj[i][:, :], op=ALU.add),
                                     reads=[r_acc[c], r_pj[i]], writes=[r_acc[c]])
                            else:
                                s.op("dve", lambda e: e.tensor_tensor(out=mg[:, ch, :], in0=acc[c][:, :], in1=pj[i][:, :], op=ALU.add),
                                     reads=[r_acc[c], r_pj[i]], writes=[r_mg[ch]])
                    nm = names[n]
                    units.append(([dict(segs=[(0, c * 128)], act=(lambda kc, nm=nm: (ot[nm][:, kc, :], r_ot[nm][kc])), mode="fm", n=T),
                                   dict(segs=[(1, c * 128)], act=(lambda kc: (xT[:, kc, :], r_xT[kc])), mode="fm", n=T)], evac))
                jobs.append((panels, units))
        self.run_jobs(jobs)

        def ev_wo(c, B, rB):
            s.op("dve", lambda e: e.scalar_tensor_tensor(out=resid[:, c, :], in0=B[:, :], scalar=1.0 / ALPHA, in1=resid[:, c, :], op0=ALU.mult, op1=ALU.add),
                 reads=[rB, r_res[c]], writes=[r_res[c]])
        self.proj_fm(self.wv("w_out", l), 0, NCD, mg, r_mg, ev_wo)
        self.ln_fm(resid, r_res, NCD, LN_EPS / ALPHA ** 2, lambda c: self.lnp(l, 2, c), lambda c: self.lnp(l, 3, c),
                   lambda c: (xT[:, c, :], r_xT[c]), scr)
        s.barrier()
        ar.off = ph
        hT = ar.take([128, NCF, T], BF16); r_hT = rl(NCF)
        self.ffn_ln(l, self.wv("ffn2_w_in", l), self.wv("ffn2_w_out", l), 4, resid, r_res, xT, r_xT, hT, r_hT, scr)
        self.store_resid(Xout, rXout, t, resid, r_res)
        s.barrier()
        ar.off = mark

    def build_all(self, emit=True):
        self.setup()
        self.convert_layer(0)
        self.conv_tick(None, 10 ** 9)
        self.rope_tables()
        Xin, rXin = self.xT, [Res() for _ in range(self.NT)]
        for l in range(self.L):
            self.conv_tick(None, 10 ** 9)
            self.layer_prep(l)
            if l + 1 < self.L:
                self.convert_layer(l + 1)
                self.conv_rate = (len(self.conv_pending) + 2 * self.NT - 1) // (2 * self.NT)
            for t in range(self.NT):
                self.stage_a(l, t, Xin, rXin[t])
            self.stage_att(l)
            last = l == self.L - 1
            Xo = self.out if last else self.X[l % 2]
            rXo = [Res() for _ in range(self.NT)]
            for t in range(self.NT):
                self.stage_b(l, t, Xo, rXo[t])
            Xin, rXin = Xo, rXo
        if emit:
            self.s.emit()


def host_inputs(b, inp, S, L, bi):
    f = np.float32
    m = {}
    m["xT"] = np.ascontiguousarray(inp["x"][bi, :S].T)
    m["pos"] = np.ascontiguousarray(inp["positions"][bi:bi + 1, :S]).astype(np.int32)
    for k in ("ffn1_w_in", "ffn1_w_out", "w_in", "w_branch", "w_out", "ffn2_w_in", "ffn2_w_out"):
        m[k] = np.ascontiguousarray(inp[k][:L])
    ln = np.stack([inp[k][:L] for k in ("ln1_g", "ln1_b", "ln2_g", "ln2_b", "ln3_g", "ln3_b")], axis=1)
    m["p_ln"] = np.ascontiguousarray(ln.reshape(L, 6, NCD, 128).transpose(3, 0, 1, 2)).astype(f)
    row = np.stack([inp["gmlp_ln_g"][:L], inp["gmlp_ln_b"][:L], inp["conv_b"][:L], inp["diff_norm_g"][:L], inp["conv_ln_g"][:L]], axis=1)
    m["p_row"] = np.ascontiguousarray(row).astype(f)
    col = np.stack([inp[k][:L] for k in ("conv_b", "conv_ln_g", "conv_ln_b", "diff_norm_g")], axis=1)
    m["p_col"] = np.ascontiguousarray(col.reshape(L, 4, 8, 128).transpose(3, 0, 1, 2)).astype(f)
    m["p_convw"] = np.ascontiguousarray(inp["conv_w"][:L].reshape(L, CONV_K, 8, 128).transpose(3, 0, 2, 1)).astype(f)
    m["p_wsT"] = np.ascontiguousarray(inp["gmlp_ws"][:L].transpose(0, 1, 3, 2)).astype(f)
    m["p_bs"] = np.ascontiguousarray(inp["gmlp_bs"][:L]).astype(f)
    m["p_lam"] = np.ascontiguousarray(np.stack([inp["diff_lq1"][:L], inp["diff_lk1"][:L], inp["diff_lq2"][:L], inp["diff_lk2"][:L]], axis=1)).astype(f)
    for k, v in build_consts().items():
        m["c_" + k] = v
    return m


_CACHE = {}


def run(inputs, S=4096, L=DEPTH, n_cores=4, dbg=None):
    inputs = {k: np.asarray(v) for k, v in inputs.items()}
    B = inputs["x"].shape[0]
    b0 = Builder(S, L, dbg)
    b0.build_all(emit=False)
    b = Builder(S, L, dbg, hints=b0.first_log, no_pre=b0.no_pre_log)
    b.build_all()
    in_maps = [host_inputs(b, inputs, S, L, c % B) for c in range(n_cores)]
    res = run_bass_kernel_spmd(b.nc, in_maps, core_ids=list(range(n_cores)))
    out = np.stack([np.ascontiguousarray(res.results[c]["outT"].T) for c in range(B)], axis=0)
    return out.astype(np.float32), res


def kernel(**inputs):
    out, _ = run(inputs)
    return out
```
